# Optimizing a Trainium2 kernel written in Bass

```python
import math
import jax, jax.numpy as jnp
from jax import lax
import numpy as np

D_MODEL = 1024
BATCH = 8
SEQ = 4096
DEPTH = 2
DEC_BATCH = 128
DEC_SEQ = 8
PAST_LEN = 16384
PAGE_SIZE = 128

N_MIXERS = 2
N_SSM_LAYERS = (DEPTH + 1) // 2
N_MLA_LAYERS = DEPTH // 2
D_FF = 2816
SSM_GROUP = 16
N_GROUPS = D_MODEL // SSM_GROUP
SSM_STATE = 64
DT_MIN = 1e-3
DT_MAX = 1e-1
N_HEADS = 16
QK_NOPE = 64
QK_ROPE = 32
V_HEAD = 64
KV_LORA = 256
Q_LORA = 768
ROPE_THETA = 10000.0
ATTN_SCALE = (QK_NOPE + QK_ROPE) ** -0.5
Q_BLOCK = 128
EPS = 1e-6

kernel_name = "hybrid_s5_mla_macaron_decode_step"

F32 = jnp.float32


def rms_norm(x, g):
    xf = x.astype(F32)
    y = xf * lax.rsqrt(jnp.mean(xf * xf, axis=-1, keepdims=True) + EPS)
    return (y * g.astype(F32)).astype(x.dtype)


def swiglu(x, w_gate, w_up, w_down):
    return (jax.nn.silu(x @ w_gate) * (x @ w_up)) @ w_down


def rope_tables(pos):
    inv = ROPE_THETA ** (-jnp.arange(0, QK_ROPE, 2, dtype=F32) / QK_ROPE)
    ang = pos.astype(F32)[:, None] * inv[None, :]
    return jnp.cos(ang), jnp.sin(ang)


def apply_rope(x, cos, sin):
    x1, x2 = jnp.split(x.astype(F32), 2, axis=-1)
    return jnp.concatenate([x1 * cos - x2 * sin, x1 * sin + x2 * cos], axis=-1).astype(x.dtype)


def s5_discretize(a_re, a_im, log_dt):
    dt = jnp.exp(log_dt.astype(F32))[:, None]
    lr = a_re.astype(F32)
    li = a_im.astype(F32)
    mag = jnp.exp(lr * dt)
    ang = li * dt
    ab_re = mag * jnp.cos(ang)
    ab_im = mag * jnp.sin(ang)
    nr = ab_re - 1.0
    den = lr * lr + li * li
    g_re = (nr * lr + ab_im * li) / den
    g_im = (ab_im * lr - nr * li) / den
    return ab_re, ab_im, g_re, g_im


def _ssm_combine(e1, e2):
    a1r, a1i, b1r, b1i = e1
    a2r, a2i, b2r, b2i = e2
    return (a2r * a1r - a2i * a1i,
            a2r * a1i + a2i * a1r,
            a2r * b1r - a2i * b1i + b2r,
            a2r * b1i + a2i * b1r + b2i)


def s5_mixer(u, h0_re, h0_im, a_re, a_im, log_dt, b_re, b_im, c_re, c_im, d, w_glu):
    B, T, _ = u.shape
    ab_re, ab_im, g_re, g_im = s5_discretize(a_re, a_im, log_dt)
    uf = u.astype(F32)
    ug = uf.reshape(B, T, N_GROUPS, SSM_GROUP)
    bu_re = jnp.einsum('btgi,gpi->btgp', ug, b_re.astype(F32))
    bu_im = jnp.einsum('btgi,gpi->btgp', ug, b_im.astype(F32))
    x_re = g_re * bu_re - g_im * bu_im
    x_im = g_re * bu_im + g_im * bu_re
    h0r = h0_re.astype(F32)
    h0i = h0_im.astype(F32)
    x_re = x_re.at[:, 0].add(ab_re * h0r - ab_im * h0i)
    x_im = x_im.at[:, 0].add(ab_re * h0i + ab_im * h0r)
    a_seq_re = jnp.broadcast_to(ab_re, (1, T, N_GROUPS, SSM_STATE))
    a_seq_im = jnp.broadcast_to(ab_im, (1, T, N_GROUPS, SSM_STATE))
    _, _, s_re, s_im = lax.associative_scan(_ssm_combine, (a_seq_re, a_seq_im, x_re, x_im), axis=1)
    y = (jnp.einsum('gip,btgp->btgi', c_re.astype(F32), s_re)
         - jnp.einsum('gip,btgp->btgi', c_im.astype(F32), s_im))
    y = y.reshape(B, T, D_MODEL) + d.astype(F32) * uf
    h = jax.nn.gelu(y).astype(u.dtype)
    z = h @ w_glu
    out = z[..., :D_MODEL] * jax.nn.sigmoid(z[..., D_MODEL:])
    return out, s_re[:, -1], s_im[:, -1]


def mla_project(x, cos, sin, w_in, q_norm, kv_norm, w_uq):
    B, T, _ = x.shape
    h = x @ w_in
    c_q = rms_norm(h[..., :Q_LORA], q_norm)
    c_kv = rms_norm(h[..., Q_LORA:Q_LORA + KV_LORA], kv_norm)
    k_pe = apply_rope(h[..., Q_LORA + KV_LORA:], cos, sin)
    q = (c_q @ w_uq).reshape(B, T, N_HEADS, QK_NOPE + QK_ROPE)
    q_nope = q[..., :QK_NOPE]
    q_pe = apply_rope(q[..., QK_NOPE:], cos[:, None, :], sin[:, None, :])
    return q_nope, q_pe, c_kv, k_pe


def prompt_attention(q_nope, q_pe, c_kv, k_pe, w_ukv):
    B, S, _ = c_kv.shape
    kv = (c_kv @ w_ukv).reshape(B, S, N_HEADS, QK_NOPE + V_HEAD)
    k_nope = kv[..., :QK_NOPE]
    v = kv[..., QK_NOPE:]
    n_blk = S // Q_BLOCK
    qn = q_nope.reshape(B, n_blk, Q_BLOCK, N_HEADS, QK_NOPE).swapaxes(0, 1)
    qp = q_pe.reshape(B, n_blk, Q_BLOCK, N_HEADS, QK_ROPE).swapaxes(0, 1)
    k_pos = jnp.arange(S)

    def block(args):
        i, qn_b, qp_b = args
        s = (jnp.einsum('bqhd,bkhd->bhqk', qn_b, k_nope, preferred_element_type=F32)
             + jnp.einsum('bqhr,bkr->bhqk', qp_b, k_pe, preferred_element_type=F32)) * ATTN_SCALE
        q_pos = i * Q_BLOCK + jnp.arange(Q_BLOCK)
        s = jnp.where(k_pos[None, :] <= q_pos[:, None], s, -jnp.inf)
        p = jax.nn.softmax(s, axis=-1).astype(v.dtype)
        return jnp.einsum('bhqk,bkhd->bqhd', p, v)

    out = lax.map(block, (jnp.arange(n_blk), qn, qp))
    return out.swapaxes(0, 1).reshape(B, S, N_HEADS, V_HEAD)


def sample_attention(q_nope, q_pe, c_kv, k_pe, w_ukv, cache_ckv, cache_kpe, page_table):
    w = w_ukv.reshape(KV_LORA, N_HEADS, QK_NOPE + V_HEAD).astype(F32)
    w_uk = w[..., :QK_NOPE]
    w_uv = w[..., QK_NOPE:]
    T = c_kv.shape[1]
    q_lat = jnp.einsum('bthd,chd->bhtc', q_nope.astype(F32), w_uk)
    q_r = q_pe.astype(F32).transpose(0, 2, 1, 3)

    def scores(ckv, kpe):
        return (jnp.einsum('bhtc,bkc->bhtk', q_lat, ckv.astype(F32))
                + jnp.einsum('bhtr,bkr->bhtk', q_r, kpe.astype(F32))) * ATTN_SCALE

    s = scores(c_kv, k_pe)
    causal = jnp.tril(jnp.ones((T, T), dtype=bool))
    s = jnp.where(causal, s, -jnp.inf)
    m = jnp.max(s, axis=-1)
    p = jnp.exp(s - m[..., None])
    l = jnp.sum(p, axis=-1)
    acc = jnp.einsum('bhtk,bkc->bhtc', p, c_kv.astype(F32))

    def page_step(carry, phys):
        m, l, acc = carry
        ckv = cache_ckv[phys].astype(F32)
        kpe = cache_kpe[phys]
        s = scores(ckv, kpe)
        m_new = jnp.maximum(m, jnp.max(s, axis=-1))
        corr = jnp.exp(m - m_new)
        p = jnp.exp(s - m_new[..., None])
        l = l * corr + jnp.sum(p, axis=-1)
        acc = acc * corr[..., None] + jnp.einsum('bhtk,bkc->bhtc', p, ckv)
        return (m_new, l, acc), None

    (m, l, acc), _ = lax.scan(page_step, (m, l, acc), page_table.T)
    o_lat = acc / l[..., None]
    return jnp.einsum('bhtc,chv->bthv', o_lat, w_uv).astype(q_nope.dtype)


def forward(x, pos, h0_re, h0_im, attend, p):
    B, T, _ = x.shape
    cos, sin = rope_tables(pos)
    new_re, new_im, new_lat, new_kpe = [], [], [], []
    for i in range(DEPTH):
        j = i // N_MIXERS
        h = swiglu(rms_norm(x, p['norm_pre'][i, 0]), p['ffn_w_gate'][i, 0], p['ffn_w_up'][i, 0], p['ffn_w_down'][i, 0])
        x = x + 0.5 * rms_norm(h, p['norm_post'][i, 0])
        h = rms_norm(x, p['norm_pre'][i, 1])
        if i % N_MIXERS == 0:
            h, s_re, s_im = s5_mixer(h, h0_re[j], h0_im[j], p['ssm_a_re'][j], p['ssm_a_im'][j], p['ssm_log_dt'][j],
                                     p['ssm_b_re'][j], p['ssm_b_im'][j], p['ssm_c_re'][j], p['ssm_c_im'][j],
                                     p['ssm_d'][j], p['ssm_w_glu'][j])
            new_re.append(s_re)
            new_im.append(s_im)
        else:
            q_nope, q_pe, c_kv, k_pe = mla_project(h, cos, sin, p['mla_w_in'][j], p['mla_q_norm'][j],
                                                   p['mla_kv_norm'][j], p['mla_w_uq'][j])
            o = attend(j, q_nope, q_pe, c_kv, k_pe, p['mla_w_ukv'][j])
            h = o.reshape(B, T, N_HEADS * V_HEAD) @ p['mla_w_o'][j]
            new_lat.append(c_kv)
            new_kpe.append(k_pe)
        x = x + rms_norm(h, p['norm_post'][i, 1])
        h = swiglu(rms_norm(x, p['norm_pre'][i, 2]), p['ffn_w_gate'][i, 1], p['ffn_w_up'][i, 1], p['ffn_w_down'][i, 1])
        x = x + 0.5 * rms_norm(h, p['norm_post'][i, 2])
    return x, jnp.stack(new_re), jnp.stack(new_im), jnp.stack(new_lat), jnp.stack(new_kpe)


def setup_inputs(seed: int = 0) -> dict:
    key = jax.random.key(seed)
    ks = jax.random.split(key, 32)
    n_pages = PAST_LEN // PAGE_SIZE
    n_phys = (DEC_BATCH * n_pages * 5) // 4
    nrm = lambda k, shape, s: jax.random.normal(k, shape, F32) * s
    perm = jax.random.permutation(ks[0], n_phys)
    page_table = perm[:DEC_BATCH * n_pages].reshape(DEC_BATCH, n_pages).astype(jnp.int32)
    a_im0 = math.pi * jnp.arange(SSM_STATE, dtype=F32)
    return {
        'x_prompt': nrm(ks[1], (BATCH, SEQ, D_MODEL), 1.0),
        'x_sample': nrm(ks[2], (DEC_BATCH, DEC_SEQ, D_MODEL), 1.0),
        'state_ssm_re': nrm(ks[3], (N_SSM_LAYERS, DEC_BATCH, N_GROUPS, SSM_STATE), 0.1),
        'state_ssm_im': nrm(ks[4], (N_SSM_LAYERS, DEC_BATCH, N_GROUPS, SSM_STATE), 0.1),
        'cache_kv_latent': nrm(ks[5], (N_MLA_LAYERS, n_phys, PAGE_SIZE, KV_LORA), 1.0),
        'cache_k_rope': nrm(ks[6], (N_MLA_LAYERS, n_phys, PAGE_SIZE, QK_ROPE), 1.0),
        'page_table': page_table,
        'norm_pre': 1.0 + nrm(ks[7], (DEPTH, 3, D_MODEL), 0.01),
        'norm_post': 1.0 + nrm(ks[8], (DEPTH, 3, D_MODEL), 0.01),
        'ffn_w_gate': nrm(ks[9], (DEPTH, 2, D_MODEL, D_FF), D_MODEL ** -0.5),
        'ffn_w_up': nrm(ks[10], (DEPTH, 2, D_MODEL, D_FF), D_MODEL ** -0.5),
        'ffn_w_down': nrm(ks[11], (DEPTH, 2, D_FF, D_MODEL), D_FF ** -0.5),
        'ssm_a_re': -0.5 + nrm(ks[12], (N_SSM_LAYERS, N_GROUPS, SSM_STATE), 0.01),
        'ssm_a_im': a_im0 + nrm(ks[13], (N_SSM_LAYERS, N_GROUPS, SSM_STATE), 0.01),
        'ssm_log_dt': jax.random.uniform(ks[14], (N_SSM_LAYERS, N_GROUPS), F32, math.log(DT_MIN), math.log(DT_MAX)),
        'ssm_b_re': nrm(ks[15], (N_SSM_LAYERS, N_GROUPS, SSM_STATE, SSM_GROUP), (2 * SSM_GROUP) ** -0.5),
        'ssm_b_im': nrm(ks[16], (N_SSM_LAYERS, N_GROUPS, SSM_STATE, SSM_GROUP), (2 * SSM_GROUP) ** -0.5),
        'ssm_c_re': nrm(ks[17], (N_SSM_LAYERS, N_GROUPS, SSM_GROUP, SSM_STATE), (2 * SSM_STATE) ** -0.5),
        'ssm_c_im': nrm(ks[18], (N_SSM_LAYERS, N_GROUPS, SSM_GROUP, SSM_STATE), (2 * SSM_STATE) ** -0.5),
        'ssm_d': nrm(ks[19], (N_SSM_LAYERS, D_MODEL), 1.0),
        'ssm_w_glu': nrm(ks[20], (N_SSM_LAYERS, D_MODEL, 2 * D_MODEL), D_MODEL ** -0.5),
        'mla_w_in': nrm(ks[21], (N_MLA_LAYERS, D_MODEL, Q_LORA + KV_LORA + QK_ROPE), D_MODEL ** -0.5),
        'mla_q_norm': 1.0 + nrm(ks[22], (N_MLA_LAYERS, Q_LORA), 0.01),
        'mla_kv_norm': 1.0 + nrm(ks[23], (N_MLA_LAYERS, KV_LORA), 0.01),
        'mla_w_uq': nrm(ks[24], (N_MLA_LAYERS, Q_LORA, N_HEADS * (QK_NOPE + QK_ROPE)), Q_LORA ** -0.5),
        'mla_w_ukv': nrm(ks[25], (N_MLA_LAYERS, KV_LORA, N_HEADS * (QK_NOPE + V_HEAD)), KV_LORA ** -0.5),
        'mla_w_o': nrm(ks[26], (N_MLA_LAYERS, N_HEADS * V_HEAD, D_MODEL), (N_HEADS * V_HEAD) ** -0.5),
    }


def reference(x_prompt, x_sample, state_ssm_re, state_ssm_im, cache_kv_latent, cache_k_rope, page_table,
              norm_pre, norm_post, ffn_w_gate, ffn_w_up, ffn_w_down,
              ssm_a_re, ssm_a_im, ssm_log_dt, ssm_b_re, ssm_b_im, ssm_c_re, ssm_c_im, ssm_d, ssm_w_glu,
              mla_w_in, mla_q_norm, mla_kv_norm, mla_w_uq, mla_w_ukv, mla_w_o):
    p = {
        'norm_pre': norm_pre, 'norm_post': norm_post,
        'ffn_w_gate': ffn_w_gate, 'ffn_w_up': ffn_w_up, 'ffn_w_down': ffn_w_down,
        'ssm_a_re': ssm_a_re, 'ssm_a_im': ssm_a_im, 'ssm_log_dt': ssm_log_dt,
        'ssm_b_re': ssm_b_re, 'ssm_b_im': ssm_b_im, 'ssm_c_re': ssm_c_re, 'ssm_c_im': ssm_c_im,
        'ssm_d': ssm_d, 'ssm_w_glu': ssm_w_glu,
        'mla_w_in': mla_w_in, 'mla_q_norm': mla_q_norm, 'mla_kv_norm': mla_kv_norm,
        'mla_w_uq': mla_w_uq, 'mla_w_ukv': mla_w_ukv, 'mla_w_o': mla_w_o,
    }
    b_p, s_p = x_prompt.shape[0], x_prompt.shape[1]
    h0 = jnp.zeros((N_SSM_LAYERS, b_p, N_GROUPS, SSM_STATE), dtype=state_ssm_re.dtype)
    pos_p = jnp.arange(s_p, dtype=jnp.int32)
    prompt_attend = lambda j, qn, qp, ckv, kp, w: prompt_attention(qn, qp, ckv, kp, w)
    y_prompt, ssm_re_p, ssm_im_p, lat_p, kpe_p = forward(x_prompt, pos_p, h0, h0, prompt_attend, p)
    pos_s = PAST_LEN + jnp.arange(x_sample.shape[1], dtype=jnp.int32)
    sample_attend = lambda j, qn, qp, ckv, kp, w: sample_attention(qn, qp, ckv, kp, w, cache_kv_latent[j],
                                                                  cache_k_rope[j], page_table)
    y_sample, ssm_re_s, ssm_im_s, lat_s, kpe_s = forward(x_sample, pos_s, state_ssm_re, state_ssm_im,
                                                         sample_attend, p)
    return (y_prompt, y_sample, ssm_re_p, ssm_im_p, ssm_re_s, ssm_im_s, lat_p, kpe_p, lat_s, kpe_s)
```

```python
import math
import os
import numpy as np
import concourse.bass as bass
import concourse.mybir as mybir
from concourse.bass_utils import run_bass_kernel_spmd
from contextlib import ExitStack

F32 = mybir.dt.float32
BF16 = mybir.dt.bfloat16
I32 = mybir.dt.int32
AF = mybir.ActivationFunctionType
ALU = mybir.AluOpType
AX = mybir.AxisListType

D = 1024
DFF = 2816
NF = DFF // 128
EPS = 1e-6
QL = 768
KVL = 256
RP = 32
NH = 16
DN = 64
DV = 64
DQK = DN + RP
ATT_SCALE = DQK ** -0.5
ROPE_THETA = 10000.0
PAGE = 128
TWO_PI = 2.0 * math.pi
C1_2PI = 6.28125
C2_2PI = TWO_PI - 6.28125


class DSem:
    __slots__ = ("sem", "cnt")

    def __init__(self, sem):
        self.sem = sem
        self.cnt = 0


class Buf:
    __slots__ = ("name", "w", "r", "ds", "excl")

    def __init__(self, name, excl=False):
        self.name = name
        self.w = {}
        self.r = {}
        self.ds = None
        self.excl = excl


class Sched:
    ENGS = ("pe", "act", "dve", "pool", "sp")
    HANDLES = {"pe": "tensor", "act": "scalar", "dve": "vector", "pool": "gpsimd", "sp": "sync"}

    def __init__(self, nc, es):
        self.nc = nc
        self.es = es
        self.q = {e: [] for e in self.ENGS}
        self.cnt = {e: 0 for e in self.ENGS}
        self.pending = {e: False for e in self.ENGS}
        self.sem = {e: es.enter_context(nc.semaphore("sem_" + e)) for e in self.ENGS}
        self.waited = {}
        self.ds_free = {}
        self.ds_used = []
        self.ds_all = []
        self.ninst = 0

    def buf(self, name):
        return Buf(name)

    def bufs(self, name, n):
        return [Buf("%s%d" % (name, i)) for i in range(n)]

    def _dsem(self, b, kind):
        if b.ds is None:
            b.ds = {}
        if kind not in b.ds:
            free = self.ds_free.setdefault(kind, [])
            if free:
                d = free.pop()
            else:
                d = DSem(self.es.enter_context(self.nc.semaphore("ds%s_%d" % (kind, len(self.ds_all)))))
                self.ds_all.append(d)
            b.ds[kind] = d
            self.ds_used.append((kind, d))
        return b.ds[kind]

    def _filter(self, eng, deps):
        out = []
        for s, v in deps.items():
            if eng == "pe" and s is self.sem["pe"]:
                continue
            key = (eng, id(s))
            if self.waited.get(key, 0) >= v:
                continue
            self.waited[key] = v
            out.append((s, v))
        return out

    def _deps(self, eng, reads, writes, shared=(), skip_sem=None):
        deps = {}

        def add(s, v):
            if deps.get(s, 0) < v:
                deps[s] = v

        for b in reads:
            for s, v in b.w.items():
                add(s, v)
            if b.excl:
                for s, v in b.r.items():
                    add(s, v)
        for b in writes:
            for s, v in b.w.items():
                add(s, v)
            for s, v in b.r.items():
                add(s, v)
        for b in shared:
            for s, v in b.r.items():
                add(s, v)
        if skip_sem is not None:
            deps.pop(skip_sem, None)
        return self._filter(eng, deps)

    def _mark(self, ev, reads, writes, shared=()):
        s, v = ev
        for b in reads:
            if b.r.get(s, 0) < v:
                b.r[s] = v
        for b in writes:
            b.w = {s: v}
            b.r = {}
        for b in shared:
            if b.w.get(s, 0) < v:
                b.w[s] = v

    def op(self, eng, mname, kw, reads=(), writes=(), inc=True):
        fn = (lambda e, mname=mname, kw=kw: getattr(e, mname)(**kw))
        waits = self._deps(eng, reads, writes)
        if inc:
            self.cnt[eng] += 1
            ev = (self.sem[eng], self.cnt[eng])
            self.pending[eng] = False
            incr = (self.sem[eng], 1)
        else:
            ev = (self.sem[eng], self.cnt[eng] + 1)
            self.pending[eng] = True
            incr = None
        self._mark(ev, reads, writes)
        self.q[eng].append((waits, fn, incr))
        self.ninst += 1

    def dma(self, q, out, in_, reads=(), writes=(), owner=None, indirect=None, shared=(), **kw):
        ds = self._dsem(owner, "sw" if q == "pool" else "hw")
        skip = ds.sem if (owner in writes) else None
        waits = self._deps(q, reads, writes, shared, skip_sem=skip)
        ds.cnt += 1
        ev = (ds.sem, 16 * ds.cnt)
        self._mark(ev, reads, writes, shared)
        if indirect is not None:
            fn = (lambda e: e.indirect_dma_start(out=out, out_offset=None, in_=in_,
                                                 in_offset=bass.IndirectOffsetOnAxis(ap=indirect, axis=0)))
        else:
            fn = (lambda e: e.dma_start(out=out, in_=in_, **kw))
        self.q[q].append((waits, fn, (ds.sem, 16)))
        self.ninst += 1

    def barrier(self):
        for en in self.ENGS:
            if self.pending[en]:
                self.cnt[en] += 1
                self.pending[en] = False
                self.q[en].append(([], (lambda e: e.nop()), (self.sem[en], 1)))
        deps = {}
        for en in self.ENGS:
            if self.cnt[en] > 0:
                deps[self.sem[en]] = self.cnt[en]
        for ds in self.ds_all:
            if ds.cnt > 0:
                deps[ds.sem] = 16 * ds.cnt
        for en in self.ENGS:
            self.q[en].append((self._filter(en, dict(deps)), None, None))

    def end_phase(self):
        self.barrier()
        self.emit()
        for kind, d in self.ds_used:
            self.ds_free.setdefault(kind, []).append(d)
        self.ds_used = []

    def emit(self):
        nc = self.nc
        with nc.Block() as block:
            for en in self.ENGS:
                items = self.q[en]

                def body(e, items=items):
                    for waits, fn, incr in items:
                        for s, v in waits:
                            e.wait_ge(s, v)
                        if fn is not None:
                            ins = fn(e)
                            if incr is not None:
                                ins.then_inc(incr[0], incr[1])

                getattr(block, self.HANDLES[en])(body)
        self.q = {e: [] for e in self.ENGS}


class RR:
    def __init__(self, tensors, bufs):
        self.t = tensors
        self.b = bufs
        self.i = 0

    def next(self):
        k = self.i % len(self.t)
        self.i += 1
        return self.t[k], self.b[k]


class Cfg:
    def __init__(self, TP=4096, NS=16, TS=8, NPG=128, NPHYS=20480, stages=None):
        self.TP = TP
        self.NS = NS
        self.TS = TS
        self.NPG = NPG
        self.NPHYS = NPHYS
        self.NTOK = TP + NS * TS
        assert NS * TS == 128 and TP % 512 == 0 and TS == 8
        self.NT = self.NTOK // 128
        self.NB = TP // 8
        self.NBT = self.NB + NS
        self.PAST = NPG * PAGE
        self.stages = stages


def build(cfg):
    nc = bass.Bass("TRN2", target_bir_lowering=False)
    es = ExitStack()
    with es:
        P = Prog(nc, es, cfg)
        P.run()
    return nc


class Prog:
    def __init__(self, nc, es, cfg):
        self.nc = nc
        self.es = es
        self.cfg = cfg
        self.S = Sched(nc, es)
        self.pes = None
        groups = []
        t = 0
        while t < cfg.NT:
            n = 4 if (t * 128) < cfg.TP else 1
            groups.append((t, n))
            t += n
        self.groups = groups

    def dram_in(self, name, shape, dt=F32):
        return self.nc.dram_tensor(name, list(shape), dt, kind="ExternalInput").ap()

    def dram_out(self, name, shape, dt=F32):
        return self.nc.dram_tensor(name, list(shape), dt, kind="ExternalOutput").ap()

    def dram_tmp(self, name, shape, dt=F32):
        return self.nc.dram_tensor(name, list(shape), dt, kind="Internal").ap()

    def gsb(self, name, shape, dt=F32):
        return self.es.enter_context(self.nc.sbuf_tensor(name, list(shape), dt))

    def sb(self, name, shape, dt=F32):
        self._n = getattr(self, "_n", 0) + 1
        return self.pes.enter_context(self.nc.sbuf_tensor("%s_%d" % (name, self._n), list(shape), dt))

    def ps(self, name, shape, dt=F32):
        self._n = getattr(self, "_n", 0) + 1
        return self.pes.enter_context(self.nc.psum_tensor("%s_%d" % (name, self._n), list(shape), dt))

    def sbn(self, name, n, shape, dt=F32):
        return RR([self.sb("%s%d" % (name, i), shape, dt) for i in range(n)], self.S.bufs(name, n))

    def psn(self, name, n, shape, dt=F32):
        return RR([self.ps("%s%d" % (name, i), shape, dt) for i in range(n)],
                  [Buf("%s%d" % (name, i), excl=True) for i in range(n)])

    def act(self, kw, reads=(), writes=()):
        self.S.op("act", "activation", kw, reads, writes)

    def dve(self, m, kw, reads=(), writes=()):
        self.S.op("dve", m, kw, reads, writes)

    def pool(self, m, kw, reads=(), writes=()):
        self.S.op("pool", m, kw, reads, writes)

    def mm(self, out, lhsT, rhs, start, stop, reads, writes, inc):
        self.S.op("pe", "matmul", dict(out=out, lhsT=lhsT, rhs=rhs, start=start, stop=stop), reads, writes, inc=inc)

    def tr(self, out, in_, reads, writes, inc):
        n = in_.shape[0]
        self.S.op("pe", "transpose", dict(out=out, in_=in_, identity=self.ident[0:n, 0:n]),
                  list(reads) + [self.ident_b], writes, inc=inc)

    def ld(self, out, in_, reads, writes, owner, q="sp", shared=()):
        self.S.dma(q, out, in_, reads=reads, writes=writes, owner=owner, shared=shared)

    def run(self):
        cfg = self.cfg
        S = self.S
        NT, NTOK, TP = cfg.NT, cfg.NTOK, cfg.TP
        self.x_in = self.dram_in("x_in", [NTOK, D])
        self.y_out = self.dram_out("y_out", [NTOK, D])
        self.xs = self.dram_tmp("xs", [NTOK, D])
        self.xs_b = S.bufs("xs", NT)
        self.xin_b = S.buf("x_in")
        self.y_b = S.bufs("y", NT)
        self.norm_pre = self.dram_in("norm_pre", [6, D])
        self.norm_post = self.dram_in("norm_post", [6, D])
        self.wg_h = self.dram_in("wg_h", [4, NF * 128, 8 * 128])
        self.wu_h = self.dram_in("wu_h", [4, NF * 128, 8 * 128])
        self.wd_h = self.dram_in("wd_h", [4, 128, NF * D])
        self.wg_b = self.dram_tmp("wg_b", [4, NF * 128, 8 * 128], BF16)
        self.wu_b = self.dram_tmp("wu_b", [4, NF * 128, 8 * 128], BF16)
        self.wd_b = self.dram_tmp("wd_b", [4, 128, NF * D], BF16)
        self.wcast_b = S.bufs("wcast", 4)
        self.s5_are = self.dram_in("s5_are", [128, 32])
        self.s5_aim = self.dram_in("s5_aim", [128, 32])
        self.s5_ldt = self.dram_in("s5_ldt", [128, 32])
        self.s5_bre = self.dram_in("s5_bre", [128, 32 * 16])
        self.s5_bim = self.dram_in("s5_bim", [128, 32 * 16])
        self.s5_cre = self.dram_in("s5_cre", [128, 32 * 16])
        self.s5_cim = self.dram_in("s5_cim", [128, 32 * 16])
        self.s5_d = self.dram_in("s5_d", [128, 8])
        self.s5_h0 = self.dram_in("s5_h0", [128, 32 * 2 * 16])
        self.wglu_h = self.dram_in("wglu_h", [128, 8 * 2048])
        self.wglu_b = self.dram_tmp("wglu_b", [128, 8 * 2048], BF16)
        self.ssm_p = self.dram_out("ssm_p", [128, 32 * 2])
        self.ssm_s = self.dram_out("ssm_s", [128, 32 * 2 * 16])
        self.ssm_b = S.buf("ssm_out")
        self.uT_s = self.dram_tmp("uT_s", [8, 128, NTOK], BF16)
        self.us_b = S.bufs("uT_s", 8)
        self.win_h = self.dram_in("win_h", [128, 8 * 1056])
        self.win_b = self.dram_tmp("win_b", [128, 8 * 1056], BF16)
        self.wuq_h = self.dram_in("wuq_h", [128, 6 * 1536])
        self.wuq_b = self.dram_tmp("wuq_b", [128, 6 * 1536], BF16)
        self.wukv_h = self.dram_in("wukv_h", [128, 2 * 2048])
        self.wukv_b = self.dram_tmp("wukv_b", [128, 2 * 2048], BF16)
        self.wukT_h = self.dram_in("wukT_h", [64, 16 * 256])
        self.wukT_b = self.dram_tmp("wukT_b", [64, 16 * 256], BF16)
        self.wo_h = self.dram_in("wo_h", [128, 8 * 1024])
        self.wo_b = self.dram_tmp("wo_b", [128, 8 * 1024], BF16)
        self.qnorm = self.dram_in("qnorm", [1, QL])
        self.kvnorm = self.dram_in("kvnorm", [1, KVL])
        self.cache_cat = self.dram_in("cache_cat", [cfg.NPHYS * PAGE, KVL + RP])
        self.ptab = self.dram_in("ptab", [cfg.NS, cfg.NPG], I32)
        self.lat_out = self.dram_out("lat_out", [NTOK, KVL])
        self.kpe_out = self.dram_out("kpe_out", [NTOK, RP])
        self.mla_out_b = S.buf("mla_out")
        self.qT_s = self.dram_tmp("qT_s", [NH, DQK, TP], BF16)
        self.kT_s = self.dram_tmp("kT_s", [NH, DQK, TP], BF16)
        self.v_s = self.dram_tmp("v_s", [TP, NH * 66], BF16)
        self.qkv_s_b = S.buf("qkv_s")
        self.wcast2_b = S.buf("wcast2")

        self.ident = self.gsb("ident", [128, 128], BF16)
        self.ident_f = self.gsb("ident_f", [128, 128], F32)
        self.ident_b = S.buf("ident")
        self.eps_t = self.gsb("eps_t", [128, 1])
        self.eps_b = S.buf("eps")
        self.mask32 = self.gsb("mask32", [128, 128])
        self.mask32_b = S.buf("mask32")

        self.pes = ExitStack()
        with self.pes:
            self.pool("memset", dict(ap=self.ident_f[:], constant=0.0), writes=[self.ident_b])
            self.pool("affine_select", dict(out=self.ident_f[:], in_=self.ident_f[:], pattern=[[-1, 128]],
                                            compare_op=ALU.not_equal, fill=1.0, base=0, channel_multiplier=1),
                      writes=[self.ident_b])
            self.dve("tensor_copy", dict(out=self.ident[:], in_=self.ident_f[:]), writes=[self.ident_b])
            self.dve("memset", dict(ap=self.eps_t[:], constant=EPS), writes=[self.eps_b])
            self.pool("memset", dict(ap=self.mask32[:], constant=0.0), writes=[self.mask32_b])
            for k in range(4):
                self.pool("memset", dict(ap=self.mask32[32 * k:32 * k + 32, 32 * k:32 * k + 32], constant=1.0),
                          writes=[self.mask32_b])
            for i in range(4):
                for (o, s_) in ((self.wg_b, self.wg_h), (self.wu_b, self.wu_h), (self.wd_b, self.wd_h)):
                    S.dma("pool", o[i], s_[i], writes=[self.wcast_b[i]], owner=self.wcast_b[i])
            for (o, s_) in ((self.wglu_b, self.wglu_h), (self.win_b, self.win_h), (self.wuq_b, self.wuq_h),
                            (self.wukv_b, self.wukv_h), (self.wukT_b, self.wukT_h), (self.wo_b, self.wo_h)):
                S.dma("pool", o[:, :], s_[:, :], writes=[self.wcast2_b], owner=self.wcast2_b)
            S.end_phase()

        st = cfg.stages
        xsb = lambda ti: [self.xs_b[ti]]
        yb = lambda ti: [self.y_b[ti]]
        self.ffn(0, 0, self.x_in, lambda ti: [self.xin_b], self.xs, xsb)
        last = (st == "ffn0")
        if not last:
            self.s5(1)
            last = (st == "s5")
        if not last:
            self.ffn(1, 2, self.xs, xsb, self.xs, xsb)
            self.ffn(2, 3, self.xs, xsb, self.xs, xsb)
            last = (st == "ffn2")
        if not last:
            self.mla(4)
            last = (st == "mla")
        if not last:
            self.ffn(3, 5, self.xs, xsb, self.y_out, yb)
        else:
            self.copy_out()

    def copy_out(self):
        S = self.S
        self.pes = ExitStack()
        with self.pes:
            xt = self.sbn("xt", 4, [128, D])
            for ti in range(self.cfg.NT):
                t, b = xt.next()
                self.ld(t[:], self.xs[ti * 128:(ti + 1) * 128, :], [self.xs_b[ti]], [b], b)
                self.ld(self.y_out[ti * 128:(ti + 1) * 128, :], t[:], [b], [self.y_b[ti]], b, q="pool")
            S.end_phase()

    def common_alloc(self, nxt=8):
        self.xt = self.sbn("xt", nxt, [128, D])
        self.gpre = self.sb("gpre", [128, D])
        self.gpost = self.sb("gpost", [128, D])
        self.gpre_b = self.S.buf("gpre")
        self.gpost_b = self.S.buf("gpost")
        self.junk = self.sb("junk", [128, D], BF16)
        self.junk_b = self.S.buf("junk")
        self.small = self.sbn("small", 8, [128, 8])
        self.xn = self.sbn("xn", 2, [128, D], BF16)
        self.yt = self.sbn("yt", 2, [128, D])
        self.pT = self.psn("pT", 2, [128, 1024], BF16)

    def load_gamma(self, dst, dst_b, src_row):
        self.ld(dst[:], src_row.partition_broadcast(128), [], [dst_b], dst_b)

    def rstd_of(self, parts, ncols):
        sm, smb = self.small.next()
        off = 0
        for i, (ap, bufs) in enumerate(parts):
            w = ap.shape[-1]
            self.act(dict(out=self.junk[:, off:off + w], in_=ap, func=AF.Square, accum_out=sm[:, i:i + 1]),
                     reads=bufs, writes=[self.junk_b, smb])
            off += w
        col = len(parts)
        if len(parts) > 1:
            assert len(parts) == 2
            self.dve("tensor_tensor", dict(out=sm[:, 2:3], in0=sm[:, 0:1], in1=sm[:, 1:2], op=ALU.add), [smb], [smb])
            src = sm[:, 2:3]
            col = 3
        else:
            src = sm[:, 0:1]
        self.act(dict(out=sm[:, col:col + 1], in_=src, func=AF.Sqrt, scale=1.0 / ncols, bias=self.eps_t[:, 0:1]),
                 reads=[smb, self.eps_b], writes=[smb])
        self.dve("reciprocal", dict(out=sm[:, col + 1:col + 2], in_=sm[:, col:col + 1]), [smb], [smb])
        return sm[:, col + 1:col + 2], smb

    def transpose_into(self, src, src_b, nch, dstT, dstT_b, col0, eng="act"):
        p, pb = self.pT.next()
        for c in range(nch):
            self.tr(p[:, c * 128:(c + 1) * 128], src[:, c * 128:(c + 1) * 128], [src_b], [pb], inc=(c == nch - 1))
        kw = dict(out=dstT[:, 0:nch, col0:col0 + 128], in_=p[:, 0:nch * 128].rearrange("p (c n) -> p c n", n=128))
        if eng == "act":
            self.S.op("act", "copy", kw, [pb], [dstT_b])
        else:
            self.S.op("dve", "tensor_copy", kw, [pb], [dstT_b])

    def front(self, ti, src, src_bufs, xnT_t, xnT_tb, col0):
        xt, xb = self.xt.next()
        self.ld(xt[:], src[ti * 128:(ti + 1) * 128, :], src_bufs, [xb], xb)
        rstd, rb = self.rstd_of([(xt[:], [xb])], D)
        xn, xnb = self.xn.next()
        self.dve("scalar_tensor_tensor", dict(out=xn[:], in0=xt[:], scalar=rstd, in1=self.gpre[:],
                                              op0=ALU.mult, op1=ALU.mult), [xb, rb, self.gpre_b], [xnb])
        self.transpose_into(xn, xnb, 8, xnT_t, xnT_tb, col0)
        return xt, xb

    def post_residual(self, halves, xt, xb, dst, dst_bufs, ti, coef):
        rstd, rb = self.rstd_of([(h[0], [h[1]]) for h in halves], D)
        yt, yb = self.yt.next()
        for h in range(2):
            self.dve("scalar_tensor_tensor", dict(out=yt[:, h * 512:(h + 1) * 512], in0=halves[h][0], scalar=rstd,
                                                  in1=self.gpost[:, h * 512:(h + 1) * 512], op0=ALU.mult, op1=ALU.mult),
                     [halves[h][1], rb, self.gpost_b], [yb])
        self.dve("scalar_tensor_tensor", dict(out=yt[:], in0=yt[:], scalar=float(coef), in1=xt[:],
                                              op0=ALU.mult, op1=ALU.add), [yb, xb], [yb])
        self.ld(dst[ti * 128:(ti + 1) * 128, :], yt[:], [yb], dst_bufs, yb, q="pool")

    def ffn(self, fi, ni, src, src_bufs_of, dst, dst_bufs_of):
        S = self.S
        self.pes = ExitStack()
        with self.pes:
            self.common_alloc(8)
            xnT = self.sbn("xnT", 2, [128, 8, 512], BF16)
            wg = self.sbn("wg", 6, [128, 8 * 128], BF16)
            wu = self.sbn("wu", 6, [128, 8 * 128], BF16)
            wd = self.sbn("wd", 2, [128, 11 * D], BF16)
            sg = self.sbn("sg", 2, [128, 512])
            hT = self.sb("hT", [128, NF, 512], BF16)
            hT_b = S.buf("hT")
            pA = self.psn("pA", 2, [128, 512])
            pB = self.psn("pB", 2, [128, 512])
            pY = self.psn("pY", 2, [128, 512])
            self.load_gamma(self.gpre, self.gpre_b, self.norm_pre[ni])
            self.load_gamma(self.gpost, self.gpost_b, self.norm_post[ni])
            wc = [self.wcast_b[fi]]
            def do_front(gi_):
                t0_, n_ = self.groups[gi_]
                xT_, xTb_ = xnT.next()
                return (xT_, xTb_, [self.front(t0_ + i, src, src_bufs_of(t0_ + i), xT_, xTb_, i * 128) for i in range(n_)])
            nxt = do_front(0)
            for gi, (t0, n) in enumerate(self.groups):
                G = n * 128
                xT, xTb, xts = nxt
                for f in range(NF):
                    wgt, wgb = wg.next()
                    wut, wub = wu.next()
                    self.ld(wgt[:], self.wg_b[fi, f * 128:(f + 1) * 128, :], wc, [wgb], wgb)
                    self.ld(wut[:], self.wu_b[fi, f * 128:(f + 1) * 128, :], wc, [wub], wub)
                    a, ab = pA.next()
                    b, bb = pB.next()
                    for kc in range(8):
                        self.mm(a[:, 0:G], wgt[:, kc * 128:(kc + 1) * 128], xT[:, kc, 0:G], kc == 0, kc == 7,
                                [wgb, xTb], [ab], kc == 7)
                    for kc in range(8):
                        self.mm(b[:, 0:G], wut[:, kc * 128:(kc + 1) * 128], xT[:, kc, 0:G], kc == 0, kc == 7,
                                [wub, xTb], [bb], kc == 7)
                    s, sb_ = sg.next()
                    self.act(dict(out=s[:, 0:G], in_=a[:, 0:G], func=AF.Silu), [ab], [sb_])
                    self.dve("tensor_tensor", dict(out=hT[:, f, 0:G], in0=s[:, 0:G], in1=b[:, 0:G], op=ALU.mult),
                             [sb_, bb], [hT_b])
                if gi + 1 < len(self.groups):
                    nxt = do_front(gi + 1)
                wds = []
                for hf in range(2):
                    wdt, wdb = wd.next()
                    self.ld(wdt[:], self.wd_b[fi, :, hf * 11 * D:(hf + 1) * 11 * D], wc, [wdb], wdb)
                    wds.append((wdt, wdb))
                for i in range(n):
                    ys = []
                    for h in range(2):
                        y, ybuf = pY.next()
                        for f in range(NF):
                            hf, ff = divmod(f, 11)
                            self.mm(y[:], hT[:, f, i * 128:(i + 1) * 128],
                                    wds[hf][0][:, ff * D + h * 512: ff * D + (h + 1) * 512],
                                    f == 0, f == NF - 1, [hT_b, wds[hf][1]], [ybuf], f == NF - 1)
                        ys.append((y[:], ybuf))
                    self.post_residual(ys, xts[i][0], xts[i][1], dst, dst_bufs_of(t0 + i), t0 + i, 0.5)
            S.end_phase()

    def bc(self, ap, axis, shape):
        return ap.unsqueeze(axis).to_broadcast(list(shape))

    def sincos(self, th, thb, n, want):
        out = {}
        for name in want:
            shift = 0.0 if name == "sin" else math.pi / 2
            t = self.sb("sc_t", [128, n])
            ti = self.sb("sc_i", [128, n], I32)
            b = self.S.buf("sc")
            self.dve("tensor_scalar", dict(out=t[:], in0=th, scalar1=shift, scalar2=1.0 / TWO_PI, op0=ALU.add,
                                           op1=ALU.mult), [thb], [b])
            self.dve("tensor_copy", dict(out=ti[:], in_=t[:]), [b], [b])
            self.dve("tensor_copy", dict(out=t[:], in_=ti[:]), [b], [b])
            r = self.sb("sc_r", [128, n])
            self.dve("scalar_tensor_tensor", dict(out=r[:], in0=t[:], scalar=-C1_2PI, in1=th, op0=ALU.mult,
                                                  op1=ALU.add), [b, thb], [b])
            self.dve("scalar_tensor_tensor", dict(out=r[:], in0=t[:], scalar=-C2_2PI, in1=r[:], op0=ALU.mult,
                                                  op1=ALU.add), [b], [b])
            if shift != 0.0:
                self.dve("tensor_scalar", dict(out=r[:], in0=r[:], scalar1=shift, scalar2=None, op0=ALU.add), [b], [b])
            self.dve("tensor_scalar", dict(out=r[:], in0=r[:], scalar1=math.pi, scalar2=-math.pi, op0=ALU.min,
                                           op1=ALU.max), [b], [b])
            o = self.sb("sc_o", [128, n])
            self.act(dict(out=o[:], in_=r[:], func=AF.Sin), [b], [b])
            out[name] = (o, b)
        return out

    def cmul(self, o_re, o_im, a_re, a_im, b_re, b_im, reads, writes, tmp, neg_im=False):
        t1, t2 = tmp
        tt = lambda o, x, y, op: self.dve("tensor_tensor", dict(out=o, in0=x, in1=y, op=op), reads, writes)
        tt(t1, a_re, b_re, ALU.mult)
        tt(t2, a_im, b_im, ALU.mult)
        tt(o_re, t1, t2, ALU.subtract)
        tt(t1, a_re, b_im, ALU.mult)
        tt(t2, a_im, b_re, ALU.mult)
        tt(o_im, t1, t2, ALU.add)
        if neg_im:
            self.dve("tensor_scalar", dict(out=o_im, in0=o_im, scalar1=-1.0, scalar2=None, op0=ALU.mult), reads, writes)

    def s5(self, ni):
        S = self.S
        cfg = self.cfg
        NB, NBT, NTOK, TP, NS = cfg.NB, cfg.NBT, cfg.NTOK, cfg.TP, cfg.NS
        LOGNB = int(round(math.log2(NB)))
        assert 2 ** LOGNB == NB
        self.pes = ExitStack()
        with self.pes:
            self.common_alloc(4)
            self.load_gamma(self.gpre, self.gpre_b, self.norm_pre[ni])
            xnT = self.sbn("xnT", 2, [128, 8, 512], BF16)
            for gi, (t0, n) in enumerate(self.groups):
                G_ = n * 128
                xT, xTb = xnT.next()
                for i in range(n):
                    self.front(t0 + i, self.xs, [self.xs_b[t0 + i]], xT, xTb, i * 128)
                self.ld(self.uT_s[:, :, t0 * 128:t0 * 128 + G_].rearrange("c p t -> p c t"), xT[:, :, 0:G_], [xTb],
                        [], xTb, q="pool", shared=self.us_b)
            S.end_phase()
        self.pes = ExitStack()
        with self.pes:
            self.small = self.sbn("small", 8, [128, 8])
            self.pT = self.psn("pT", 2, [128, 1024], BF16)
            ucs = self.sbn("uc", 2, [128, NTOK], BF16)
            bk = self.psn("bk", 6, [128, 512])
            gb = S.buf("s5gen")
            G = [gb]
            are = self.sb("are", [128, 32]); aim = self.sb("aim", [128, 32]); ldt = self.sb("ldt", [128, 32])
            bre = self.sb("bre", [128, 32, 16]); bim = self.sb("bim", [128, 32, 16])
            cre = self.sb("cre", [128, 32, 16]); cim = self.sb("cim", [128, 32, 16])
            dcol = self.sb("dcol", [128, 8]); h0 = self.sb("h0", [128, 32, 2, NS])
            lb = S.bufs("s5ld", 9)
            for k, (t, src) in enumerate(((are, self.s5_are), (aim, self.s5_aim), (ldt, self.s5_ldt), (dcol, self.s5_d))):
                self.ld(t[:], src[:, :], [], [lb[k]], lb[k])
            for k, (t, src) in enumerate(((bre, self.s5_bre), (bim, self.s5_bim), (cre, self.s5_cre), (cim, self.s5_cim))):
                self.ld(t[:].rearrange("p a b -> p (a b)"), src[:, :], [], [lb[4 + k]], lb[4 + k])
            self.ld(h0[:].rearrange("p a b c -> p (a b c)"), self.s5_h0[:, :], [], [lb[8]], lb[8])
            LB = list(lb)
            dt = self.sb("dt", [128, 32]); trd = self.sb("trd", [128, 32]); th = self.sb("th", [128, 32])
            mag = self.sb("mag", [128, 32]); rho = self.sb("rho", [128, 32])
            self.act(dict(out=dt[:], in_=ldt[:], func=AF.Exp), LB, G)
            self.dve("tensor_tensor", dict(out=trd[:], in0=are[:], in1=dt[:], op=ALU.mult), LB + G, G)
            self.dve("tensor_tensor", dict(out=th[:], in0=aim[:], in1=dt[:], op=ALU.mult), LB + G, G)
            self.act(dict(out=mag[:], in_=trd[:], func=AF.Exp), G, G)
            self.act(dict(out=rho[:], in_=trd[:], func=AF.Exp, scale=8.0), G, G)
            sc = self.sincos(th[:], gb, 32, ("sin", "cos"))
            LRI = self.sb("LRI", [128, 9, 2, 32])
            nLI = self.sb("nLI", [128, 9, 32])
            t1 = self.sb("t1", [128, 32]); t2 = self.sb("t2", [128, 32])
            self.dve("memset", dict(ap=LRI[:, 0, 0, :], constant=1.0), [], G)
            self.dve("memset", dict(ap=LRI[:, 0, 1, :], constant=0.0), [], G)
            self.dve("tensor_tensor", dict(out=LRI[:, 1, 0, :], in0=mag[:], in1=sc["cos"][0][:], op=ALU.mult),
                     G + [sc["cos"][1]], G)
            self.dve("tensor_tensor", dict(out=LRI[:, 1, 1, :], in0=mag[:], in1=sc["sin"][0][:], op=ALU.mult),
                     G + [sc["sin"][1]], G)
            for tau in range(2, 9):
                self.cmul(LRI[:, tau, 0, :], LRI[:, tau, 1, :], LRI[:, tau - 1, 0, :], LRI[:, tau - 1, 1, :],
                          LRI[:, 1, 0, :], LRI[:, 1, 1, :], G, G, (t1[:], t2[:]))
            self.dve("tensor_scalar", dict(out=nLI[:], in0=LRI[:, :, 1, :], scalar1=-1.0, scalar2=None, op0=ALU.mult), G, G)
            gre = self.sb("gre", [128, 32]); gim = self.sb("gim", [128, 32]); nr = self.sb("nr", [128, 32])
            den = self.sb("den", [128, 32])
            tt = lambda o, x, y, op: self.dve("tensor_tensor", dict(out=o, in0=x, in1=y, op=op), LB + G, G)
            self.dve("tensor_scalar", dict(out=nr[:], in0=LRI[:, 1, 0, :], scalar1=-1.0, scalar2=None, op0=ALU.add), G, G)
            tt(t1[:], are[:], are[:], ALU.mult)
            tt(t2[:], aim[:], aim[:], ALU.mult)
            tt(den[:], t1[:], t2[:], ALU.add)
            self.dve("reciprocal", dict(out=den[:], in_=den[:]), G, G)
            tt(t1[:], nr[:], are[:], ALU.mult)
            tt(t2[:], LRI[:, 1, 1, :], aim[:], ALU.mult)
            tt(gre[:], t1[:], t2[:], ALU.add)
            tt(gre[:], gre[:], den[:], ALU.mult)
            tt(t1[:], LRI[:, 1, 1, :], are[:], ALU.mult)
            tt(t2[:], nr[:], aim[:], ALU.mult)
            tt(gim[:], t1[:], t2[:], ALU.subtract)
            tt(gim[:], gim[:], den[:], ALU.mult)
            Pk = self.sb("Pk", [128, LOGNB + 1, 2, 32])
            self.dve("reciprocal", dict(out=t1[:], in_=rho[:]), G, G)
            tt(Pk[:, 0, 0, :], LRI[:, 8, 0, :], t1[:], ALU.mult)
            tt(Pk[:, 0, 1, :], LRI[:, 8, 1, :], t1[:], ALU.mult)
            for k in range(LOGNB):
                self.cmul(Pk[:, k + 1, 0, :], Pk[:, k + 1, 1, :], Pk[:, k, 0, :], Pk[:, k, 1, :],
                          Pk[:, k, 0, :], Pk[:, k, 1, :], G, G, (t1[:], t2[:]))
            Bre = self.sb("Bre", [128, 32, 16]); Bim = self.sb("Bim", [128, 32, 16])
            T1 = self.sb("T1", [128, 32, 16]); T2 = self.sb("T2", [128, 32, 16])
            gre_b = self.bc(gre[:], 2, [128, 32, 16]); gim_b = self.bc(gim[:], 2, [128, 32, 16])
            self.cmul(Bre[:], Bim[:], bre[:], bim[:], gre_b, gim_b, LB + G, G, (T1[:], T2[:]))
            Wre = self.sb("Wre", [128, 8, 4, 16]); Wim = self.sb("Wim", [128, 8, 4, 16])
            Wt1 = self.sb("Wt1", [128, 9, 4, 16]); Wt2 = self.sb("Wt2", [128, 9, 4, 16])
            VVre = self.sb("VVre", [128, 9, 4, 16]); VVim = self.sb("VVim", [128, 9, 4, 16])
            Wexp = self.sb("Wexp", [128, 2, 8, 4, 2, 16], BF16)
            VVexp = self.sb("VVexp", [128, 2, 9, 4, 2, 16], BF16)
            Bexp = self.sb("Bexp", [128, 2, 4, 2, 16], BF16)
            WTz = self.sb("WTz", [128, 4, 2, 8, 128], BF16)
            VVz = self.sb("VVz", [128, 8, 4, 2, 128], BF16)
            BD = self.sb("BD", [128, 8, 128], BF16)
            Ect = self.sb("Ec", [128, 4, NB]); Est = self.sb("Es", [128, 4, NB])
            w_sb = self.sb("w_sb", [128, 4, 2, NBT])
            v_sb = self.sb("v_sb", [128, 4, 2, NB])
            g_sb = self.sb("g_sb", [128, 4, 2, NB])
            zbf = self.sb("zbf", [128, 4, 2, NBT], BF16)
            zs = self.sb("zs", [128, 4, 2, NS])
            zfin = self.sb("zfin", [128, 8, 4, 2])
            E1 = self.sb("E1", [128, 4, NB]); E2 = self.sb("E2", [128, 4, NB])
            ytmp = self.sbn("ytmp", 2, [128, 512])
            cb = S.buf("s5chunk")
            C = [cb]
            for t in (Wexp, VVexp, Bexp, WTz, VVz):
                self.pool("memset", dict(ap=t[:], constant=0.0), [], C)
            self.dve("memset", dict(ap=zbf[:, :, :, 0:1], constant=0.0), [], C)
            for c in range(8):
                ps = slice(4 * c, 4 * c + 4)
                RD = LB + G + C
                uc, ucb = ucs.next()
                self.ld(uc[:], self.uT_s[c], [self.us_b[c]], [ucb], ucb)
                lr8 = self.bc(LRI[:, 0:8, 0, ps], 3, [128, 8, 4, 16]); li8 = self.bc(LRI[:, 0:8, 1, ps], 3, [128, 8, 4, 16])
                Bre8 = self.bc(Bre[:, ps, :], 1, [128, 8, 4, 16]); Bim8 = self.bc(Bim[:, ps, :], 1, [128, 8, 4, 16])
                self.cmul(Wre[:], Wim[:], lr8, li8, Bre8, Bim8, RD, C, (Wt1[:, 0:8], Wt2[:, 0:8]))
                lr9 = self.bc(LRI[:, :, 0, ps], 3, [128, 9, 4, 16]); li9 = self.bc(LRI[:, :, 1, ps], 3, [128, 9, 4, 16])
                cre9 = self.bc(cre[:, ps, :], 1, [128, 9, 4, 16]); cim9 = self.bc(cim[:, ps, :], 1, [128, 9, 4, 16])
                self.cmul(VVre[:], VVim[:], cre9, cim9, lr9, li9, RD, C, (Wt1[:], Wt2[:]), neg_im=True)
                for g2, pr in ((0, slice(0, 64)), (1, slice(64, 128))):
                    for ri, (wsrc, vsrc, bsrc) in enumerate(((Wre, VVre, Bre), (Wim, VVim, Bim))):
                        self.dve("tensor_copy", dict(out=Wexp[pr, ri, :, :, g2, :], in_=wsrc[pr]), RD, C)
                        self.dve("tensor_copy", dict(out=VVexp[pr, ri, :, :, g2, :], in_=vsrc[pr]), RD, C)
                        self.dve("tensor_copy", dict(out=Bexp[pr, ri, :, g2, :], in_=bsrc[pr, ps, :]), RD, C)
                for ri in range(2):
                    p, pb = self.pT.next()
                    for n in range(8):
                        self.tr(p[:, n * 128:(n + 1) * 128], Wexp[:, ri, n].rearrange("p a b c -> p (a b c)"), C, [pb],
                                inc=(n == 7))
                    for p4 in range(4):
                        self.S.op("act", "copy", dict(out=WTz[32 * p4:32 * p4 + 32, p4, ri, :, :],
                                                      in_=p[32 * p4:32 * p4 + 32, :].rearrange("p (n q) -> p n q", q=128)),
                                  [pb], C)
                for p4 in range(4):
                    for ri in range(2):
                        self.dve("tensor_copy", dict(out=VVz[:, :, p4, ri, 32 * p4:32 * p4 + 32],
                                                     in_=VVexp[:, ri, 1:9, p4].rearrange("p t a b -> p t (a b)")), C, C)
                for half in range(2):
                    bank, bb = bk.next()
                    first = True
                    for t4 in range(4):
                        tau = half * 4 + t4
                        for ri in range(2):
                            self.mm(bank[:, t4 * 128:(t4 + 1) * 128], Bexp[:, ri].rearrange("p a b c -> p (a b c)"),
                                    VVexp[:, ri, tau].rearrange("p a b c -> p (a b c)"), ri == 0, ri == 1,
                                    C, [bb], (t4 == 3 and ri == 1))
                    self.dve("tensor_tensor", dict(out=BD[:, half * 4:half * 4 + 4, :],
                                                   in0=bank[:, :].rearrange("p (t n) -> p t n", n=128),
                                                   in1=self.bc(self.mask32[:], 1, [128, 4, 128]), op=ALU.mult),
                             [bb, self.mask32_b], C)
                self.dve("memset", dict(ap=Ect[:, :, 0:1], constant=1.0), [], C)
                self.dve("memset", dict(ap=Est[:, :, 0:1], constant=0.0), [], C)
                for k in range(LOGNB):
                    n = 2 ** k
                    pc = self.bc(Pk[:, k, 0, ps], 2, [128, 4, n]); psn_ = self.bc(Pk[:, k, 1, ps], 2, [128, 4, n])
                    self.cmul(Ect[:, :, n:2 * n], Est[:, :, n:2 * n], Ect[:, :, 0:n], Est[:, :, 0:n], pc, psn_, G + C, C,
                              (E1[:, :, 0:n], E2[:, :, 0:n]))
                for gi, (t0, n) in enumerate(self.groups):
                    c0 = t0 * 128
                    ntok = n * 128
                    nb = ntok // 8
                    gb0 = c0 // 8
                    bank, bb = bk.next()
                    first = True
                    for p4 in range(4):
                        for ri in range(2):
                            for s_ in range(8):
                                self.mm(bank[:, (p4 * 2 + ri) * 64:(p4 * 2 + ri) * 64 + nb], WTz[:, p4, ri, 7 - s_, :],
                                        uc[:, c0 + s_:c0 + ntok:8], s_ == 0, s_ == 7, C + [ucb], [bb],
                                        (p4 == 3 and ri == 1 and s_ == 7))
                    self.S.op("act", "copy", dict(out=w_sb[:, :, :, gb0:gb0 + nb],
                                                  in_=bank[:, :].rearrange("p (a b n) -> p a b n", a=4, b=2)[:, :, :, 0:nb]),
                              [bb], C)
                wre_ = w_sb[:, :, 0, 0:NB]; wim_ = w_sb[:, :, 1, 0:NB]
                tt2 = lambda o, x, y, op: self.dve("tensor_tensor", dict(out=o, in0=x, in1=y, op=op), C, C)
                tt2(E1[:], Ect[:], wre_, ALU.mult); tt2(E2[:], Est[:], wim_, ALU.mult)
                tt2(v_sb[:, :, 0, :], E1[:], E2[:], ALU.add)
                tt2(E1[:], Ect[:], wim_, ALU.mult); tt2(E2[:], Est[:], wre_, ALU.mult)
                tt2(v_sb[:, :, 1, :], E1[:], E2[:], ALU.subtract)
                for p4 in range(4):
                    for ri in range(2):
                        self.dve("tensor_tensor_scan", dict(out=g_sb[:, p4, ri, :],
                                                            data0=rho[:, 4 * c + p4:4 * c + p4 + 1].to_broadcast([128, NB]),
                                                            data1=v_sb[:, p4, ri, :], initial=0.0, op0=ALU.mult, op1=ALU.add),
                                 G + C, C)
                tt2(E1[:], Ect[:], g_sb[:, :, 0, :], ALU.mult); tt2(E2[:], Est[:], g_sb[:, :, 1, :], ALU.mult)
                tt2(zbf[:, :, 0, 1:NB], E1[:, :, 0:NB - 1], E2[:, :, 0:NB - 1], ALU.subtract)
                tt2(zfin[:, c, :, 0], E1[:, :, NB - 1], E2[:, :, NB - 1], ALU.subtract)
                tt2(E1[:], Ect[:], g_sb[:, :, 1, :], ALU.mult); tt2(E2[:], Est[:], g_sb[:, :, 0, :], ALU.mult)
                tt2(zbf[:, :, 1, 1:NB], E1[:, :, 0:NB - 1], E2[:, :, 0:NB - 1], ALU.add)
                tt2(zfin[:, c, :, 1], E1[:, :, NB - 1], E2[:, :, NB - 1], ALU.add)
                self.dve("tensor_copy", dict(out=zbf[:, :, :, NB:NBT], in_=h0[:, ps, :, :]), LB + C, C)
                self.ld(self.ssm_p[:, 8 * c:8 * c + 8], zfin[:, c].rearrange("p a b -> p (a b)"), C, [], cb, q="pool", shared=[self.ssm_b])
                l8r = self.bc(LRI[:, 8, 0, ps], 2, [128, 4, NS]); l8i = self.bc(LRI[:, 8, 1, ps], 2, [128, 4, NS])
                self.cmul(zs[:, :, 0, :], zs[:, :, 1, :], h0[:, ps, 0, :], h0[:, ps, 1, :], l8r, l8i, LB + G + C, C,
                          (E1[:, :, 0:NS], E2[:, :, 0:NS]))
                tt2(zs[:], zs[:], w_sb[:, :, :, NB:NBT], ALU.add)
                self.ld(self.ssm_s[:, 4 * c * 2 * NS:(4 * c + 4) * 2 * NS], zs[:].rearrange("p a b c -> p (a b c)"), C,
                        [], cb, q="pool", shared=[self.ssm_b])
                for gi, (t0, n) in enumerate(self.groups):
                    c0 = t0 * 128
                    ntok = n * 128
                    nb = ntok // 8
                    gb0 = (c0 // 8) if c0 < TP else NB
                    bank, bb = bk.next()
                    first = True
                    for r in range(8):
                        for tau in range(r + 1):
                            self.mm(bank[:, r * 64:r * 64 + nb], BD[:, tau, :], uc[:, c0 + r - tau:c0 + ntok:8],
                                    tau == 0, False, C + [ucb], [bb], False)
                        for p4 in range(4):
                            for ri in range(2):
                                last = (r == 7 and p4 == 3 and ri == 1)
                                self.mm(bank[:, r * 64:r * 64 + nb], VVz[:, r, p4, ri, :], zbf[:, p4, ri, gb0:gb0 + nb],
                                        False, (p4 == 3 and ri == 1), C, [bb], last)
                    yt_, ytb = ytmp.next()
                    self.dve("scalar_tensor_tensor", dict(
                        out=yt_[:, 0:ntok].rearrange("p (b r) -> p b r", r=8),
                        in0=uc[:, c0:c0 + ntok].rearrange("p (b r) -> p b r", r=8), scalar=dcol[:, c:c + 1],
                        in1=bank[:, :].rearrange("p (r b) -> p b r", r=8)[:, 0:nb, :], op0=ALU.mult, op1=ALU.add),
                        [bb, ucb] + LB, [ytb])
                    self.act(dict(out=uc[:, c0:c0 + ntok], in_=yt_[:, 0:ntok], func=AF.Gelu_apprx_tanh), [ytb],
                             [ucb])
                self.ld(self.uT_s[c], uc[:], [ucb], [self.us_b[c]], ucb, q="pool")
            S.end_phase()
        self.pes = ExitStack()
        with self.pes:
            self.common_alloc(4)
            self.load_gamma(self.gpost, self.gpost_b, self.norm_post[ni])
            wglu = self.sb("wglu", [128, 8, 2048], BF16)
            wglu_b = S.buf("wglu")
            self.ld(wglu[:].rearrange("p a b -> p (a b)"), self.wglu_b[:, :], [self.wcast2_b], [wglu_b], wglu_b)
            bk = self.psn("bk", 6, [128, 512])
            hTs = self.sbn("hTt", 3, [128, 8, 128], BF16)
            sgl = self.sbn("sgl", 2, [128, 512])
            mo = self.sbn("mo", 2, [128, D])
            for gi, (t0, n) in enumerate(self.groups):
                for i in range(n):
                    ti = t0 + i
                    xt, xb = self.xt.next()
                    self.ld(xt[:], self.xs[ti * 128:(ti + 1) * 128, :], [self.xs_b[ti]], [xb], xb)
                    hT, hTb = hTs.next()
                    self.ld(hT[:], self.uT_s[:, :, ti * 128:(ti + 1) * 128].rearrange("c p t -> p c t"), self.us_b, [hTb], hTb)
                    m, mb = mo.next()
                    for h in range(2):
                        zv, zvb = bk.next()
                        zg, zgb = bk.next()
                        for (bank, bb, col) in ((zv, zvb, h * 512), (zg, zgb, 1024 + h * 512)):
                            for kc in range(8):
                                self.mm(bank[:], hT[:, kc, :], wglu[:, kc, col:col + 512], kc == 0,
                                        kc == 7, [hTb, wglu_b], [bb], kc == 7)
                        sg, sgb = sgl.next()
                        self.act(dict(out=sg[:], in_=zg[:], func=AF.Sigmoid), [zgb], [sgb])
                        self.dve("tensor_tensor", dict(out=m[:, h * 512:(h + 1) * 512], in0=sg[:], in1=zv[:], op=ALU.mult),
                                 [sgb, zvb], [mb])
                    self.post_residual([(m[:, 0:512], mb), (m[:, 512:1024], mb)], xt, xb, self.xs, [self.xs_b[ti]], ti, 1.0)
            S.end_phase()

    def mla(self, ni):
        S = self.S
        cfg = self.cfg
        NT, NTOK, TP, NS, NPG = cfg.NT, cfg.NTOK, cfg.TP, cfg.NS, cfg.NPG
        NTP = TP // 128
        QTs = self.gsb("QTs", [128, NH, 128], BF16)
        ckvn_s = self.gsb("ckvn_s", [128, 289], BF16)
        ckvnT_s = self.gsb("ckvnT_s", [128, 2, 128], BF16)
        kpeT_s = self.gsb("kpeT_s", [32, 128], BF16)
        smp_b = S.buf("smp")
        SM = [smp_b]
        self.pes = ExitStack()
        with self.pes:
            self.common_alloc(4)
            self.load_gamma(self.gpre, self.gpre_b, self.norm_pre[ni])
            xnT = self.sbn("xnT", 2, [128, 8, 512], BF16)
            win = self.sb("win", [128, 8, 1056], BF16)
            wuq = self.sb("wuq", [128, 6, 1536], BF16)
            wukv = self.sb("wukv", [128, 2, 2048], BF16)
            wb = S.bufs("mlaw", 3)
            for k, (t, src) in enumerate(((win, self.win_b), (wuq, self.wuq_b), (wukv, self.wukv_b))):
                self.ld(t[:].rearrange("p a b -> p (a b)"), src[:, :], [self.wcast2_b], [wb[k]], wb[k])
            qn_bc = self.sb("qn_bc", [128, QL]); kvn_bc = self.sb("kvn_bc", [128, KVL])
            nb_ = S.bufs("nrm", 2)
            self.ld(qn_bc[:], self.qnorm[0].partition_broadcast(128), [], [nb_[0]], nb_[0])
            self.ld(kvn_bc[:], self.kvnorm[0].partition_broadcast(128), [], [nb_[1]], nb_[1])
            tb = S.buf("ropetab")
            TB = [tb]
            posf = self.sb("posf", [128, NT]); pi_ = self.sb("pi_", [128, 1], I32); invf = self.sb("invf", [128, 16])
            ang = self.sb("ang", [128, NT, 16])
            self.pool("iota", dict(out=posf[:, 0:NTP], pattern=[[128, NTP]], base=0, channel_multiplier=1,
                                   allow_small_or_imprecise_dtypes=True), [], TB)
            self.pool("iota", dict(out=pi_[:], pattern=[[0, 1]], base=0, channel_multiplier=1), [], TB)
            if "and" not in os.environ.get("M1_SKIP", ""):
                self.dve("tensor_single_scalar", dict(out=pi_[:], in_=pi_[:], scalar=7, op=ALU.bitwise_and), TB, TB)
            self.dve("tensor_copy", dict(out=posf[:, NTP:NT], in_=pi_[:]), TB, TB)
            self.dve("tensor_scalar", dict(out=posf[:, NTP:NT], in0=posf[:, NTP:NT], scalar1=float(cfg.PAST), scalar2=None,
                                           op0=ALU.add), TB, TB)
            self.pool("iota", dict(out=invf[:], pattern=[[1, 16]], base=0, channel_multiplier=0,
                                   allow_small_or_imprecise_dtypes=True), [], TB)
            self.act(dict(out=invf[:], in_=invf[:], func=AF.Exp, scale=-math.log(ROPE_THETA) / 16.0), TB, TB)
            self.dve("tensor_tensor", dict(out=ang[:], in0=self.bc(posf[:], 2, [128, NT, 16]),
                                           in1=self.bc(invf[:], 1, [128, NT, 16]), op=ALU.mult), TB, TB)
            sc = self.sincos(ang[:].rearrange("p a b -> p (a b)"), tb, NT * 16, ("sin", "cos"))
            cosT = sc["cos"][0][:].rearrange("p (a b) -> p a b", b=16)
            sinT = sc["sin"][0][:].rearrange("p (a b) -> p a b", b=16)
            RT = [sc["cos"][1], sc["sin"][1]]
            bk = self.psn("bk", 6, [128, 512])
            cqn = self.sbn("cqn", 2, [128, QL], BF16)
            ckf = self.sbn("ckf", 2, [128, KVL])
            ckb = self.sbn("ckb", 2, [128, KVL], BF16)
            kpf = self.sbn("kpf", 2, [128, RP])
            rt = self.sbn("rt", 2, [128, 2, 16, 16])
            cqnT = self.sbn("cqnT", 2, [128, 6, 128], BF16)
            ckvT = self.sbn("ckvT", 2, [128, 2, 128], BF16)
            qsb = self.sbn("qsb", 2, [128, NH, DQK], BF16)
            ksb = self.sbn("ksb", 2, [128, NH, DQK], BF16)
            vsb = self.sbn("vsb", 2, [128, NH, 66], BF16)
            for k in range(2):
                self.pool("memset", dict(ap=vsb.t[k][:, :, 64:66], constant=1.0), [], [vsb.b[k]])
            self.pool("memset", dict(ap=ckvn_s[:, 256:288], constant=0.0), [], SM)
            self.pool("memset", dict(ap=ckvn_s[:, 288:289], constant=1.0), [], SM)
            QTst = self.sbn("QTst", 1, [128, NH, 512], BF16)
            KTst = self.sbn("KTst", 1, [128, NH, 512], BF16)
            STOP = os.environ.get("M1_STOP", "")
            if os.environ.get("KDEBUG"):
                print("M1 sbuf remaining", self.nc.sbuf_bytes_remaining)
            for gi, (t0, n) in enumerate(self.groups if STOP != "A" else []):
                G = n * 128
                c0 = t0 * 128
                is_s = (c0 >= TP)
                xT, xTb = xnT.next()
                for i in range(n):
                    self.front(t0 + i, self.xs, [self.xs_b[t0 + i]], xT, xTb, i * 128)
                qst, qstb = QTst.next()
                kst, kstb = KTst.next()
                for i in range(n):
                    ti = t0 + i
                    rows = slice(ti * 128, (ti + 1) * 128)
                    pj = [bk.next() for _ in range(3)]
                    for (bank, bb), (col, ncol) in zip(pj, ((0, 512), (512, 512), (1024, 32))):
                        for kc in range(8):
                            self.mm(bank[:, 0:ncol], xT[:, kc, i * 128:(i + 1) * 128], win[:, kc, col:col + ncol],
                                    kc == 0, kc == 7, [xTb, wb[0]], [bb], kc == 7)
                    (p0, p0b), (p1, p1b), (p2, p2b) = pj
                    rq, rqb = self.rstd_of([(p0[:, 0:512], [p0b]), (p1[:, 0:256], [p1b])], QL)
                    cq, cqb = cqn.next()
                    self.dve("scalar_tensor_tensor", dict(out=cq[:, 0:512], in0=p0[:, 0:512], scalar=rq, in1=qn_bc[:, 0:512],
                                                          op0=ALU.mult, op1=ALU.mult), [p0b, rqb, nb_[0]], [cqb])
                    self.dve("scalar_tensor_tensor", dict(out=cq[:, 512:768], in0=p1[:, 0:256], scalar=rq,
                                                          in1=qn_bc[:, 512:768], op0=ALU.mult, op1=ALU.mult),
                             [p1b, rqb, nb_[0]], [cqb])
                    rk, rkb = self.rstd_of([(p1[:, 256:512], [p1b])], KVL)
                    cf, cfb = ckf.next()
                    self.dve("scalar_tensor_tensor", dict(out=cf[:], in0=p1[:, 256:512], scalar=rk, in1=kvn_bc[:],
                                                          op0=ALU.mult, op1=ALU.mult), [p1b, rkb, nb_[1]], [cfb])
                    self.ld(self.lat_out[rows, :], cf[:], [cfb], [], cfb, q="pool", shared=[self.mla_out_b])
                    cb_, cbb = ckb.next()
                    self.S.op("act", "copy", dict(out=cb_[:], in_=cf[:]), [cfb], [cbb])
                    if is_s:
                        self.S.op("act", "copy", dict(out=ckvn_s[:, 0:256], in_=cf[:]), [cfb], SM)
                    kp, kpb = kpf.next()
                    r_, rb_ = rt.next()
                    cs = cosT[:, ti, :]
                    sn = sinT[:, ti, :]
                    t1 = r_[:, 0, 0, :]; t2 = r_[:, 1, 0, :]
                    tt = lambda o, x, y, op: self.dve("tensor_tensor", dict(out=o, in0=x, in1=y, op=op),
                                                      [p2b, rb_] + RT, [rb_, kpb])
                    tt(t1, p2[:, 0:16], cs, ALU.mult); tt(t2, p2[:, 16:32], sn, ALU.mult)
                    tt(kp[:, 0:16], t1, t2, ALU.subtract)
                    tt(t1, p2[:, 0:16], sn, ALU.mult); tt(t2, p2[:, 16:32], cs, ALU.mult)
                    tt(kp[:, 16:32], t1, t2, ALU.add)
                    self.ld(self.kpe_out[rows, :], kp[:], [kpb], [], kpb, q="pool", shared=[self.mla_out_b])
                    if STOP == "B":
                        continue
                    cqT, cqTb = cqnT.next()
                    self.transpose_into(cq, cqb, 6, cqT, cqTb, 0, eng="dve")
                    ckT, ckTb = ckvT.next()
                    self.transpose_into(cb_, cbb, 2, ckT, ckTb, 0, eng="dve")
                    if is_s:
                        self.dve("tensor_copy", dict(out=ckvnT_s[:], in_=ckT[:]), [ckTb], SM)
                    qb_ = [bk.next() for _ in range(3)]
                    for nbk, (bank, bb) in enumerate(qb_):
                        for kc in range(6):
                            self.mm(bank[:], cqT[:, kc, :], wuq[:, kc, nbk * 512:(nbk + 1) * 512], kc == 0, kc == 5,
                                    [cqTb, wb[1]], [bb], kc == 5)
                    q_, q_b = qsb.next()
                    for hh in range(2):
                        self.S.op("act", "copy", dict(out=q_[:, hh * 8:(hh + 1) * 8, 0:64],
                                                      in_=qb_[hh][0][:, :].rearrange("p (h d) -> p h d", d=64)),
                                  [qb_[hh][1]], [q_b])
                    qr = qb_[2][0][:, :].rearrange("p (h r) -> p h r", r=32)
                    qrb = qb_[2][1]
                    csb = self.bc(cs, 1, [128, 16, 16]); snb = self.bc(sn, 1, [128, 16, 16])
                    T1 = r_[:, 0]; T2 = r_[:, 1]
                    tq = lambda o, x, y, op: self.dve("tensor_tensor", dict(out=o, in0=x, in1=y, op=op),
                                                      [qrb, rb_] + RT, [rb_, q_b])
                    tq(T1, qr[:, :, 0:16], csb, ALU.mult); tq(T2, qr[:, :, 16:32], snb, ALU.mult)
                    tq(q_[:, :, 64:80], T1, T2, ALU.subtract)
                    tq(T1, qr[:, :, 0:16], snb, ALU.mult); tq(T2, qr[:, :, 16:32], csb, ALU.mult)
                    tq(q_[:, :, 80:96], T1, T2, ALU.add)
                    for hh in range(2):
                        p, pb = self.pT.next()
                        for h8 in range(8):
                            self.tr(p[0:96, h8 * 128:(h8 + 1) * 128], q_[:, hh * 8 + h8, :], [q_b], [pb], inc=(h8 == 7))
                        dst = QTs[0:96, hh * 8:(hh + 1) * 8, :] if is_s else qst[0:96, hh * 8:(hh + 1) * 8, i * 128:(i + 1) * 128]
                        self.S.op("act", "copy", dict(out=dst, in_=p[0:96, :].rearrange("p (h n) -> p h n", n=128)),
                                  [pb], SM if is_s else [qstb])
                    if STOP == "C":
                        continue
                    if is_s:
                        kb16 = cq
                        kpb16 = self.sb("kpb16", [128, RP], BF16)
                        kpb16_b = S.buf("kpb16")
                        self.dve("tensor_copy", dict(out=kpb16[:], in_=kp[:]), [kpb], [kpb16_b])
                        if "kpT" not in os.environ.get("M1_SKIP", ""):
                            p, pb = self.pT.next()
                            self.tr(p[0:32, 0:128], kpb16[:, :], [kpb16_b], [pb], inc=True)
                            self.dve("tensor_copy", dict(out=kpeT_s[:, :], in_=p[0:32, 0:128]), [pb], SM)
                        continue
                    kvb = [bk.next() for _ in range(4)]
                    for nbk, (bank, bb) in enumerate(kvb):
                        for kc in range(2):
                            self.mm(bank[:], ckT[:, kc, :], wukv[:, kc, nbk * 512:(nbk + 1) * 512], kc == 0, kc == 1,
                                    [ckTb, wb[2]], [bb], kc == 1)
                    if STOP == "D1":
                        continue
                    k_, k_b = ksb.next()
                    v_, v_b = vsb.next()
                    for nbk, (bank, bb) in enumerate(kvb):
                        kv4 = bank[:, :].rearrange("p (h e) -> p h e", e=128)
                        self.S.op("act", "copy", dict(out=k_[:, nbk * 4:(nbk + 1) * 4, 0:64], in_=kv4[:, :, 0:64]), [bb], [k_b])
                        self.dve("tensor_copy", dict(out=v_[:, nbk * 4:(nbk + 1) * 4, 0:64], in_=kv4[:, :, 64:128]), [bb], [v_b])
                    if STOP == "D2":
                        continue
                    self.dve("tensor_copy", dict(out=k_[:, :, 64:96], in_=self.bc(kp[:], 1, [128, NH, RP])), [kpb], [k_b])
                    if STOP == "D3":
                        continue
                    if "vst" not in os.environ.get("M1_SKIP", ""):
                        self.ld(self.v_s[rows, :], v_[:].rearrange("p h e -> p (h e)"), [v_b], [], v_b, q="pool", shared=[self.qkv_s_b])
                    for hh in range(2):
                        p, pb = self.pT.next()
                        for h8 in range(8):
                            self.tr(p[0:96, h8 * 128:(h8 + 1) * 128], k_[:, hh * 8 + h8, :], [k_b], [pb], inc=(h8 == 7))
                        self.dve("tensor_copy", dict(out=kst[0:96, hh * 8:(hh + 1) * 8, i * 128:(i + 1) * 128],
                                                     in_=p[0:96, :].rearrange("p (h n) -> p h n", n=128)), [pb], [kstb])
                if not is_s and "qkst" not in os.environ.get("M1_SKIP", ""):
                    self.ld(self.qT_s[:, :, c0:c0 + G].rearrange("h d t -> d h t"), qst[0:96, :, 0:G], [qstb],
                            [], qstb, q="pool", shared=[self.qkv_s_b])
                    self.ld(self.kT_s[:, :, c0:c0 + G].rearrange("h d t -> d h t"), kst[0:96, :, 0:G], [kstb],
                            [], kstb, q="pool", shared=[self.qkv_s_b])
            S.end_phase()

        oTs = self.gsb("oTs", [128, 8, 128], BF16)
        oTs_b = S.buf("oTs")
        if os.environ.get("MLA_M1_ONLY"):
            return
        SKIP = os.environ.get("MLA_SKIP", "").split(",")
        self.pes = ExitStack()
        with self.pes:
            self.pT = self.psn("pT", 2, [128, 1024], BF16)
            bko = self.psn("bko", 4, [128, 512])
            bks = self.psn("bks", 2, [128, 512])
            bk = bko
            PT = self.sbn("PT", 3, [128, 512], BF16)
            wukT = self.sb("wukT", [64, NH, 256], BF16)
            wukv = self.sb("wukv2", [128, 2, 2048], BF16)
            wuvz = self.sb("wuvz", [128, 2, NH, 128], BF16)
            wb = S.bufs("mlaw2", 4)
            self.ld(wukT[:].rearrange("p a b -> p (a b)"), self.wukT_b[:, :], [self.wcast2_b], [wb[0]], wb[0])
            self.ld(wukv[:].rearrange("p a b -> p (a b)"), self.wukv_b[:, :], [self.wcast2_b], [wb[1]], wb[1])
            self.pool("memset", dict(ap=wuvz[:], constant=0.0), [], [wb[2]])
            for cc in range(2):
                for par in range(2):
                    src = wukv[:, cc, :].rearrange("p (k two e) -> p k two e", two=2, e=128)[:, :, par, 64:128]
                    dst = wuvz[:, cc].rearrange("p (k two) n -> p k two n", two=2)[:, :, par, par * 64:(par + 1) * 64]
                    self.dve("tensor_copy", dict(out=dst, in_=src), [wb[1]], [wb[2]])
            qlatT = self.sb("qlatT", [128, 2, NS, 8, NH], BF16)
            qrT = self.sb("qrT", [32, NS, 8, NH], BF16)
            olatT = self.sb("olatT", [128, 2, NS, 8, NH], BF16)
            ql_b = S.buf("qlat")
            ol_b = S.buf("olat")
            for h in range(NH):
                for cc in range(2):
                    bank, bb = bk.next()
                    self.mm(bank[:, 0:128], wukT[0:64, h, cc * 128:(cc + 1) * 128], QTs[0:64, h, :], True, True,
                            [wb[0]] + SM, [bb], True)
                    self.dve("tensor_copy", dict(out=qlatT[:, cc].rearrange("p s t h -> p (s t) h")[:, :, h],
                                                 in_=bank[:, 0:128]), [bb], [ql_b])
                bank, bb = bk.next()
                self.mm(bank[0:32, 0:128], self.ident[0:96, 64:96], QTs[0:96, h, :], True, True, [self.ident_b] + SM, [bb], True)
                self.dve("tensor_copy", dict(out=qrT[:].rearrange("p s t h -> p (s t) h")[:, :, h], in_=bank[0:32, 0:128]),
                         [bb], [ql_b])
            nmask = self.sb("nmask", [8, 8, NH], BF16)
            nm_b = S.buf("nmask")
            self.pool("memset", dict(ap=nmask[:], constant=1.0), [], [nm_b])
            self.pool("affine_select", dict(out=nmask[:], in_=nmask[:], pattern=[[1, 8], [0, NH]], compare_op=ALU.is_ge,
                                            fill=0.0, base=0, channel_multiplier=-1), [], [nm_b])
            ptb_ = self.sb("ptb", [128, NS * NPG], I32)
            idx = self.sb("idx", [128, NS * NPG // 4], I32)
            iof = self.sb("iof", [128, 1])
            ix_b = S.buf("idx")
            self.ld(ptb_[:], self.ptab.rearrange("s j -> (s j)").partition_broadcast(128), [], [ix_b], ix_b)
            for q4 in range(4):
                self.pool("iota", dict(out=iof[32 * q4:32 * q4 + 32, :], pattern=[[0, 1]], base=0, channel_multiplier=1,
                                       allow_small_or_imprecise_dtypes=True), [], [ix_b])
            for q4 in range(4):
                self.dve("tensor_scalar", dict(out=idx[32 * q4:32 * q4 + 32, :],
                                               in0=ptb_[32 * q4:32 * q4 + 32, :].rearrange("p (m f) -> p m f", f=4)[:, :, q4],
                                               scalar1=32.0, scalar2=iof[32 * q4:32 * q4 + 32, 0:1], op0=ALU.mult,
                                               op1=ALU.add), [ix_b], [ix_b])
            ones_k = self.sb("ones_k", [128, 1], BF16)
            self.dve("memset", dict(ap=ones_k[:], constant=1.0), [], [ix_b])
            cache4 = self.cache_cat.rearrange("(a f) c -> a (f c)", f=4)
            NPGS = 6
            pg = self.sbn("pg", NPGS, [128, 4, 288], BF16)
            ckT4 = self.sbn("ckT4", 3, [128, 4, 2, 128], BF16)
            kpT4 = self.sbn("kpT4", 3, [32, 4, 128], BF16)
            pn = self.sbn("pn", 2, [8, 128], BF16)
            knew = self.sbn("knew", 2, [8, 289], BF16)
            olat = self.sbn("olat", 2, [128, 256], BF16)
            rl1 = self.sbn("rl1", 2, [128, 1])
            for b in range(NS if "decode" not in SKIP else 0):
                acc, accb = bk.next()
                qlb = [qlatT[:, cc, b].rearrange("p t h -> p (t h)") for cc in range(2)]
                qrb = qrT[:, b].rearrange("p t h -> p (t h)")
                sn_, snb = bk.next()
                for cc in range(2):
                    self.mm(sn_[0:8, 0:128], ckvnT_s[:, cc, 8 * b:8 * b + 8], qlb[cc], cc == 0, False, SM + [ql_b], [snb], False)
                self.mm(sn_[0:8, 0:128], kpeT_s[0:32, 8 * b:8 * b + 8], qrb, False, True, SM + [ql_b], [snb], True)
                pn_, pnb = pn.next()
                self.act(dict(out=pn_[:], in_=sn_[0:8, 0:128], func=AF.Exp, scale=ATT_SCALE), [snb], [pnb])
                self.dve("tensor_tensor", dict(out=pn_[:], in0=pn_[:], in1=nmask[:].rearrange("p t h -> p (t h)"), op=ALU.mult),
                         [pnb, nm_b], [pnb])
                kn, knb = bk.next()
                self.mm(kn[0:8, 0:289], self.ident[:, 8 * b:8 * b + 8], ckvn_s[:, 0:289], True, True, SM + [self.ident_b], [knb], True)
                kw_, kwb = knew.next()
                self.dve("tensor_copy", dict(out=kw_[:], in_=kn[0:8, 0:289]), [knb], [kwb])
                accl, acclb = bk.next()
                self.mm(acc[:, 0:288], pn_[:], kw_[:, 0:288], True, False, [pnb, kwb], [accb], False)
                self.mm(accl[:, 0:1], pn_[:], kw_[:, 288:289], True, False, [pnb, kwb], [acclb], False)
                nb4 = NPG // 4
                pgs = {}
                ctk = {}
                pts = {}

                def stage_G(k):
                    pgt, pgb = pg.next()
                    col = b * nb4 + k
                    S.dma("pool", pgt[:].rearrange("p a c -> p (a c)"), cache4, reads=[ix_b], writes=[pgb], owner=pgb,
                          indirect=idx[:, col:col + 1])
                    pgs[k] = (pgt, pgb)

                def stage_A(k):
                    pgt, pgb = pgs[k]
                    p, pb = self.pT.next()
                    for jj in range(4):
                        for cc in range(2):
                            self.tr(p[:, (jj * 2 + cc) * 128:(jj * 2 + cc + 1) * 128], pgt[:, jj, cc * 128:(cc + 1) * 128],
                                    [pgb], [pb], inc=(jj == 3 and cc == 1))
                    c4, c4b = ckT4.next()
                    self.S.op("act", "copy", dict(out=c4[:].rearrange("p a b n -> p (a b n)"), in_=p[:, :]), [pb], [c4b])
                    p2, p2b = self.pT.next()
                    for jj in range(4):
                        self.tr(p2[0:32, jj * 128:(jj + 1) * 128], pgt[:, jj, 256:288], [pgb], [p2b], inc=(jj == 3))
                    k4, k4b = kpT4.next()
                    self.dve("tensor_copy", dict(out=k4[:].rearrange("p a n -> p (a n)"), in_=p2[0:32, 0:512]), [p2b], [k4b])
                    ctk[k] = (c4, c4b, k4, k4b)

                def stage_B(k):
                    c4, c4b, k4, k4b = ctk.pop(k)
                    st, stb = bks.next()
                    for jj in range(4):
                        for cc in range(2):
                            self.mm(st[:, jj * 128:(jj + 1) * 128], c4[:, jj, cc, :], qlb[cc], cc == 0, False,
                                    [c4b, ql_b], [stb], False)
                        self.mm(st[:, jj * 128:(jj + 1) * 128], k4[0:32, jj, :], qrb, False, True, [k4b, ql_b], [stb], jj == 3)
                    pt, ptb = PT.next()
                    self.act(dict(out=pt[:], in_=st[:], func=AF.Exp, scale=ATT_SCALE), [stb], [ptb])
                    pts[k] = (pt, ptb)

                def stage_C(k):
                    pt, ptb = pts.pop(k)
                    pgt, pgb = pgs.pop(k)
                    for jj in range(4):
                        last = (k == nb4 - 1 and jj == 3)
                        self.mm(acc[:, 0:288], pt[:, jj * 128:(jj + 1) * 128], pgt[:, jj, :], False, last, [ptb, pgb],
                                [accb], False)
                        self.mm(accl[:, 0:1], pt[:, jj * 128:(jj + 1) * 128], ones_k[:, 0:1], False, last, [ptb, ix_b],
                                [acclb], last)

                PF = min(NPGS - 2, nb4)
                for k in range(PF):
                    stage_G(k)
                stage_A(0)
                if nb4 > 1:
                    stage_A(1)
                stage_B(0)
                for k in range(nb4):
                    if k + PF < nb4:
                        stage_G(k + PF)
                    if k + 2 < nb4:
                        stage_A(k + 2)
                    if k + 1 < nb4:
                        stage_B(k + 1)
                    stage_C(k)
                r1, r1b = rl1.next()
                self.dve("reciprocal", dict(out=r1[:], in_=accl[:, 0:1]), [acclb], [r1b])
                ol, olb = olat.next()
                self.dve("tensor_scalar", dict(out=ol[:], in0=acc[:, 0:256], scalar1=r1[:, 0:1], scalar2=None, op0=ALU.mult),
                         [accb, r1b], [olb])
                p, pb = self.pT.next()
                for cc in range(2):
                    self.tr(p[:, cc * 128:(cc + 1) * 128], ol[:, cc * 128:(cc + 1) * 128], [olb], [pb], inc=(cc == 1))
                self.dve("tensor_copy", dict(out=olatT[:, :, b].rearrange("p c t h -> p c (t h)"),
                                             in_=p[:, 0:256].rearrange("p (c n) -> p c n", n=128)), [pb], [ol_b])
            for k in range(8):
                bank, bb = bk.next()
                first = True
                for par in range(2):
                    h = 2 * k + par
                    for cc in range(2):
                        self.mm(bank[:, 0:128], wuvz[:, cc, h, :], olatT[:, cc].rearrange("p s t h -> p (s t) h")[:, :, h],
                                first, (par == 1 and cc == 1), [wb[2], ol_b], [bb], (par == 1 and cc == 1))
                        first = False
                self.dve("tensor_copy", dict(out=oTs[:, k, :], in_=bank[:, 0:128]), [bb], [oTs_b])
            S.end_phase()

        self.pes = ExitStack()
        with self.pes:
            self.common_alloc(4)
            self.load_gamma(self.gpost, self.gpost_b, self.norm_post[ni])
            wo = self.sb("wo", [128, 8, D], BF16)
            wo_b_ = S.buf("wo")
            self.ld(wo[:].rearrange("p a b -> p (a b)"), self.wo_b[:, :], [self.wcast2_b], [wo_b_], wo_b_)
            bko = self.psn("bko", 4, [128, 512])
            bks = self.psn("bks", 2, [128, 512])
            bk = bko
            o_all = self.sb("o_all", [128, NTP, D], BF16)
            o_b = S.bufs("o_all", NTP // 4)
            masks = self.sb("masks", [128, 4, 512], BF16)
            mk_b = S.buf("masks")
            self.pool("memset", dict(ap=masks[:], constant=1.0), [], [mk_b])
            for j in range(4):
                self.pool("affine_select", dict(out=masks[:, j, :], in_=masks[:, j, :], pattern=[[1, 512]],
                                                compare_op=ALU.is_ge, fill=0.0, base=-128 * j, channel_multiplier=-1),
                          [], [mk_b])
            QTh = self.sbn("QTh", 2, [128, TP], BF16)
            KTh = self.sbn("KTh", 2, [128, TP], BF16)
            Vh = self.sbn("Vh", 2, [128, NTP, 66], BF16)
            PT = self.sbn("PT", 3, [128, 512], BF16)
            rl = self.sbn("rl", 2, [128, 4])
            v_view = self.v_s.rearrange("(t p) (h e) -> p t h e", p=128, e=66)
            NQG = TP // 512

            for h in range(NH if "prompt" not in SKIP else 0):
                qt, qtb = QTh.next()
                kt_, ktb = KTh.next()
                vh, vhb = Vh.next()
                self.ld(qt[0:96, :], self.qT_s[h], [self.qkv_s_b], [qtb], qtb)
                self.ld(kt_[0:96, :], self.kT_s[h], [self.qkv_s_b], [ktb], ktb)
                with self.nc.allow_non_contiguous_dma(reason="per-head V slice (130B rows)"):
                    pass
                self.S.dma("sp", vh[:], v_view[:, :, h, :], reads=[self.qkv_s_b], writes=[vhb], owner=vhb)
                for Qg in range(NQG):
                    oaccs = [bko.next() for _ in range(4)]
                    nkt = 4 * Qg + 4
                    def issue_st(kt):
                        st, stb = bks.next()
                        self.mm(st[:], kt_[0:96, kt * 128:(kt + 1) * 128], qt[0:96, Qg * 512:(Qg + 1) * 512], True, True,
                                [ktb, qtb], [stb], True)
                        pt, ptb = PT.next()
                        self.act(dict(out=pt[:], in_=st[:], func=AF.Exp, scale=ATT_SCALE), [stb], [ptb])
                        j = kt - 4 * Qg
                        if j >= 0:
                            self.dve("tensor_tensor", dict(out=pt[:], in0=pt[:], in1=masks[:, j, :], op=ALU.mult),
                                     [ptb, mk_b], [ptb])
                        return pt, ptb
                    nxt_pt = issue_st(0)
                    for kt in range(nkt):
                        pt, ptb = nxt_pt
                        if kt + 1 < nkt:
                            nxt_pt = issue_st(kt + 1)
                        for qi in range(4):
                            if 4 * Qg + qi >= kt:
                                last = (kt == 4 * Qg + qi)
                                self.mm(oaccs[qi][0][:, 0:65], pt[:, qi * 128:(qi + 1) * 128], vh[:, kt, 0:65],
                                        kt == 0, last, [ptb, vhb], [oaccs[qi][1]], last)
                    r, rb = rl.next()
                    for qi in range(4):
                        oacc, oab = oaccs[qi]
                        self.dve("reciprocal", dict(out=r[:, qi:qi + 1], in_=oacc[:, 64:65]), [oab], [rb])
                        self.dve("tensor_scalar", dict(out=o_all[:, Qg * 4 + qi, h * 64:(h + 1) * 64], in0=oacc[:, 0:64],
                                                       scalar1=r[:, qi:qi + 1], scalar2=None, op0=ALU.mult),
                                 [oab, rb], [o_b[Qg]])
            oT = self.sbn("oT", 2, [128, 8, 128], BF16)
            for ti in range(NT if "m3" not in SKIP else 0):
                xt, xb = self.xt.next()
                self.ld(xt[:], self.xs[ti * 128:(ti + 1) * 128, :], [self.xs_b[ti]], [xb], xb)
                if ti < NTP:
                    ot, otb = oT.next()
                    p, pb = self.pT.next()
                    for c in range(8):
                        self.tr(p[:, c * 128:(c + 1) * 128], o_all[:, ti, c * 128:(c + 1) * 128], [o_b[ti // 4]], [pb], inc=(c == 7))
                    self.S.op("act", "copy", dict(out=ot[:], in_=p[:, :].rearrange("p (c n) -> p c n", n=128)), [pb], [otb])
                else:
                    ot, otb = oTs, oTs_b
                ys = []
                for hh in range(2):
                    y, ybuf = bk.next()
                    for kc in range(8):
                        self.mm(y[:], ot[:, kc, :], wo[:, kc, hh * 512:(hh + 1) * 512], kc == 0, kc == 7, [otb, wo_b_], [ybuf],
                                kc == 7)
                    ys.append((y[:], ybuf))
                self.post_residual(ys, xt, xb, self.xs, [self.xs_b[ti]], ti, 1.0)
            S.end_phase()


def _kc(w, nk):
    n = w.shape[1]
    return np.ascontiguousarray(w.reshape(nk, 128, n).transpose(1, 0, 2)).reshape(128, nk * n)


def _qp(a):
    rest = a.shape[2:]
    a = a.reshape((32, 2, 64) + rest)
    perm = (1, 2, 0) + tuple(range(3, 3 + len(rest)))
    return np.ascontiguousarray(a.transpose(perm)).reshape((128, 32) + rest)


def prep_shared(inp):
    f32 = np.float32
    A = lambda k: np.asarray(inp[k], f32)
    g = A("ffn_w_gate").reshape(4, 8, 128, NF, 128)
    u = A("ffn_w_up").reshape(4, 8, 128, NF, 128)
    d = A("ffn_w_down").reshape(4, NF, 128, D)
    sh = {
        "wg_h": np.ascontiguousarray(g.transpose(0, 3, 2, 1, 4)).reshape(4, NF * 128, 8 * 128),
        "wu_h": np.ascontiguousarray(u.transpose(0, 3, 2, 1, 4)).reshape(4, NF * 128, 8 * 128),
        "wd_h": np.ascontiguousarray(d.transpose(0, 2, 1, 3)).reshape(4, 128, NF * D),
        "norm_pre": np.ascontiguousarray(A("norm_pre").reshape(6, D)),
        "norm_post": np.ascontiguousarray(A("norm_post").reshape(6, D)),
        "s5_are": _qp(A("ssm_a_re")[0]),
        "s5_aim": _qp(A("ssm_a_im")[0]),
        "s5_ldt": _qp(np.broadcast_to(A("ssm_log_dt")[0][:, None], (64, 64))),
        "s5_bre": _qp(A("ssm_b_re")[0]).reshape(128, 512),
        "s5_bim": _qp(A("ssm_b_im")[0]).reshape(128, 512),
        "s5_cre": _qp(A("ssm_c_re")[0].transpose(0, 2, 1)).reshape(128, 512),
        "s5_cim": _qp(A("ssm_c_im")[0].transpose(0, 2, 1)).reshape(128, 512),
        "s5_d": np.ascontiguousarray(A("ssm_d")[0].reshape(8, 128).T),
        "wglu_h": _kc(A("ssm_w_glu")[0], 8),
        "win_h": _kc(A("mla_w_in")[0], 8),
        "wuq_h": _kc(np.concatenate([A("mla_w_uq")[0].reshape(QL, NH, DQK)[:, :, :DN].reshape(QL, NH * DN),
                                     A("mla_w_uq")[0].reshape(QL, NH, DQK)[:, :, DN:].reshape(QL, NH * RP)], axis=1), 6),
        "wukv_h": _kc(A("mla_w_ukv")[0], 2),
        "wukT_h": np.ascontiguousarray(A("mla_w_ukv")[0].reshape(256, 16, 128)[:, :, :64].transpose(2, 1, 0)).reshape(64, 4096),
        "wo_h": _kc(A("mla_w_o")[0], 8),
        "qnorm": np.ascontiguousarray(A("mla_q_norm").reshape(1, QL)),
        "kvnorm": np.ascontiguousarray(A("mla_kv_norm").reshape(1, KVL)),
        "cache_cat": np.concatenate([A("cache_kv_latent")[0].reshape(-1, KVL), A("cache_k_rope")[0].reshape(-1, RP)], axis=1),
    }
    return sh


def prep_core(inp, c, cfg):
    f32 = np.float32
    ns = cfg.NS
    xp = np.asarray(inp["x_prompt"], f32)[c].reshape(cfg.TP, D)
    xsm = np.asarray(inp["x_sample"], f32)[c * ns:(c + 1) * ns].reshape(ns * cfg.TS, D)
    hre = np.asarray(inp["state_ssm_re"], f32)[0, c * ns:(c + 1) * ns]
    him = np.asarray(inp["state_ssm_im"], f32)[0, c * ns:(c + 1) * ns]
    h = np.stack([hre, him], axis=0)
    h = h.transpose(2, 3, 0, 1)
    h0 = _qp(np.ascontiguousarray(h)).reshape(128, 32 * 2 * ns)
    return {
        "x_in": np.ascontiguousarray(np.concatenate([xp, xsm], axis=0)),
        "s5_h0": h0,
        "ptab": np.ascontiguousarray(np.asarray(inp["page_table"], np.int32)[c * ns:(c + 1) * ns]),
    }


def _unqp(a):
    rest = a.shape[2:]
    a = a.reshape((2, 64, 32) + rest)
    perm = (2, 0, 1) + tuple(range(3, 3 + len(rest)))
    return np.ascontiguousarray(a.transpose(perm)).reshape((64, 64) + rest)


def assemble(results, cfg, n_cores):
    TP, ns, ts = cfg.TP, cfg.NS, cfg.TS
    f32 = np.float32
    yp = np.zeros((n_cores, TP, D), f32)
    ys = np.zeros((n_cores * ns, ts, D), f32)
    srp = np.zeros((1, n_cores, 64, 64), f32)
    sip = np.zeros((1, n_cores, 64, 64), f32)
    srs = np.zeros((1, n_cores * ns, 64, 64), f32)
    sis = np.zeros((1, n_cores * ns, 64, 64), f32)
    lp = np.zeros((1, n_cores, TP, KVL), f32)
    kp = np.zeros((1, n_cores, TP, RP), f32)
    ls = np.zeros((1, n_cores * ns, ts, KVL), f32)
    ks = np.zeros((1, n_cores * ns, ts, RP), f32)
    for c, r in enumerate(results):
        y = r["y_out"]
        yp[c] = y[:TP]
        ys[c * ns:(c + 1) * ns] = y[TP:].reshape(ns, ts, D)
        sp = _unqp(r["ssm_p"].reshape(128, 32, 2))
        srp[0, c] = sp[:, :, 0]
        sip[0, c] = sp[:, :, 1]
        ss = _unqp(r["ssm_s"].reshape(128, 32, 2, ns))
        srs[0, c * ns:(c + 1) * ns] = ss[:, :, 0, :].transpose(2, 0, 1)
        sis[0, c * ns:(c + 1) * ns] = ss[:, :, 1, :].transpose(2, 0, 1)
        lat = r["lat_out"]
        kpe = r["kpe_out"]
        lp[0, c] = lat[:TP]
        kp[0, c] = kpe[:TP]
        ls[0, c * ns:(c + 1) * ns] = lat[TP:].reshape(ns, ts, KVL)
        ks[0, c * ns:(c + 1) * ns] = kpe[TP:].reshape(ns, ts, RP)
    return (yp, ys, srp, sip, srs, sis, lp, kp, ls, ks)


def run(inputs, n_cores, cfg):
    nc = build(cfg)
    sh = prep_shared(inputs)
    in_maps = []
    for c in range(n_cores):
        m = dict(sh)
        m.update(prep_core(inputs, c, cfg))
        in_maps.append(m)
    res = run_bass_kernel_spmd(nc, in_maps, core_ids=list(range(n_cores)))
    return assemble(res.results, cfg, n_cores)


def kernel(**inputs):
    cfg = Cfg(TP=4096, NS=16, TS=8, NPG=128, NPHYS=int(np.asarray(inputs["cache_kv_latent"]).shape[1]))
    return run(inputs, 8, cfg)
```

```python
import math
import os
import numpy as np
import concourse.bass as bass
import concourse.mybir as mybir
from concourse.bass_utils import run_bass_kernel_spmd
from contextlib import ExitStack

F32 = mybir.dt.float32
BF16 = mybir.dt.bfloat16
I32 = mybir.dt.int32
AF = mybir.ActivationFunctionType
ALU = mybir.AluOpType
AX = mybir.AxisListType

D = 1024
DFF = 2816
NF = DFF // 128
EPS = 1e-6
QL = 768
KVL = 256
RP = 32
NH = 16
DN = 64
DV = 64
DQK = DN + RP
ATT_SCALE = DQK ** -0.5
ROPE_THETA = 10000.0
PAGE = 128
TWO_PI = 2.0 * math.pi
C1_2PI = 6.28125
C2_2PI = TWO_PI - 6.28125


class DSem:
    __slots__ = ("sem", "cnt")

    def __init__(self, sem):
        self.sem = sem
        self.cnt = 0


class Buf:
    __slots__ = ("name", "w", "r", "ds", "excl")

    def __init__(self, name, excl=False):
        self.name = name
        self.w = {}
        self.r = {}
        self.ds = None
        self.excl = excl


class Sched:
    ENGS = ("pe", "act", "dve", "pool", "sp")
    HANDLES = {"pe": "tensor", "act": "scalar", "dve": "vector", "pool": "gpsimd", "sp": "sync"}

    def __init__(self, nc, es):
        self.nc = nc
        self.es = es
        self.q = {e: [] for e in self.ENGS}
        self.cnt = {e: 0 for e in self.ENGS}
        self.pending = {e: False for e in self.ENGS}
        self.sem = {e: es.enter_context(nc.semaphore("sem_" + e)) for e in self.ENGS}
        self.waited = {}
        self.ds_free = {}
        self.ds_used = []
        self.ds_all = []
        self.ninst = 0

    def buf(self, name):
        return Buf(name)

    def bufs(self, name, n):
        return [Buf("%s%d" % (name, i)) for i in range(n)]

    def _dsem(self, b, kind):
        if b.ds is None:
            b.ds = {}
        if kind not in b.ds:
            free = self.ds_free.setdefault(kind, [])
            if free:
                d = free.pop()
            else:
                d = DSem(self.es.enter_context(self.nc.semaphore("ds%s_%d" % (kind, len(self.ds_all)))))
                self.ds_all.append(d)
            b.ds[kind] = d
            self.ds_used.append((kind, d))
        return b.ds[kind]

    def _filter(self, eng, deps):
        out = []
        for s, v in deps.items():
            if eng == "pe" and s is self.sem["pe"]:
                continue
            key = (eng, id(s))
            if self.waited.get(key, 0) >= v:
                continue
            self.waited[key] = v
            out.append((s, v))
        return out

    def _deps(self, eng, reads, writes, shared=(), skip_sem=None):
        deps = {}

        def add(s, v):
            if deps.get(s, 0) < v:
                deps[s] = v

        for b in reads:
            for s, v in b.w.items():
                add(s, v)
            if b.excl:
                for s, v in b.r.items():
                    add(s, v)
        for b in writes:
            for s, v in b.w.items():
                add(s, v)
            for s, v in b.r.items():
                add(s, v)
        for b in shared:
            for s, v in b.r.items():
                add(s, v)
        if skip_sem is not None:
            deps.pop(skip_sem, None)
        return self._filter(eng, deps)

    def _mark(self, ev, reads, writes, shared=()):
        s, v = ev
        for b in reads:
            if b.r.get(s, 0) < v:
                b.r[s] = v
        for b in writes:
            b.w = {s: v}
            b.r = {}
        for b in shared:
            if b.w.get(s, 0) < v:
                b.w[s] = v

    def op(self, eng, mname, kw, reads=(), writes=(), inc=True):
        fn = (lambda e, mname=mname, kw=kw: getattr(e, mname)(**kw))
        waits = self._deps(eng, reads, writes)
        if inc:
            self.cnt[eng] += 1
            ev = (self.sem[eng], self.cnt[eng])
            self.pending[eng] = False
            incr = (self.sem[eng], 1)
        else:
            ev = (self.sem[eng], self.cnt[eng] + 1)
            self.pending[eng] = True
            incr = None
        self._mark(ev, reads, writes)
        self.q[eng].append((waits, fn, incr))
        self.ninst += 1

    def dma(self, q, out, in_, reads=(), writes=(), owner=None, indirect=None, shared=(), **kw):
        ds = self._dsem(owner, "sw" if q == "pool" else "hw")
        skip = ds.sem if (owner in writes) else None
        waits = self._deps(q, reads, writes, shared, skip_sem=skip)
        ds.cnt += 1
        ev = (ds.sem, 16 * ds.cnt)
        self._mark(ev, reads, writes, shared)
        if indirect is not None:
            fn = (lambda e: e.indirect_dma_start(out=out, out_offset=None, in_=in_,
                                                 in_offset=bass.IndirectOffsetOnAxis(ap=indirect, axis=0)))
        else:
            fn = (lambda e: e.dma_start(out=out, in_=in_, **kw))
        self.q[q].append((waits, fn, (ds.sem, 16)))
        self.ninst += 1

    def barrier(self):
        for en in self.ENGS:
            if self.pending[en]:
                self.cnt[en] += 1
                self.pending[en] = False
                self.q[en].append(([], (lambda e: e.nop()), (self.sem[en], 1)))
        deps = {}
        for en in self.ENGS:
            if self.cnt[en] > 0:
                deps[self.sem[en]] = self.cnt[en]
        for ds in self.ds_all:
            if ds.cnt > 0:
                deps[ds.sem] = 16 * ds.cnt
        for en in self.ENGS:
            self.q[en].append((self._filter(en, dict(deps)), None, None))

    def end_phase(self):
        self.barrier()
        self.emit()
        for kind, d in self.ds_used:
            self.ds_free.setdefault(kind, []).append(d)
        self.ds_used = []

    def emit(self):
        nc = self.nc
        with nc.Block() as block:
            for en in self.ENGS:
                items = self.q[en]

                def body(e, items=items):
                    for waits, fn, incr in items:
                        for s, v in waits:
                            e.wait_ge(s, v)
                        if fn is not None:
                            ins = fn(e)
                            if incr is not None:
                                ins.then_inc(incr[0], incr[1])

                getattr(block, self.HANDLES[en])(body)
        self.q = {e: [] for e in self.ENGS}


class RR:
    def __init__(self, tensors, bufs):
        self.t = tensors
        self.b = bufs
        self.i = 0

    def next(self):
        k = self.i % len(self.t)
        self.i += 1
        return self.t[k], self.b[k]


class Cfg:
    def __init__(self, TP=4096, NS=16, TS=8, NPG=128, NPHYS=20480, stages=None):
        self.TP = TP
        self.NS = NS
        self.TS = TS
        self.NPG = NPG
        self.NPHYS = NPHYS
        self.NTOK = TP + NS * TS
        assert NS * TS == 128 and TP % 512 == 0 and TS == 8
        self.NT = self.NTOK // 128
        self.NB = TP // 8
        self.NBT = self.NB + NS
        self.PAST = NPG * PAGE
        self.stages = stages


def build(cfg):
    nc = bass.Bass("TRN2", target_bir_lowering=False)
    es = ExitStack()
    with es:
        P = Prog(nc, es, cfg)
        P.run()
    return nc


class Prog:
    def __init__(self, nc, es, cfg):
        self.nc = nc
        self.es = es
        self.cfg = cfg
        self.S = Sched(nc, es)
        self.pes = None
        groups = []
        t = 0
        while t < cfg.NT:
            n = 4 if (t * 128) < cfg.TP else 1
            groups.append((t, n))
            t += n
        self.groups = groups

    def dram_in(self, name, shape, dt=F32):
        return self.nc.dram_tensor(name, list(shape), dt, kind="ExternalInput").ap()

    def dram_out(self, name, shape, dt=F32):
        return self.nc.dram_tensor(name, list(shape), dt, kind="ExternalOutput").ap()

    def dram_tmp(self, name, shape, dt=F32):
        return self.nc.dram_tensor(name, list(shape), dt, kind="Internal").ap()

    def gsb(self, name, shape, dt=F32):
        return self.es.enter_context(self.nc.sbuf_tensor(name, list(shape), dt))

    def sb(self, name, shape, dt=F32):
        self._n = getattr(self, "_n", 0) + 1
        return self.pes.enter_context(self.nc.sbuf_tensor("%s_%d" % (name, self._n), list(shape), dt))

    def ps(self, name, shape, dt=F32):
        self._n = getattr(self, "_n", 0) + 1
        return self.pes.enter_context(self.nc.psum_tensor("%s_%d" % (name, self._n), list(shape), dt))

    def sbn(self, name, n, shape, dt=F32):
        return RR([self.sb("%s%d" % (name, i), shape, dt) for i in range(n)], self.S.bufs(name, n))

    def psn(self, name, n, shape, dt=F32):
        return RR([self.ps("%s%d" % (name, i), shape, dt) for i in range(n)],
                  [Buf("%s%d" % (name, i), excl=True) for i in range(n)])

    def act(self, kw, reads=(), writes=()):
        self.S.op("act", "activation", kw, reads, writes)

    def dve(self, m, kw, reads=(), writes=()):
        self.S.op("dve", m, kw, reads, writes)

    def pool(self, m, kw, reads=(), writes=()):
        self.S.op("pool", m, kw, reads, writes)

    def mm(self, out, lhsT, rhs, start, stop, reads, writes, inc):
        self.S.op("pe", "matmul", dict(out=out, lhsT=lhsT, rhs=rhs, start=start, stop=stop), reads, writes, inc=inc)

    def tr(self, out, in_, reads, writes, inc):
        n = in_.shape[0]
        self.S.op("pe", "transpose", dict(out=out, in_=in_, identity=self.ident[0:n, 0:n]),
                  list(reads) + [self.ident_b], writes, inc=inc)

    def ld(self, out, in_, reads, writes, owner, q="sp", shared=()):
        self.S.dma(q, out, in_, reads=reads, writes=writes, owner=owner, shared=shared)

    def run(self):
        cfg = self.cfg
        S = self.S
        NT, NTOK, TP = cfg.NT, cfg.NTOK, cfg.TP
        self.x_in = self.dram_in("x_in", [NTOK, D])
        self.y_out = self.dram_out("y_out", [NTOK, D])
        self.xs = self.dram_tmp("xs", [NTOK, D])
        self.xs_b = S.bufs("xs", NT)
        self.xin_b = S.buf("x_in")
        self.y_b = S.bufs("y", NT)
        self.norm_pre = self.dram_in("norm_pre", [6, D])
        self.norm_post = self.dram_in("norm_post", [6, D])
        self.wg_h = self.dram_in("wg_h", [4, NF * 128, 8 * 128])
        self.wu_h = self.dram_in("wu_h", [4, NF * 128, 8 * 128])
        self.wd_h = self.dram_in("wd_h", [4, 128, NF * D])
        self.wg_b = self.dram_tmp("wg_b", [4, NF * 128, 8 * 128], BF16)
        self.wu_b = self.dram_tmp("wu_b", [4, NF * 128, 8 * 128], BF16)
        self.wd_b = self.dram_tmp("wd_b", [4, 128, NF * D], BF16)
        self.wcast_b = S.bufs("wcast", 4)
        self.s5_are = self.dram_in("s5_are", [128, 32])
        self.s5_aim = self.dram_in("s5_aim", [128, 32])
        self.s5_ldt = self.dram_in("s5_ldt", [128, 32])
        self.s5_bre = self.dram_in("s5_bre", [128, 32 * 16])
        self.s5_bim = self.dram_in("s5_bim", [128, 32 * 16])
        self.s5_cre = self.dram_in("s5_cre", [128, 32 * 16])
        self.s5_cim = self.dram_in("s5_cim", [128, 32 * 16])
        self.s5_d = self.dram_in("s5_d", [128, 8])
        self.s5_h0 = self.dram_in("s5_h0", [128, 32 * 2 * 16])
        self.wglu_h = self.dram_in("wglu_h", [128, 8 * 2048])
        self.wglu_b = self.dram_tmp("wglu_b", [128, 8 * 2048], BF16)
        self.ssm_p = self.dram_out("ssm_p", [128, 32 * 2])
        self.ssm_s = self.dram_out("ssm_s", [128, 32 * 2 * 16])
        self.ssm_b = S.buf("ssm_out")
        self.uT_s = self.dram_tmp("uT_s", [8, 128, NTOK], BF16)
        self.us_b = S.bufs("uT_s", 8)
        self.win_h = self.dram_in("win_h", [128, 8 * 1056])
        self.win_b = self.dram_tmp("win_b", [128, 8 * 1056], BF16)
        self.wuq_h = self.dram_in("wuq_h", [128, 6 * 1536])
        self.wuq_b = self.dram_tmp("wuq_b", [128, 6 * 1536], BF16)
        self.wukv_h = self.dram_in("wukv_h", [128, 2 * 2048])
        self.wukv_b = self.dram_tmp("wukv_b", [128, 2 * 2048], BF16)
        self.wukT_h = self.dram_in("wukT_h", [64, 16 * 256])
        self.wukT_b = self.dram_tmp("wukT_b", [64, 16 * 256], BF16)
        self.wo_h = self.dram_in("wo_h", [128, 8 * 1024])
        self.wo_b = self.dram_tmp("wo_b", [128, 8 * 1024], BF16)
        self.qnorm = self.dram_in("qnorm", [1, QL])
        self.kvnorm = self.dram_in("kvnorm", [1, KVL])
        self.cache_cat = self.dram_in("cache_cat", [cfg.NPHYS * PAGE, KVL + RP])
        self.ptab = self.dram_in("ptab", [cfg.NS, cfg.NPG], I32)
        self.lat_out = self.dram_out("lat_out", [NTOK, KVL])
        self.kpe_out = self.dram_out("kpe_out", [NTOK, RP])
        self.mla_out_b = S.buf("mla_out")
        self.qT_s = self.dram_tmp("qT_s", [NH, DQK, TP], BF16)
        self.kT_s = self.dram_tmp("kT_s", [NH, DQK, TP], BF16)
        self.v_s = self.dram_tmp("v_s", [TP, NH * 66], BF16)
        self.qkv_s_b = S.buf("qkv_s")
        self.wcast2_b = S.buf("wcast2")

        self.ident = self.gsb("ident", [128, 128], BF16)
        self.ident_f = self.gsb("ident_f", [128, 128], F32)
        self.ident_b = S.buf("ident")
        self.eps_t = self.gsb("eps_t", [128, 1])
        self.eps_b = S.buf("eps")
        self.mask32 = self.gsb("mask32", [128, 128])
        self.mask32_b = S.buf("mask32")

        self.pes = ExitStack()
        with self.pes:
            self.pool("memset", dict(ap=self.ident_f[:], constant=0.0), writes=[self.ident_b])
            self.pool("affine_select", dict(out=self.ident_f[:], in_=self.ident_f[:], pattern=[[-1, 128]],
                                            compare_op=ALU.not_equal, fill=1.0, base=0, channel_multiplier=1),
                      writes=[self.ident_b])
            self.dve("tensor_copy", dict(out=self.ident[:], in_=self.ident_f[:]), writes=[self.ident_b])
            self.dve("memset", dict(ap=self.eps_t[:], constant=EPS), writes=[self.eps_b])
            self.pool("memset", dict(ap=self.mask32[:], constant=0.0), writes=[self.mask32_b])
            for k in range(4):
                self.pool("memset", dict(ap=self.mask32[32 * k:32 * k + 32, 32 * k:32 * k + 32], constant=1.0),
                          writes=[self.mask32_b])
            for i in range(4):
                for (o, s_) in ((self.wg_b, self.wg_h), (self.wu_b, self.wu_h), (self.wd_b, self.wd_h)):
                    S.dma("pool", o[i], s_[i], writes=[self.wcast_b[i]], owner=self.wcast_b[i])
            for (o, s_) in ((self.wglu_b, self.wglu_h), (self.win_b, self.win_h), (self.wuq_b, self.wuq_h),
                            (self.wukv_b, self.wukv_h), (self.wukT_b, self.wukT_h), (self.wo_b, self.wo_h)):
                S.dma("pool", o[:, :], s_[:, :], writes=[self.wcast2_b], owner=self.wcast2_b)
            S.end_phase()

        st = cfg.stages
        xsb = lambda ti: [self.xs_b[ti]]
        yb = lambda ti: [self.y_b[ti]]
        self.ffn(0, 0, self.x_in, lambda ti: [self.xin_b], self.xs, xsb)
        last = (st == "ffn0")
        if not last:
            self.s5(1)
            last = (st == "s5")
        if not last:
            self.ffn(1, 2, self.xs, xsb, self.xs, xsb)
            self.ffn(2, 3, self.xs, xsb, self.xs, xsb)
            last = (st == "ffn2")
        if not last:
            self.mla(4)
            last = (st == "mla")
        if not last:
            self.ffn(3, 5, self.xs, xsb, self.y_out, yb)
        else:
            self.copy_out()

    def copy_out(self):
        S = self.S
        self.pes = ExitStack()
        with self.pes:
            xt = self.sbn("xt", 4, [128, D])
            for ti in range(self.cfg.NT):
                t, b = xt.next()
                self.ld(t[:], self.xs[ti * 128:(ti + 1) * 128, :], [self.xs_b[ti]], [b], b)
                self.ld(self.y_out[ti * 128:(ti + 1) * 128, :], t[:], [b], [self.y_b[ti]], b, q="pool")
            S.end_phase()

    def common_alloc(self, nxt=8):
        self.xt = self.sbn("xt", nxt, [128, D])
        self.gpre = self.sb("gpre", [128, D])
        self.gpost = self.sb("gpost", [128, D])
        self.gpre_b = self.S.buf("gpre")
        self.gpost_b = self.S.buf("gpost")
        self.junk = self.sb("junk", [128, D], BF16)
        self.junk_b = self.S.buf("junk")
        self.small = self.sbn("small", 8, [128, 8])
        self.xn = self.sbn("xn", 2, [128, D], BF16)
        self.yt = self.sbn("yt", 2, [128, D])
        self.pT = self.psn("pT", 2, [128, 1024], BF16)

    def load_gamma(self, dst, dst_b, src_row):
        self.ld(dst[:], src_row.partition_broadcast(128), [], [dst_b], dst_b)

    def rstd_of(self, parts, ncols):
        sm, smb = self.small.next()
        off = 0
        for i, (ap, bufs) in enumerate(parts):
            w = ap.shape[-1]
            self.act(dict(out=self.junk[:, off:off + w], in_=ap, func=AF.Square, accum_out=sm[:, i:i + 1]),
                     reads=bufs, writes=[self.junk_b, smb])
            off += w
        col = len(parts)
        if len(parts) > 1:
            assert len(parts) == 2
            self.dve("tensor_tensor", dict(out=sm[:, 2:3], in0=sm[:, 0:1], in1=sm[:, 1:2], op=ALU.add), [smb], [smb])
            src = sm[:, 2:3]
            col = 3
        else:
            src = sm[:, 0:1]
        self.act(dict(out=sm[:, col:col + 1], in_=src, func=AF.Sqrt, scale=1.0 / ncols, bias=self.eps_t[:, 0:1]),
                 reads=[smb, self.eps_b], writes=[smb])
        self.dve("reciprocal", dict(out=sm[:, col + 1:col + 2], in_=sm[:, col:col + 1]), [smb], [smb])
        return sm[:, col + 1:col + 2], smb

    def transpose_into(self, src, src_b, nch, dstT, dstT_b, col0, eng="act"):
        p, pb = self.pT.next()
        for c in range(nch):
            self.tr(p[:, c * 128:(c + 1) * 128], src[:, c * 128:(c + 1) * 128], [src_b], [pb], inc=(c == nch - 1))
        kw = dict(out=dstT[:, 0:nch, col0:col0 + 128], in_=p[:, 0:nch * 128].rearrange("p (c n) -> p c n", n=128))
        if eng == "act":
            self.S.op("act", "copy", kw, [pb], [dstT_b])
        else:
            self.S.op("dve", "tensor_copy", kw, [pb], [dstT_b])

    def front(self, ti, src, src_bufs, xnT_t, xnT_tb, col0):
        xt, xb = self.xt.next()
        self.ld(xt[:], src[ti * 128:(ti + 1) * 128, :], src_bufs, [xb], xb)
        rstd, rb = self.rstd_of([(xt[:], [xb])], D)
        xn, xnb = self.xn.next()
        self.dve("scalar_tensor_tensor", dict(out=xn[:], in0=xt[:], scalar=rstd, in1=self.gpre[:],
                                              op0=ALU.mult, op1=ALU.mult), [xb, rb, self.gpre_b], [xnb])
        self.transpose_into(xn, xnb, 8, xnT_t, xnT_tb, col0)
        return xt, xb

    def post_residual(self, halves, xt, xb, dst, dst_bufs, ti, coef):
        rstd, rb = self.rstd_of([(h[0], [h[1]]) for h in halves], D)
        yt, yb = self.yt.next()
        for h in range(2):
            self.dve("scalar_tensor_tensor", dict(out=yt[:, h * 512:(h + 1) * 512], in0=halves[h][0], scalar=rstd,
                                                  in1=self.gpost[:, h * 512:(h + 1) * 512], op0=ALU.mult, op1=ALU.mult),
                     [halves[h][1], rb, self.gpost_b], [yb])
        self.dve("scalar_tensor_tensor", dict(out=yt[:], in0=yt[:], scalar=float(coef), in1=xt[:],
                                              op0=ALU.mult, op1=ALU.add), [yb, xb], [yb])
        self.ld(dst[ti * 128:(ti + 1) * 128, :], yt[:], [yb], dst_bufs, yb, q="pool")

    def ffn(self, fi, ni, src, src_bufs_of, dst, dst_bufs_of):
        S = self.S
        self.pes = ExitStack()
        with self.pes:
            self.common_alloc(8)
            xnT = self.sbn("xnT", 2, [128, 8, 512], BF16)
            wg = self.sbn("wg", 6, [128, 8 * 128], BF16)
            wu = self.sbn("wu", 6, [128, 8 * 128], BF16)
            wd = self.sbn("wd", 2, [128, 11 * D], BF16)
            sg = self.sbn("sg", 4, [128, 512])
            hT = self.sb("hT", [128, NF, 512], BF16)
            hT_b = S.buf("hT")
            pA = self.psn("pAll", 6, [128, 512])
            pB = pA
            pY = pA
            self.load_gamma(self.gpre, self.gpre_b, self.norm_pre[ni])
            self.load_gamma(self.gpost, self.gpost_b, self.norm_post[ni])
            wc = [self.wcast_b[fi]]
            def do_front(gi_):
                t0_, n_ = self.groups[gi_]
                xT_, xTb_ = xnT.next()
                return (xT_, xTb_, [self.front(t0_ + i, src, src_bufs_of(t0_ + i), xT_, xTb_, i * 128) for i in range(n_)])
            nxt = do_front(0)
            for gi, (t0, n) in enumerate(self.groups):
                G = n * 128
                xT, xTb, xts = nxt
                for f in range(NF):
                    wgt, wgb = wg.next()
                    wut, wub = wu.next()
                    self.ld(wgt[:], self.wg_b[fi, f * 128:(f + 1) * 128, :], wc, [wgb], wgb)
                    self.ld(wut[:], self.wu_b[fi, f * 128:(f + 1) * 128, :], wc, [wub], wub)
                    a, ab = pA.next()
                    b, bb = pB.next()
                    for kc in range(8):
                        self.mm(a[:, 0:G], wgt[:, kc * 128:(kc + 1) * 128], xT[:, kc, 0:G], kc == 0, kc == 7,
                                [wgb, xTb], [ab], kc == 7)
                    for kc in range(8):
                        self.mm(b[:, 0:G], wut[:, kc * 128:(kc + 1) * 128], xT[:, kc, 0:G], kc == 0, kc == 7,
                                [wub, xTb], [bb], kc == 7)
                    s, sb_ = sg.next()
                    self.act(dict(out=s[:, 0:G], in_=a[:, 0:G], func=AF.Silu), [ab], [sb_])
                    self.dve("tensor_tensor", dict(out=hT[:, f, 0:G], in0=s[:, 0:G], in1=b[:, 0:G], op=ALU.mult),
                             [sb_, bb], [hT_b])
                if gi + 1 < len(self.groups):
                    nxt = do_front(gi + 1)
                wds = []
                for hf in range(2):
                    wdt, wdb = wd.next()
                    self.ld(wdt[:], self.wd_b[fi, :, hf * 11 * D:(hf + 1) * 11 * D], wc, [wdb], wdb)
                    wds.append((wdt, wdb))
                for i in range(n):
                    ys = []
                    for h in range(2):
                        y, ybuf = pY.next()
                        for f in range(NF):
                            hf, ff = divmod(f, 11)
                            self.mm(y[:], hT[:, f, i * 128:(i + 1) * 128],
                                    wds[hf][0][:, ff * D + h * 512: ff * D + (h + 1) * 512],
                                    f == 0, f == NF - 1, [hT_b, wds[hf][1]], [ybuf], f == NF - 1)
                        ys.append((y[:], ybuf))
                    self.post_residual(ys, xts[i][0], xts[i][1], dst, dst_bufs_of(t0 + i), t0 + i, 0.5)
            S.end_phase()

    def bc(self, ap, axis, shape):
        return ap.unsqueeze(axis).to_broadcast(list(shape))

    def sincos(self, th, thb, n, want):
        out = {}
        for name in want:
            shift = 0.0 if name == "sin" else math.pi / 2
            t = self.sb("sc_t", [128, n])
            ti = self.sb("sc_i", [128, n], I32)
            b = self.S.buf("sc")
            self.dve("tensor_scalar", dict(out=t[:], in0=th, scalar1=shift, scalar2=1.0 / TWO_PI, op0=ALU.add,
                                           op1=ALU.mult), [thb], [b])
            self.dve("tensor_copy", dict(out=ti[:], in_=t[:]), [b], [b])
            self.dve("tensor_copy", dict(out=t[:], in_=ti[:]), [b], [b])
            r = self.sb("sc_r", [128, n])
            self.dve("scalar_tensor_tensor", dict(out=r[:], in0=t[:], scalar=-C1_2PI, in1=th, op0=ALU.mult,
                                                  op1=ALU.add), [b, thb], [b])
            self.dve("scalar_tensor_tensor", dict(out=r[:], in0=t[:], scalar=-C2_2PI, in1=r[:], op0=ALU.mult,
                                                  op1=ALU.add), [b], [b])
            if shift != 0.0:
                self.dve("tensor_scalar", dict(out=r[:], in0=r[:], scalar1=shift, scalar2=None, op0=ALU.add), [b], [b])
            self.dve("tensor_scalar", dict(out=r[:], in0=r[:], scalar1=math.pi, scalar2=-math.pi, op0=ALU.min,
                                           op1=ALU.max), [b], [b])
            o = self.sb("sc_o", [128, n])
            self.act(dict(out=o[:], in_=r[:], func=AF.Sin), [b], [b])
            out[name] = (o, b)
        return out

    def cmul(self, o_re, o_im, a_re, a_im, b_re, b_im, reads, writes, tmp, neg_im=False):
        t1, t2 = tmp
        tt = lambda o, x, y, op: self.dve("tensor_tensor", dict(out=o, in0=x, in1=y, op=op), reads, writes)
        tt(t1, a_re, b_re, ALU.mult)
        tt(t2, a_im, b_im, ALU.mult)
        tt(o_re, t1, t2, ALU.subtract)
        tt(t1, a_re, b_im, ALU.mult)
        tt(t2, a_im, b_re, ALU.mult)
        tt(o_im, t1, t2, ALU.add)
        if neg_im:
            self.dve("tensor_scalar", dict(out=o_im, in0=o_im, scalar1=-1.0, scalar2=None, op0=ALU.mult), reads, writes)

    def s5(self, ni):
        S = self.S
        cfg = self.cfg
        NB, NBT, NTOK, TP, NS = cfg.NB, cfg.NBT, cfg.NTOK, cfg.TP, cfg.NS
        LOGNB = int(round(math.log2(NB)))
        assert 2 ** LOGNB == NB
        self.pes = ExitStack()
        with self.pes:
            self.common_alloc(4)
            self.load_gamma(self.gpre, self.gpre_b, self.norm_pre[ni])
            xnT = self.sbn("xnT", 2, [128, 8, 512], BF16)
            for gi, (t0, n) in enumerate(self.groups):
                G_ = n * 128
                xT, xTb = xnT.next()
                for i in range(n):
                    self.front(t0 + i, self.xs, [self.xs_b[t0 + i]], xT, xTb, i * 128)
                self.ld(self.uT_s[:, :, t0 * 128:t0 * 128 + G_].rearrange("c p t -> p c t"), xT[:, :, 0:G_], [xTb],
                        [], xTb, q="pool", shared=self.us_b)
            S.end_phase()
        self.pes = ExitStack()
        with self.pes:
            self.small = self.sbn("small", 8, [128, 8])
            self.pT = self.psn("pT", 2, [128, 1024], BF16)
            ucs = self.sbn("uc", 2, [128, NTOK], BF16)
            bk = self.psn("bk", 6, [128, 512])
            gb = S.buf("s5gen")
            G = [gb]
            are = self.sb("are", [128, 32]); aim = self.sb("aim", [128, 32]); ldt = self.sb("ldt", [128, 32])
            bre = self.sb("bre", [128, 32, 16]); bim = self.sb("bim", [128, 32, 16])
            cre = self.sb("cre", [128, 32, 16]); cim = self.sb("cim", [128, 32, 16])
            dcol = self.sb("dcol", [128, 8]); h0 = self.sb("h0", [128, 32, 2, NS])
            lb = S.bufs("s5ld", 9)
            for k, (t, src) in enumerate(((are, self.s5_are), (aim, self.s5_aim), (ldt, self.s5_ldt), (dcol, self.s5_d))):
                self.ld(t[:], src[:, :], [], [lb[k]], lb[k])
            for k, (t, src) in enumerate(((bre, self.s5_bre), (bim, self.s5_bim), (cre, self.s5_cre), (cim, self.s5_cim))):
                self.ld(t[:].rearrange("p a b -> p (a b)"), src[:, :], [], [lb[4 + k]], lb[4 + k])
            self.ld(h0[:].rearrange("p a b c -> p (a b c)"), self.s5_h0[:, :], [], [lb[8]], lb[8])
            LB = list(lb)
            dt = self.sb("dt", [128, 32]); trd = self.sb("trd", [128, 32]); th = self.sb("th", [128, 32])
            mag = self.sb("mag", [128, 32]); rho = self.sb("rho", [128, 32])
            self.act(dict(out=dt[:], in_=ldt[:], func=AF.Exp), LB, G)
            self.dve("tensor_tensor", dict(out=trd[:], in0=are[:], in1=dt[:], op=ALU.mult), LB + G, G)
            self.dve("tensor_tensor", dict(out=th[:], in0=aim[:], in1=dt[:], op=ALU.mult), LB + G, G)
            self.act(dict(out=mag[:], in_=trd[:], func=AF.Exp), G, G)
            self.act(dict(out=rho[:], in_=trd[:], func=AF.Exp, scale=8.0), G, G)
            sc = self.sincos(th[:], gb, 32, ("sin", "cos"))
            LRI = self.sb("LRI", [128, 9, 2, 32])
            nLI = self.sb("nLI", [128, 9, 32])
            t1 = self.sb("t1", [128, 32]); t2 = self.sb("t2", [128, 32])
            self.dve("memset", dict(ap=LRI[:, 0, 0, :], constant=1.0), [], G)
            self.dve("memset", dict(ap=LRI[:, 0, 1, :], constant=0.0), [], G)
            self.dve("tensor_tensor", dict(out=LRI[:, 1, 0, :], in0=mag[:], in1=sc["cos"][0][:], op=ALU.mult),
                     G + [sc["cos"][1]], G)
            self.dve("tensor_tensor", dict(out=LRI[:, 1, 1, :], in0=mag[:], in1=sc["sin"][0][:], op=ALU.mult),
                     G + [sc["sin"][1]], G)
            for tau in range(2, 9):
                self.cmul(LRI[:, tau, 0, :], LRI[:, tau, 1, :], LRI[:, tau - 1, 0, :], LRI[:, tau - 1, 1, :],
                          LRI[:, 1, 0, :], LRI[:, 1, 1, :], G, G, (t1[:], t2[:]))
            self.dve("tensor_scalar", dict(out=nLI[:], in0=LRI[:, :, 1, :], scalar1=-1.0, scalar2=None, op0=ALU.mult), G, G)
            gre = self.sb("gre", [128, 32]); gim = self.sb("gim", [128, 32]); nr = self.sb("nr", [128, 32])
            den = self.sb("den", [128, 32])
            tt = lambda o, x, y, op: self.dve("tensor_tensor", dict(out=o, in0=x, in1=y, op=op), LB + G, G)
            self.dve("tensor_scalar", dict(out=nr[:], in0=LRI[:, 1, 0, :], scalar1=-1.0, scalar2=None, op0=ALU.add), G, G)
            tt(t1[:], are[:], are[:], ALU.mult)
            tt(t2[:], aim[:], aim[:], ALU.mult)
            tt(den[:], t1[:], t2[:], ALU.add)
            self.dve("reciprocal", dict(out=den[:], in_=den[:]), G, G)
            tt(t1[:], nr[:], are[:], ALU.mult)
            tt(t2[:], LRI[:, 1, 1, :], aim[:], ALU.mult)
            tt(gre[:], t1[:], t2[:], ALU.add)
            tt(gre[:], gre[:], den[:], ALU.mult)
            tt(t1[:], LRI[:, 1, 1, :], are[:], ALU.mult)
            tt(t2[:], nr[:], aim[:], ALU.mult)
            tt(gim[:], t1[:], t2[:], ALU.subtract)
            tt(gim[:], gim[:], den[:], ALU.mult)
            Pk = self.sb("Pk", [128, LOGNB + 1, 2, 32])
            self.dve("reciprocal", dict(out=t1[:], in_=rho[:]), G, G)
            tt(Pk[:, 0, 0, :], LRI[:, 8, 0, :], t1[:], ALU.mult)
            tt(Pk[:, 0, 1, :], LRI[:, 8, 1, :], t1[:], ALU.mult)
            for k in range(LOGNB):
                self.cmul(Pk[:, k + 1, 0, :], Pk[:, k + 1, 1, :], Pk[:, k, 0, :], Pk[:, k, 1, :],
                          Pk[:, k, 0, :], Pk[:, k, 1, :], G, G, (t1[:], t2[:]))
            Bre = self.sb("Bre", [128, 32, 16]); Bim = self.sb("Bim", [128, 32, 16])
            T1 = self.sb("T1", [128, 32, 16]); T2 = self.sb("T2", [128, 32, 16])
            gre_b = self.bc(gre[:], 2, [128, 32, 16]); gim_b = self.bc(gim[:], 2, [128, 32, 16])
            self.cmul(Bre[:], Bim[:], bre[:], bim[:], gre_b, gim_b, LB + G, G, (T1[:], T2[:]))
            Wre = self.sb("Wre", [128, 8, 4, 16]); Wim = self.sb("Wim", [128, 8, 4, 16])
            Wt1 = self.sb("Wt1", [128, 9, 4, 16]); Wt2 = self.sb("Wt2", [128, 9, 4, 16])
            VVre = self.sb("VVre", [128, 9, 4, 16]); VVim = self.sb("VVim", [128, 9, 4, 16])
            Wexp = self.sb("Wexp", [128, 2, 8, 4, 2, 16], BF16)
            VVexp = self.sb("VVexp", [128, 2, 9, 4, 2, 16], BF16)
            Bexp = self.sb("Bexp", [128, 2, 4, 2, 16], BF16)
            WTz = self.sb("WTz", [128, 4, 2, 8, 128], BF16)
            VVz = self.sb("VVz", [128, 8, 4, 2, 128], BF16)
            BD = self.sb("BD", [128, 8, 128], BF16)
            Ect = self.sb("Ec", [128, 4, NB]); Est = self.sb("Es", [128, 4, NB])
            w_sb = self.sb("w_sb", [128, 4, 2, NBT])
            v_sb = self.sb("v_sb", [128, 4, 2, NB])
            g_sb = self.sb("g_sb", [128, 4, 2, NB])
            zbf = self.sb("zbf", [128, 4, 2, NBT], BF16)
            zs = self.sb("zs", [128, 4, 2, NS])
            zfin = self.sb("zfin", [128, 8, 4, 2])
            E1 = self.sb("E1", [128, 4, NB]); E2 = self.sb("E2", [128, 4, NB])
            ytmp = self.sbn("ytmp", 2, [128, 512])
            cb = S.buf("s5chunk")
            C = [cb]
            for t in (Wexp, VVexp, Bexp, WTz, VVz):
                self.pool("memset", dict(ap=t[:], constant=0.0), [], C)
            self.dve("memset", dict(ap=zbf[:, :, :, 0:1], constant=0.0), [], C)
            for c in range(8):
                ps = slice(4 * c, 4 * c + 4)
                RD = LB + G + C
                uc, ucb = ucs.next()
                self.ld(uc[:], self.uT_s[c], [self.us_b[c]], [ucb], ucb)
                lr8 = self.bc(LRI[:, 0:8, 0, ps], 3, [128, 8, 4, 16]); li8 = self.bc(LRI[:, 0:8, 1, ps], 3, [128, 8, 4, 16])
                Bre8 = self.bc(Bre[:, ps, :], 1, [128, 8, 4, 16]); Bim8 = self.bc(Bim[:, ps, :], 1, [128, 8, 4, 16])
                self.cmul(Wre[:], Wim[:], lr8, li8, Bre8, Bim8, RD, C, (Wt1[:, 0:8], Wt2[:, 0:8]))
                lr9 = self.bc(LRI[:, :, 0, ps], 3, [128, 9, 4, 16]); li9 = self.bc(LRI[:, :, 1, ps], 3, [128, 9, 4, 16])
                cre9 = self.bc(cre[:, ps, :], 1, [128, 9, 4, 16]); cim9 = self.bc(cim[:, ps, :], 1, [128, 9, 4, 16])
                self.cmul(VVre[:], VVim[:], cre9, cim9, lr9, li9, RD, C, (Wt1[:], Wt2[:]), neg_im=True)
                for g2, pr in ((0, slice(0, 64)), (1, slice(64, 128))):
                    for ri, (wsrc, vsrc, bsrc) in enumerate(((Wre, VVre, Bre), (Wim, VVim, Bim))):
                        self.dve("tensor_copy", dict(out=Wexp[pr, ri, :, :, g2, :], in_=wsrc[pr]), RD, C)
                        self.dve("tensor_copy", dict(out=VVexp[pr, ri, :, :, g2, :], in_=vsrc[pr]), RD, C)
                        self.dve("tensor_copy", dict(out=Bexp[pr, ri, :, g2, :], in_=bsrc[pr, ps, :]), RD, C)
                for ri in range(2):
                    p, pb = self.pT.next()
                    for n in range(8):
                        self.tr(p[:, n * 128:(n + 1) * 128], Wexp[:, ri, n].rearrange("p a b c -> p (a b c)"), C, [pb],
                                inc=(n == 7))
                    for p4 in range(4):
                        self.S.op("act", "copy", dict(out=WTz[32 * p4:32 * p4 + 32, p4, ri, :, :],
                                                      in_=p[32 * p4:32 * p4 + 32, :].rearrange("p (n q) -> p n q", q=128)),
                                  [pb], C)
                for p4 in range(4):
                    for ri in range(2):
                        self.dve("tensor_copy", dict(out=VVz[:, :, p4, ri, 32 * p4:32 * p4 + 32],
                                                     in_=VVexp[:, ri, 1:9, p4].rearrange("p t a b -> p t (a b)")), C, C)
                for half in range(2):
                    bank, bb = bk.next()
                    first = True
                    for t4 in range(4):
                        tau = half * 4 + t4
                        for ri in range(2):
                            self.mm(bank[:, t4 * 128:(t4 + 1) * 128], Bexp[:, ri].rearrange("p a b c -> p (a b c)"),
                                    VVexp[:, ri, tau].rearrange("p a b c -> p (a b c)"), ri == 0, ri == 1,
                                    C, [bb], (t4 == 3 and ri == 1))
                    self.dve("tensor_tensor", dict(out=BD[:, half * 4:half * 4 + 4, :],
                                                   in0=bank[:, :].rearrange("p (t n) -> p t n", n=128),
                                                   in1=self.bc(self.mask32[:], 1, [128, 4, 128]), op=ALU.mult),
                             [bb, self.mask32_b], C)
                self.dve("memset", dict(ap=Ect[:, :, 0:1], constant=1.0), [], C)
                self.dve("memset", dict(ap=Est[:, :, 0:1], constant=0.0), [], C)
                for k in range(LOGNB):
                    n = 2 ** k
                    pc = self.bc(Pk[:, k, 0, ps], 2, [128, 4, n]); psn_ = self.bc(Pk[:, k, 1, ps], 2, [128, 4, n])
                    self.cmul(Ect[:, :, n:2 * n], Est[:, :, n:2 * n], Ect[:, :, 0:n], Est[:, :, 0:n], pc, psn_, G + C, C,
                              (E1[:, :, 0:n], E2[:, :, 0:n]))
                for gi, (t0, n) in enumerate(self.groups):
                    c0 = t0 * 128
                    ntok = n * 128
                    nb = ntok // 8
                    gb0 = c0 // 8
                    bank, bb = bk.next()
                    first = True
                    for p4 in range(4):
                        for ri in range(2):
                            for s_ in range(8):
                                self.mm(bank[:, (p4 * 2 + ri) * 64:(p4 * 2 + ri) * 64 + nb], WTz[:, p4, ri, 7 - s_, :],
                                        uc[:, c0 + s_:c0 + ntok:8], s_ == 0, s_ == 7, C + [ucb], [bb],
                                        (p4 == 3 and ri == 1 and s_ == 7))
                    self.S.op("act", "copy", dict(out=w_sb[:, :, :, gb0:gb0 + nb],
                                                  in_=bank[:, :].rearrange("p (a b n) -> p a b n", a=4, b=2)[:, :, :, 0:nb]),
                              [bb], C)
                wre_ = w_sb[:, :, 0, 0:NB]; wim_ = w_sb[:, :, 1, 0:NB]
                tt2 = lambda o, x, y, op: self.dve("tensor_tensor", dict(out=o, in0=x, in1=y, op=op), C, C)
                tt2(E1[:], Ect[:], wre_, ALU.mult); tt2(E2[:], Est[:], wim_, ALU.mult)
                tt2(v_sb[:, :, 0, :], E1[:], E2[:], ALU.add)
                tt2(E1[:], Ect[:], wim_, ALU.mult); tt2(E2[:], Est[:], wre_, ALU.mult)
                tt2(v_sb[:, :, 1, :], E1[:], E2[:], ALU.subtract)
                for p4 in range(4):
                    for ri in range(2):
                        self.dve("tensor_tensor_scan", dict(out=g_sb[:, p4, ri, :],
                                                            data0=rho[:, 4 * c + p4:4 * c + p4 + 1].to_broadcast([128, NB]),
                                                            data1=v_sb[:, p4, ri, :], initial=0.0, op0=ALU.mult, op1=ALU.add),
                                 G + C, C)
                tt2(E1[:], Ect[:], g_sb[:, :, 0, :], ALU.mult); tt2(E2[:], Est[:], g_sb[:, :, 1, :], ALU.mult)
                tt2(zbf[:, :, 0, 1:NB], E1[:, :, 0:NB - 1], E2[:, :, 0:NB - 1], ALU.subtract)
                tt2(zfin[:, c, :, 0], E1[:, :, NB - 1], E2[:, :, NB - 1], ALU.subtract)
                tt2(E1[:], Ect[:], g_sb[:, :, 1, :], ALU.mult); tt2(E2[:], Est[:], g_sb[:, :, 0, :], ALU.mult)
                tt2(zbf[:, :, 1, 1:NB], E1[:, :, 0:NB - 1], E2[:, :, 0:NB - 1], ALU.add)
                tt2(zfin[:, c, :, 1], E1[:, :, NB - 1], E2[:, :, NB - 1], ALU.add)
                self.dve("tensor_copy", dict(out=zbf[:, :, :, NB:NBT], in_=h0[:, ps, :, :]), LB + C, C)
                self.ld(self.ssm_p[:, 8 * c:8 * c + 8], zfin[:, c].rearrange("p a b -> p (a b)"), C, [], cb, q="pool", shared=[self.ssm_b])
                l8r = self.bc(LRI[:, 8, 0, ps], 2, [128, 4, NS]); l8i = self.bc(LRI[:, 8, 1, ps], 2, [128, 4, NS])
                self.cmul(zs[:, :, 0, :], zs[:, :, 1, :], h0[:, ps, 0, :], h0[:, ps, 1, :], l8r, l8i, LB + G + C, C,
                          (E1[:, :, 0:NS], E2[:, :, 0:NS]))
                tt2(zs[:], zs[:], w_sb[:, :, :, NB:NBT], ALU.add)
                self.ld(self.ssm_s[:, 4 * c * 2 * NS:(4 * c + 4) * 2 * NS], zs[:].rearrange("p a b c -> p (a b c)"), C,
                        [], cb, q="pool", shared=[self.ssm_b])
                for gi, (t0, n) in enumerate(self.groups):
                    c0 = t0 * 128
                    ntok = n * 128
                    nb = ntok // 8
                    gb0 = (c0 // 8) if c0 < TP else NB
                    bank, bb = bk.next()
                    first = True
                    for r in range(8):
                        for tau in range(r + 1):
                            self.mm(bank[:, r * 64:r * 64 + nb], BD[:, tau, :], uc[:, c0 + r - tau:c0 + ntok:8],
                                    tau == 0, False, C + [ucb], [bb], False)
                        for p4 in range(4):
                            for ri in range(2):
                                last = (r == 7 and p4 == 3 and ri == 1)
                                self.mm(bank[:, r * 64:r * 64 + nb], VVz[:, r, p4, ri, :], zbf[:, p4, ri, gb0:gb0 + nb],
                                        False, (p4 == 3 and ri == 1), C, [bb], last)
                    yt_, ytb = ytmp.next()
                    self.dve("scalar_tensor_tensor", dict(
                        out=yt_[:, 0:ntok].rearrange("p (b r) -> p b r", r=8),
                        in0=uc[:, c0:c0 + ntok].rearrange("p (b r) -> p b r", r=8), scalar=dcol[:, c:c + 1],
                        in1=bank[:, :].rearrange("p (r b) -> p b r", r=8)[:, 0:nb, :], op0=ALU.mult, op1=ALU.add),
                        [bb, ucb] + LB, [ytb])
                    self.act(dict(out=uc[:, c0:c0 + ntok], in_=yt_[:, 0:ntok], func=AF.Gelu_apprx_tanh), [ytb],
                             [ucb])
                self.ld(self.uT_s[c], uc[:], [ucb], [self.us_b[c]], ucb, q="pool")
            S.end_phase()
        self.pes = ExitStack()
        with self.pes:
            self.common_alloc(4)
            self.load_gamma(self.gpost, self.gpost_b, self.norm_post[ni])
            wglu = self.sb("wglu", [128, 8, 2048], BF16)
            wglu_b = S.buf("wglu")
            self.ld(wglu[:].rearrange("p a b -> p (a b)"), self.wglu_b[:, :], [self.wcast2_b], [wglu_b], wglu_b)
            bk = self.psn("bk", 6, [128, 512])
            hTs = self.sbn("hTt", 3, [128, 8, 128], BF16)
            sgl = self.sbn("sgl", 2, [128, 512])
            mo = self.sbn("mo", 2, [128, D])
            for gi, (t0, n) in enumerate(self.groups):
                for i in range(n):
                    ti = t0 + i
                    xt, xb = self.xt.next()
                    self.ld(xt[:], self.xs[ti * 128:(ti + 1) * 128, :], [self.xs_b[ti]], [xb], xb)
                    hT, hTb = hTs.next()
                    self.ld(hT[:], self.uT_s[:, :, ti * 128:(ti + 1) * 128].rearrange("c p t -> p c t"), self.us_b, [hTb], hTb)
                    m, mb = mo.next()
                    for h in range(2):
                        zv, zvb = bk.next()
                        zg, zgb = bk.next()
                        for (bank, bb, col) in ((zv, zvb, h * 512), (zg, zgb, 1024 + h * 512)):
                            for kc in range(8):
                                self.mm(bank[:], hT[:, kc, :], wglu[:, kc, col:col + 512], kc == 0,
                                        kc == 7, [hTb, wglu_b], [bb], kc == 7)
                        sg, sgb = sgl.next()
                        self.act(dict(out=sg[:], in_=zg[:], func=AF.Sigmoid), [zgb], [sgb])
                        self.dve("tensor_tensor", dict(out=m[:, h * 512:(h + 1) * 512], in0=sg[:], in1=zv[:], op=ALU.mult),
                                 [sgb, zvb], [mb])
                    self.post_residual([(m[:, 0:512], mb), (m[:, 512:1024], mb)], xt, xb, self.xs, [self.xs_b[ti]], ti, 1.0)
            S.end_phase()

    def mla(self, ni):
        S = self.S
        cfg = self.cfg
        NT, NTOK, TP, NS, NPG = cfg.NT, cfg.NTOK, cfg.TP, cfg.NS, cfg.NPG
        NTP = TP // 128
        QTs = self.gsb("QTs", [128, NH, 128], BF16)
        ckvn_s = self.gsb("ckvn_s", [128, 289], BF16)
        ckvnT_s = self.gsb("ckvnT_s", [128, 2, 128], BF16)
        kpeT_s = self.gsb("kpeT_s", [32, 128], BF16)
        smp_b = S.buf("smp")
        SM = [smp_b]
        self.pes = ExitStack()
        with self.pes:
            self.common_alloc(4)
            self.load_gamma(self.gpre, self.gpre_b, self.norm_pre[ni])
            xnT = self.sbn("xnT", 2, [128, 8, 512], BF16)
            win = self.sb("win", [128, 8, 1056], BF16)
            wuq = self.sb("wuq", [128, 6, 1536], BF16)
            wukv = self.sb("wukv", [128, 2, 2048], BF16)
            wb = S.bufs("mlaw", 3)
            for k, (t, src) in enumerate(((win, self.win_b), (wuq, self.wuq_b), (wukv, self.wukv_b))):
                self.ld(t[:].rearrange("p a b -> p (a b)"), src[:, :], [self.wcast2_b], [wb[k]], wb[k])
            qn_bc = self.sb("qn_bc", [128, QL]); kvn_bc = self.sb("kvn_bc", [128, KVL])
            nb_ = S.bufs("nrm", 2)
            self.ld(qn_bc[:], self.qnorm[0].partition_broadcast(128), [], [nb_[0]], nb_[0])
            self.ld(kvn_bc[:], self.kvnorm[0].partition_broadcast(128), [], [nb_[1]], nb_[1])
            tb = S.buf("ropetab")
            TB = [tb]
            posf = self.sb("posf", [128, NT]); pi_ = self.sb("pi_", [128, 1], I32); invf = self.sb("invf", [128, 16])
            ang = self.sb("ang", [128, NT, 16])
            self.pool("iota", dict(out=posf[:, 0:NTP], pattern=[[128, NTP]], base=0, channel_multiplier=1,
                                   allow_small_or_imprecise_dtypes=True), [], TB)
            self.pool("iota", dict(out=pi_[:], pattern=[[0, 1]], base=0, channel_multiplier=1), [], TB)
            if "and" not in os.environ.get("M1_SKIP", ""):
                self.dve("tensor_single_scalar", dict(out=pi_[:], in_=pi_[:], scalar=7, op=ALU.bitwise_and), TB, TB)
            self.dve("tensor_copy", dict(out=posf[:, NTP:NT], in_=pi_[:]), TB, TB)
            self.dve("tensor_scalar", dict(out=posf[:, NTP:NT], in0=posf[:, NTP:NT], scalar1=float(cfg.PAST), scalar2=None,
                                           op0=ALU.add), TB, TB)
            self.pool("iota", dict(out=invf[:], pattern=[[1, 16]], base=0, channel_multiplier=0,
                                   allow_small_or_imprecise_dtypes=True), [], TB)
            self.act(dict(out=invf[:], in_=invf[:], func=AF.Exp, scale=-math.log(ROPE_THETA) / 16.0), TB, TB)
            self.dve("tensor_tensor", dict(out=ang[:], in0=self.bc(posf[:], 2, [128, NT, 16]),
                                           in1=self.bc(invf[:], 1, [128, NT, 16]), op=ALU.mult), TB, TB)
            sc = self.sincos(ang[:].rearrange("p a b -> p (a b)"), tb, NT * 16, ("sin", "cos"))
            cosT = sc["cos"][0][:].rearrange("p (a b) -> p a b", b=16)
            sinT = sc["sin"][0][:].rearrange("p (a b) -> p a b", b=16)
            RT = [sc["cos"][1], sc["sin"][1]]
            bk = self.psn("bk", 6, [128, 512])
            cqn = self.sbn("cqn", 2, [128, QL], BF16)
            ckf = self.sbn("ckf", 2, [128, KVL])
            ckb = self.sbn("ckb", 2, [128, KVL], BF16)
            kpf = self.sbn("kpf", 2, [128, RP])
            rt = self.sbn("rt", 2, [128, 2, 16, 16])
            cqnT = self.sbn("cqnT", 2, [128, 6, 128], BF16)
            ckvT = self.sbn("ckvT", 2, [128, 2, 128], BF16)
            qsb = self.sbn("qsb", 2, [128, NH, DQK], BF16)
            ksb = self.sbn("ksb", 2, [128, NH, DQK], BF16)
            vsb = self.sbn("vsb", 2, [128, NH, 66], BF16)
            for k in range(2):
                self.pool("memset", dict(ap=vsb.t[k][:, :, 64:66], constant=1.0), [], [vsb.b[k]])
            self.pool("memset", dict(ap=ckvn_s[:, 256:288], constant=0.0), [], SM)
            self.pool("memset", dict(ap=ckvn_s[:, 288:289], constant=1.0), [], SM)
            QTst = self.sbn("QTst", 1, [128, NH, 512], BF16)
            KTst = self.sbn("KTst", 1, [128, NH, 512], BF16)
            STOP = os.environ.get("M1_STOP", "")
            if os.environ.get("KDEBUG"):
                print("M1 sbuf remaining", self.nc.sbuf_bytes_remaining)
            for gi, (t0, n) in enumerate(self.groups if STOP != "A" else []):
                G = n * 128
                c0 = t0 * 128
                is_s = (c0 >= TP)
                xT, xTb = xnT.next()
                for i in range(n):
                    self.front(t0 + i, self.xs, [self.xs_b[t0 + i]], xT, xTb, i * 128)
                qst, qstb = QTst.next()
                kst, kstb = KTst.next()
                for i in range(n):
                    ti = t0 + i
                    rows = slice(ti * 128, (ti + 1) * 128)
                    pj = [bk.next() for _ in range(3)]
                    for (bank, bb), (col, ncol) in zip(pj, ((0, 512), (512, 512), (1024, 32))):
                        for kc in range(8):
                            self.mm(bank[:, 0:ncol], xT[:, kc, i * 128:(i + 1) * 128], win[:, kc, col:col + ncol],
                                    kc == 0, kc == 7, [xTb, wb[0]], [bb], kc == 7)
                    (p0, p0b), (p1, p1b), (p2, p2b) = pj
                    rq, rqb = self.rstd_of([(p0[:, 0:512], [p0b]), (p1[:, 0:256], [p1b])], QL)
                    cq, cqb = cqn.next()
                    self.dve("scalar_tensor_tensor", dict(out=cq[:, 0:512], in0=p0[:, 0:512], scalar=rq, in1=qn_bc[:, 0:512],
                                                          op0=ALU.mult, op1=ALU.mult), [p0b, rqb, nb_[0]], [cqb])
                    self.dve("scalar_tensor_tensor", dict(out=cq[:, 512:768], in0=p1[:, 0:256], scalar=rq,
                                                          in1=qn_bc[:, 512:768], op0=ALU.mult, op1=ALU.mult),
                             [p1b, rqb, nb_[0]], [cqb])
                    rk, rkb = self.rstd_of([(p1[:, 256:512], [p1b])], KVL)
                    cf, cfb = ckf.next()
                    self.dve("scalar_tensor_tensor", dict(out=cf[:], in0=p1[:, 256:512], scalar=rk, in1=kvn_bc[:],
                                                          op0=ALU.mult, op1=ALU.mult), [p1b, rkb, nb_[1]], [cfb])
                    self.ld(self.lat_out[rows, :], cf[:], [cfb], [], cfb, q="pool", shared=[self.mla_out_b])
                    cb_, cbb = ckb.next()
                    self.S.op("act", "copy", dict(out=cb_[:], in_=cf[:]), [cfb], [cbb])
                    if is_s:
                        self.S.op("act", "copy", dict(out=ckvn_s[:, 0:256], in_=cf[:]), [cfb], SM)
                    kp, kpb = kpf.next()
                    r_, rb_ = rt.next()
                    cs = cosT[:, ti, :]
                    sn = sinT[:, ti, :]
                    t1 = r_[:, 0, 0, :]; t2 = r_[:, 1, 0, :]
                    tt = lambda o, x, y, op: self.dve("tensor_tensor", dict(out=o, in0=x, in1=y, op=op),
                                                      [p2b, rb_] + RT, [rb_, kpb])
                    tt(t1, p2[:, 0:16], cs, ALU.mult); tt(t2, p2[:, 16:32], sn, ALU.mult)
                    tt(kp[:, 0:16], t1, t2, ALU.subtract)
                    tt(t1, p2[:, 0:16], sn, ALU.mult); tt(t2, p2[:, 16:32], cs, ALU.mult)
                    tt(kp[:, 16:32], t1, t2, ALU.add)
                    self.ld(self.kpe_out[rows, :], kp[:], [kpb], [], kpb, q="pool", shared=[self.mla_out_b])
                    if STOP == "B":
                        continue
                    cqT, cqTb = cqnT.next()
                    self.transpose_into(cq, cqb, 6, cqT, cqTb, 0, eng="dve")
                    ckT, ckTb = ckvT.next()
                    self.transpose_into(cb_, cbb, 2, ckT, ckTb, 0, eng="dve")
                    if is_s:
                        self.dve("tensor_copy", dict(out=ckvnT_s[:], in_=ckT[:]), [ckTb], SM)
                    qb_ = [bk.next() for _ in range(3)]
                    for nbk, (bank, bb) in enumerate(qb_):
                        for kc in range(6):
                            self.mm(bank[:], cqT[:, kc, :], wuq[:, kc, nbk * 512:(nbk + 1) * 512], kc == 0, kc == 5,
                                    [cqTb, wb[1]], [bb], kc == 5)
                    q_, q_b = qsb.next()
                    for hh in range(2):
                        self.S.op("act", "copy", dict(out=q_[:, hh * 8:(hh + 1) * 8, 0:64],
                                                      in_=qb_[hh][0][:, :].rearrange("p (h d) -> p h d", d=64)),
                                  [qb_[hh][1]], [q_b])
                    qr = qb_[2][0][:, :].rearrange("p (h r) -> p h r", r=32)
                    qrb = qb_[2][1]
                    csb = self.bc(cs, 1, [128, 16, 16]); snb = self.bc(sn, 1, [128, 16, 16])
                    T1 = r_[:, 0]; T2 = r_[:, 1]
                    tq = lambda o, x, y, op: self.dve("tensor_tensor", dict(out=o, in0=x, in1=y, op=op),
                                                      [qrb, rb_] + RT, [rb_, q_b])
                    tq(T1, qr[:, :, 0:16], csb, ALU.mult); tq(T2, qr[:, :, 16:32], snb, ALU.mult)
                    tq(q_[:, :, 64:80], T1, T2, ALU.subtract)
                    tq(T1, qr[:, :, 0:16], snb, ALU.mult); tq(T2, qr[:, :, 16:32], csb, ALU.mult)
                    tq(q_[:, :, 80:96], T1, T2, ALU.add)
                    for hh in range(2):
                        p, pb = self.pT.next()
                        for h8 in range(8):
                            self.tr(p[0:96, h8 * 128:(h8 + 1) * 128], q_[:, hh * 8 + h8, :], [q_b], [pb], inc=(h8 == 7))
                        dst = QTs[0:96, hh * 8:(hh + 1) * 8, :] if is_s else qst[0:96, hh * 8:(hh + 1) * 8, i * 128:(i + 1) * 128]
                        self.S.op("act", "copy", dict(out=dst, in_=p[0:96, :].rearrange("p (h n) -> p h n", n=128)),
                                  [pb], SM if is_s else [qstb])
                    if STOP == "C":
                        continue
                    if is_s:
                        kb16 = cq
                        kpb16 = self.sb("kpb16", [128, RP], BF16)
                        kpb16_b = S.buf("kpb16")
                        self.dve("tensor_copy", dict(out=kpb16[:], in_=kp[:]), [kpb], [kpb16_b])
                        if "kpT" not in os.environ.get("M1_SKIP", ""):
                            p, pb = self.pT.next()
                            self.tr(p[0:32, 0:128], kpb16[:, :], [kpb16_b], [pb], inc=True)
                            self.dve("tensor_copy", dict(out=kpeT_s[:, :], in_=p[0:32, 0:128]), [pb], SM)
                        continue
                    kvb = [bk.next() for _ in range(4)]
                    for nbk, (bank, bb) in enumerate(kvb):
                        for kc in range(2):
                            self.mm(bank[:], ckT[:, kc, :], wukv[:, kc, nbk * 512:(nbk + 1) * 512], kc == 0, kc == 1,
                                    [ckTb, wb[2]], [bb], kc == 1)
                    if STOP == "D1":
                        continue
                    k_, k_b = ksb.next()
                    v_, v_b = vsb.next()
                    for nbk, (bank, bb) in enumerate(kvb):
                        kv4 = bank[:, :].rearrange("p (h e) -> p h e", e=128)
                        self.S.op("act", "copy", dict(out=k_[:, nbk * 4:(nbk + 1) * 4, 0:64], in_=kv4[:, :, 0:64]), [bb], [k_b])
                        self.dve("tensor_copy", dict(out=v_[:, nbk * 4:(nbk + 1) * 4, 0:64], in_=kv4[:, :, 64:128]), [bb], [v_b])
                    if STOP == "D2":
                        continue
                    self.dve("tensor_copy", dict(out=k_[:, :, 64:96], in_=self.bc(kp[:], 1, [128, NH, RP])), [kpb], [k_b])
                    if STOP == "D3":
                        continue
                    if "vst" not in os.environ.get("M1_SKIP", ""):
                        self.ld(self.v_s[rows, :], v_[:].rearrange("p h e -> p (h e)"), [v_b], [], v_b, q="pool", shared=[self.qkv_s_b])
                    for hh in range(2):
                        p, pb = self.pT.next()
                        for h8 in range(8):
                            self.tr(p[0:96, h8 * 128:(h8 + 1) * 128], k_[:, hh * 8 + h8, :], [k_b], [pb], inc=(h8 == 7))
                        self.dve("tensor_copy", dict(out=kst[0:96, hh * 8:(hh + 1) * 8, i * 128:(i + 1) * 128],
                                                     in_=p[0:96, :].rearrange("p (h n) -> p h n", n=128)), [pb], [kstb])
                if not is_s and "qkst" not in os.environ.get("M1_SKIP", ""):
                    self.ld(self.qT_s[:, :, c0:c0 + G].rearrange("h d t -> d h t"), qst[0:96, :, 0:G], [qstb],
                            [], qstb, q="pool", shared=[self.qkv_s_b])
                    self.ld(self.kT_s[:, :, c0:c0 + G].rearrange("h d t -> d h t"), kst[0:96, :, 0:G], [kstb],
                            [], kstb, q="pool", shared=[self.qkv_s_b])
            S.end_phase()

        oTs = self.gsb("oTs", [128, 8, 128], BF16)
        oTs_b = S.buf("oTs")
        if os.environ.get("MLA_M1_ONLY"):
            return
        SKIP = os.environ.get("MLA_SKIP", "").split(",")
        self.pes = ExitStack()
        with self.pes:
            self.pT = self.psn("pT", 2, [128, 1024], BF16)
            bko = self.psn("bko", 4, [128, 512])
            bks = self.psn("bks", 2, [128, 512])
            bk = bko
            PT = self.sbn("PT", 3, [128, 512], BF16)
            wukT = self.sb("wukT", [64, NH, 256], BF16)
            wukv = self.sb("wukv2", [128, 2, 2048], BF16)
            wuvz = self.sb("wuvz", [128, 2, NH, 128], BF16)
            wb = S.bufs("mlaw2", 4)
            self.ld(wukT[:].rearrange("p a b -> p (a b)"), self.wukT_b[:, :], [self.wcast2_b], [wb[0]], wb[0])
            self.ld(wukv[:].rearrange("p a b -> p (a b)"), self.wukv_b[:, :], [self.wcast2_b], [wb[1]], wb[1])
            self.pool("memset", dict(ap=wuvz[:], constant=0.0), [], [wb[2]])
            for cc in range(2):
                for par in range(2):
                    src = wukv[:, cc, :].rearrange("p (k two e) -> p k two e", two=2, e=128)[:, :, par, 64:128]
                    dst = wuvz[:, cc].rearrange("p (k two) n -> p k two n", two=2)[:, :, par, par * 64:(par + 1) * 64]
                    self.dve("tensor_copy", dict(out=dst, in_=src), [wb[1]], [wb[2]])
            qlatT = self.sb("qlatT", [128, 2, NS, 8, NH], BF16)
            qrT = self.sb("qrT", [32, NS, 8, NH], BF16)
            olatT = self.sb("olatT", [128, 2, NS, 8, NH], BF16)
            ql_b = S.buf("qlat")
            ol_b = S.buf("olat")
            for h in range(NH):
                for cc in range(2):
                    bank, bb = bk.next()
                    self.mm(bank[:, 0:128], wukT[0:64, h, cc * 128:(cc + 1) * 128], QTs[0:64, h, :], True, True,
                            [wb[0]] + SM, [bb], True)
                    self.dve("tensor_copy", dict(out=qlatT[:, cc].rearrange("p s t h -> p (s t) h")[:, :, h],
                                                 in_=bank[:, 0:128]), [bb], [ql_b])
                bank, bb = bk.next()
                self.mm(bank[0:32, 0:128], self.ident[0:96, 64:96], QTs[0:96, h, :], True, True, [self.ident_b] + SM, [bb], True)
                self.dve("tensor_copy", dict(out=qrT[:].rearrange("p s t h -> p (s t) h")[:, :, h], in_=bank[0:32, 0:128]),
                         [bb], [ql_b])
            nmask = self.sb("nmask", [8, 8, NH], BF16)
            nm_b = S.buf("nmask")
            self.pool("memset", dict(ap=nmask[:], constant=1.0), [], [nm_b])
            self.pool("affine_select", dict(out=nmask[:], in_=nmask[:], pattern=[[1, 8], [0, NH]], compare_op=ALU.is_ge,
                                            fill=0.0, base=0, channel_multiplier=-1), [], [nm_b])
            ptb_ = self.sb("ptb", [128, NS * NPG], I32)
            idx = self.sb("idx", [128, NS * NPG // 4], I32)
            iof = self.sb("iof", [128, 1])
            ix_b = S.buf("idx")
            self.ld(ptb_[:], self.ptab.rearrange("s j -> (s j)").partition_broadcast(128), [], [ix_b], ix_b)
            for q4 in range(4):
                self.pool("iota", dict(out=iof[32 * q4:32 * q4 + 32, :], pattern=[[0, 1]], base=0, channel_multiplier=1,
                                       allow_small_or_imprecise_dtypes=True), [], [ix_b])
            for q4 in range(4):
                self.dve("tensor_scalar", dict(out=idx[32 * q4:32 * q4 + 32, :],
                                               in0=ptb_[32 * q4:32 * q4 + 32, :].rearrange("p (m f) -> p m f", f=4)[:, :, q4],
                                               scalar1=32.0, scalar2=iof[32 * q4:32 * q4 + 32, 0:1], op0=ALU.mult,
                                               op1=ALU.add), [ix_b], [ix_b])
            ones_k = self.sb("ones_k", [128, 1], BF16)
            self.dve("memset", dict(ap=ones_k[:], constant=1.0), [], [ix_b])
            cache4 = self.cache_cat.rearrange("(a f) c -> a (f c)", f=4)
            NPGS = 6
            pg = self.sbn("pg", NPGS, [128, 4, 288], BF16)
            ckT4 = self.sbn("ckT4", 3, [128, 4, 2, 128], BF16)
            kpT4 = self.sbn("kpT4", 3, [32, 4, 128], BF16)
            pn = self.sbn("pn", 2, [8, 128], BF16)
            knew = self.sbn("knew", 2, [8, 289], BF16)
            olat = self.sbn("olat", 2, [128, 256], BF16)
            rl1 = self.sbn("rl1", 2, [128, 1])
            for b in range(NS if "decode" not in SKIP else 0):
                acc, accb = bk.next()
                qlb = [qlatT[:, cc, b].rearrange("p t h -> p (t h)") for cc in range(2)]
                qrb = qrT[:, b].rearrange("p t h -> p (t h)")
                sn_, snb = bk.next()
                for cc in range(2):
                    self.mm(sn_[0:8, 0:128], ckvnT_s[:, cc, 8 * b:8 * b + 8], qlb[cc], cc == 0, False, SM + [ql_b], [snb], False)
                self.mm(sn_[0:8, 0:128], kpeT_s[0:32, 8 * b:8 * b + 8], qrb, False, True, SM + [ql_b], [snb], True)
                pn_, pnb = pn.next()
                self.act(dict(out=pn_[:], in_=sn_[0:8, 0:128], func=AF.Exp, scale=ATT_SCALE), [snb], [pnb])
                self.dve("tensor_tensor", dict(out=pn_[:], in0=pn_[:], in1=nmask[:].rearrange("p t h -> p (t h)"), op=ALU.mult),
                         [pnb, nm_b], [pnb])
                kn, knb = bk.next()
                self.mm(kn[0:8, 0:289], self.ident[:, 8 * b:8 * b + 8], ckvn_s[:, 0:289], True, True, SM + [self.ident_b], [knb], True)
                kw_, kwb = knew.next()
                self.dve("tensor_copy", dict(out=kw_[:], in_=kn[0:8, 0:289]), [knb], [kwb])
                accl, acclb = bk.next()
                self.mm(acc[:, 0:288], pn_[:], kw_[:, 0:288], True, False, [pnb, kwb], [accb], False)
                self.mm(accl[:, 0:1], pn_[:], kw_[:, 288:289], True, False, [pnb, kwb], [acclb], False)
                nb4 = NPG // 4
                pgs = {}
                ctk = {}
                pts = {}

                def stage_G(k):
                    pgt, pgb = pg.next()
                    col = b * nb4 + k
                    S.dma("pool", pgt[:].rearrange("p a c -> p (a c)"), cache4, reads=[ix_b], writes=[pgb], owner=pgb,
                          indirect=idx[:, col:col + 1])
                    pgs[k] = (pgt, pgb)

                def stage_A(k):
                    pgt, pgb = pgs[k]
                    p, pb = self.pT.next()
                    for jj in range(4):
                        for cc in range(2):
                            self.tr(p[:, (jj * 2 + cc) * 128:(jj * 2 + cc + 1) * 128], pgt[:, jj, cc * 128:(cc + 1) * 128],
                                    [pgb], [pb], inc=(jj == 3 and cc == 1))
                    c4, c4b = ckT4.next()
                    self.S.op("act", "copy", dict(out=c4[:].rearrange("p a b n -> p (a b n)"), in_=p[:, :]), [pb], [c4b])
                    p2, p2b = self.pT.next()
                    for jj in range(4):
                        self.tr(p2[0:32, jj * 128:(jj + 1) * 128], pgt[:, jj, 256:288], [pgb], [p2b], inc=(jj == 3))
                    k4, k4b = kpT4.next()
                    self.dve("tensor_copy", dict(out=k4[:].rearrange("p a n -> p (a n)"), in_=p2[0:32, 0:512]), [p2b], [k4b])
                    ctk[k] = (c4, c4b, k4, k4b)

                def stage_B(k):
                    c4, c4b, k4, k4b = ctk.pop(k)
                    st, stb = bks.next()
                    for jj in range(4):
                        for cc in range(2):
                            self.mm(st[:, jj * 128:(jj + 1) * 128], c4[:, jj, cc, :], qlb[cc], cc == 0, False,
                                    [c4b, ql_b], [stb], False)
                        self.mm(st[:, jj * 128:(jj + 1) * 128], k4[0:32, jj, :], qrb, False, True, [k4b, ql_b], [stb], jj == 3)
                    pt, ptb = PT.next()
                    self.act(dict(out=pt[:], in_=st[:], func=AF.Exp, scale=ATT_SCALE), [stb], [ptb])
                    pts[k] = (pt, ptb)

                def stage_C(k):
                    pt, ptb = pts.pop(k)
                    pgt, pgb = pgs.pop(k)
                    for jj in range(4):
                        last = (k == nb4 - 1 and jj == 3)
                        self.mm(acc[:, 0:288], pt[:, jj * 128:(jj + 1) * 128], pgt[:, jj, :], False, last, [ptb, pgb],
                                [accb], False)
                        self.mm(accl[:, 0:1], pt[:, jj * 128:(jj + 1) * 128], ones_k[:, 0:1], False, last, [ptb, ix_b],
                                [acclb], last)

                PF = min(NPGS - 2, nb4)
                for k in range(PF):
                    stage_G(k)
                stage_A(0)
                if nb4 > 1:
                    stage_A(1)
                stage_B(0)
                for k in range(nb4):
                    if k + PF < nb4:
                        stage_G(k + PF)
                    if k + 2 < nb4:
                        stage_A(k + 2)
                    if k + 1 < nb4:
                        stage_B(k + 1)
                    stage_C(k)
                r1, r1b = rl1.next()
                self.dve("reciprocal", dict(out=r1[:], in_=accl[:, 0:1]), [acclb], [r1b])
                ol, olb = olat.next()
                self.dve("tensor_scalar", dict(out=ol[:], in0=acc[:, 0:256], scalar1=r1[:, 0:1], scalar2=None, op0=ALU.mult),
                         [accb, r1b], [olb])
                p, pb = self.pT.next()
                for cc in range(2):
                    self.tr(p[:, cc * 128:(cc + 1) * 128], ol[:, cc * 128:(cc + 1) * 128], [olb], [pb], inc=(cc == 1))
                self.dve("tensor_copy", dict(out=olatT[:, :, b].rearrange("p c t h -> p c (t h)"),
                                             in_=p[:, 0:256].rearrange("p (c n) -> p c n", n=128)), [pb], [ol_b])
            for k in range(8):
                bank, bb = bk.next()
                first = True
                for par in range(2):
                    h = 2 * k + par
                    for cc in range(2):
                        self.mm(bank[:, 0:128], wuvz[:, cc, h, :], olatT[:, cc].rearrange("p s t h -> p (s t) h")[:, :, h],
                                first, (par == 1 and cc == 1), [wb[2], ol_b], [bb], (par == 1 and cc == 1))
                        first = False
                self.dve("tensor_copy", dict(out=oTs[:, k, :], in_=bank[:, 0:128]), [bb], [oTs_b])
            S.end_phase()

        self.pes = ExitStack()
        with self.pes:
            self.common_alloc(4)
            self.load_gamma(self.gpost, self.gpost_b, self.norm_post[ni])
            wo = self.sb("wo", [128, 8, D], BF16)
            wo_b_ = S.buf("wo")
            self.ld(wo[:].rearrange("p a b -> p (a b)"), self.wo_b[:, :], [self.wcast2_b], [wo_b_], wo_b_)
            bko = self.psn("bko", 4, [128, 512])
            bks = self.psn("bks", 2, [128, 512])
            bk = bko
            o_all = self.sb("o_all", [128, NTP, D], BF16)
            o_b = S.bufs("o_all", NTP // 4)
            masks = self.sb("masks", [128, 4, 512], BF16)
            mk_b = S.buf("masks")
            self.pool("memset", dict(ap=masks[:], constant=1.0), [], [mk_b])
            for j in range(4):
                self.pool("affine_select", dict(out=masks[:, j, :], in_=masks[:, j, :], pattern=[[1, 512]],
                                                compare_op=ALU.is_ge, fill=0.0, base=-128 * j, channel_multiplier=-1),
                          [], [mk_b])
            QTh = self.sbn("QTh", 2, [128, TP], BF16)
            KTh = self.sbn("KTh", 2, [128, TP], BF16)
            Vh = self.sbn("Vh", 2, [128, NTP, 66], BF16)
            PT = self.sbn("PT", 3, [128, 512], BF16)
            rl = self.sbn("rl", 2, [128, 4])
            v_view = self.v_s.rearrange("(t p) (h e) -> p t h e", p=128, e=66)
            NQG = TP // 512

            for h in range(NH if "prompt" not in SKIP else 0):
                qt, qtb = QTh.next()
                kt_, ktb = KTh.next()
                vh, vhb = Vh.next()
                self.ld(qt[0:96, :], self.qT_s[h], [self.qkv_s_b], [qtb], qtb)
                self.ld(kt_[0:96, :], self.kT_s[h], [self.qkv_s_b], [ktb], ktb)
                with self.nc.allow_non_contiguous_dma(reason="per-head V slice (130B rows)"):
                    pass
                self.S.dma("sp", vh[:], v_view[:, :, h, :], reads=[self.qkv_s_b], writes=[vhb], owner=vhb)
                for Qg in range(NQG):
                    oaccs = [bko.next() for _ in range(4)]
                    nkt = 4 * Qg + 4
                    def issue_st(kt):
                        st, stb = bks.next()
                        self.mm(st[:], kt_[0:96, kt * 128:(kt + 1) * 128], qt[0:96, Qg * 512:(Qg + 1) * 512], True, True,
                                [ktb, qtb], [stb], True)
                        pt, ptb = PT.next()
                        self.act(dict(out=pt[:], in_=st[:], func=AF.Exp, scale=ATT_SCALE), [stb], [ptb])
                        j = kt - 4 * Qg
                        if j >= 0:
                            self.dve("tensor_tensor", dict(out=pt[:], in0=pt[:], in1=masks[:, j, :], op=ALU.mult),
                                     [ptb, mk_b], [ptb])
                        return pt, ptb
                    nxt_pt = issue_st(0)
                    for kt in range(nkt):
                        pt, ptb = nxt_pt
                        if kt + 1 < nkt:
                            nxt_pt = issue_st(kt + 1)
                        for qi in range(4):
                            if 4 * Qg + qi >= kt:
                                last = (kt == 4 * Qg + qi)
                                self.mm(oaccs[qi][0][:, 0:65], pt[:, qi * 128:(qi + 1) * 128], vh[:, kt, 0:65],
                                        kt == 0, last, [ptb, vhb], [oaccs[qi][1]], last)
                    r, rb = rl.next()
                    for qi in range(4):
                        oacc, oab = oaccs[qi]
                        self.dve("reciprocal", dict(out=r[:, qi:qi + 1], in_=oacc[:, 64:65]), [oab], [rb])
                        self.dve("tensor_scalar", dict(out=o_all[:, Qg * 4 + qi, h * 64:(h + 1) * 64], in0=oacc[:, 0:64],
                                                       scalar1=r[:, qi:qi + 1], scalar2=None, op0=ALU.mult),
                                 [oab, rb], [o_b[Qg]])
            oT = self.sbn("oT", 2, [128, 8, 128], BF16)
            for ti in range(NT if "m3" not in SKIP else 0):
                xt, xb = self.xt.next()
                self.ld(xt[:], self.xs[ti * 128:(ti + 1) * 128, :], [self.xs_b[ti]], [xb], xb)
                if ti < NTP:
                    ot, otb = oT.next()
                    p, pb = self.pT.next()
                    for c in range(8):
                        self.tr(p[:, c * 128:(c + 1) * 128], o_all[:, ti, c * 128:(c + 1) * 128], [o_b[ti // 4]], [pb], inc=(c == 7))
                    self.S.op("act", "copy", dict(out=ot[:], in_=p[:, :].rearrange("p (c n) -> p c n", n=128)), [pb], [otb])
                else:
                    ot, otb = oTs, oTs_b
                ys = []
                for hh in range(2):
                    y, ybuf = bk.next()
                    for kc in range(8):
                        self.mm(y[:], ot[:, kc, :], wo[:, kc, hh * 512:(hh + 1) * 512], kc == 0, kc == 7, [otb, wo_b_], [ybuf],
                                kc == 7)
                    ys.append((y[:], ybuf))
                self.post_residual(ys, xt, xb, self.xs, [self.xs_b[ti]], ti, 1.0)
            S.end_phase()


def _kc(w, nk):
    n = w.shape[1]
    return np.ascontiguousarray(w.reshape(nk, 128, n).transpose(1, 0, 2)).reshape(128, nk * n)


def _qp(a):
    rest = a.shape[2:]
    a = a.reshape((32, 2, 64) + rest)
    perm = (1, 2, 0) + tuple(range(3, 3 + len(rest)))
    return np.ascontiguousarray(a.transpose(perm)).reshape((128, 32) + rest)


def prep_shared(inp):
    f32 = np.float32
    A = lambda k: np.asarray(inp[k], f32)
    g = A("ffn_w_gate").reshape(4, 8, 128, NF, 128)
    u = A("ffn_w_up").reshape(4, 8, 128, NF, 128)
    d = A("ffn_w_down").reshape(4, NF, 128, D)
    sh = {
        "wg_h": np.ascontiguousarray(g.transpose(0, 3, 2, 1, 4)).reshape(4, NF * 128, 8 * 128),
        "wu_h": np.ascontiguousarray(u.transpose(0, 3, 2, 1, 4)).reshape(4, NF * 128, 8 * 128),
        "wd_h": np.ascontiguousarray(d.transpose(0, 2, 1, 3)).reshape(4, 128, NF * D),
        "norm_pre": np.ascontiguousarray(A("norm_pre").reshape(6, D)),
        "norm_post": np.ascontiguousarray(A("norm_post").reshape(6, D)),
        "s5_are": _qp(A("ssm_a_re")[0]),
        "s5_aim": _qp(A("ssm_a_im")[0]),
        "s5_ldt": _qp(np.broadcast_to(A("ssm_log_dt")[0][:, None], (64, 64))),
        "s5_bre": _qp(A("ssm_b_re")[0]).reshape(128, 512),
        "s5_bim": _qp(A("ssm_b_im")[0]).reshape(128, 512),
        "s5_cre": _qp(A("ssm_c_re")[0].transpose(0, 2, 1)).reshape(128, 512),
        "s5_cim": _qp(A("ssm_c_im")[0].transpose(0, 2, 1)).reshape(128, 512),
        "s5_d": np.ascontiguousarray(A("ssm_d")[0].reshape(8, 128).T),
        "wglu_h": _kc(A("ssm_w_glu")[0], 8),
        "win_h": _kc(A("mla_w_in")[0], 8),
        "wuq_h": _kc(np.concatenate([A("mla_w_uq")[0].reshape(QL, NH, DQK)[:, :, :DN].reshape(QL, NH * DN),
                                     A("mla_w_uq")[0].reshape(QL, NH, DQK)[:, :, DN:].reshape(QL, NH * RP)], axis=1), 6),
        "wukv_h": _kc(A("mla_w_ukv")[0], 2),
        "wukT_h": np.ascontiguousarray(A("mla_w_ukv")[0].reshape(256, 16, 128)[:, :, :64].transpose(2, 1, 0)).reshape(64, 4096),
        "wo_h": _kc(A("mla_w_o")[0], 8),
        "qnorm": np.ascontiguousarray(A("mla_q_norm").reshape(1, QL)),
        "kvnorm": np.ascontiguousarray(A("mla_kv_norm").reshape(1, KVL)),
        "cache_cat": np.concatenate([A("cache_kv_latent")[0].reshape(-1, KVL), A("cache_k_rope")[0].reshape(-1, RP)], axis=1),
    }
    return sh


def prep_core(inp, c, cfg):
    f32 = np.float32
    ns = cfg.NS
    xp = np.asarray(inp["x_prompt"], f32)[c].reshape(cfg.TP, D)
    xsm = np.asarray(inp["x_sample"], f32)[c * ns:(c + 1) * ns].reshape(ns * cfg.TS, D)
    hre = np.asarray(inp["state_ssm_re"], f32)[0, c * ns:(c + 1) * ns]
    him = np.asarray(inp["state_ssm_im"], f32)[0, c * ns:(c + 1) * ns]
    h = np.stack([hre, him], axis=0)
    h = h.transpose(2, 3, 0, 1)
    h0 = _qp(np.ascontiguousarray(h)).reshape(128, 32 * 2 * ns)
    return {
        "x_in": np.ascontiguousarray(np.concatenate([xp, xsm], axis=0)),
        "s5_h0": h0,
        "ptab": np.ascontiguousarray(np.asarray(inp["page_table"], np.int32)[c * ns:(c + 1) * ns]),
    }


def _unqp(a):
    rest = a.shape[2:]
    a = a.reshape((2, 64, 32) + rest)
    perm = (2, 0, 1) + tuple(range(3, 3 + len(rest)))
    return np.ascontiguousarray(a.transpose(perm)).reshape((64, 64) + rest)


def assemble(results, cfg, n_cores):
    TP, ns, ts = cfg.TP, cfg.NS, cfg.TS
    f32 = np.float32
    yp = np.zeros((n_cores, TP, D), f32)
    ys = np.zeros((n_cores * ns, ts, D), f32)
    srp = np.zeros((1, n_cores, 64, 64), f32)
    sip = np.zeros((1, n_cores, 64, 64), f32)
    srs = np.zeros((1, n_cores * ns, 64, 64), f32)
    sis = np.zeros((1, n_cores * ns, 64, 64), f32)
    lp = np.zeros((1, n_cores, TP, KVL), f32)
    kp = np.zeros((1, n_cores, TP, RP), f32)
    ls = np.zeros((1, n_cores * ns, ts, KVL), f32)
    ks = np.zeros((1, n_cores * ns, ts, RP), f32)
    for c, r in enumerate(results):
        y = r["y_out"]
        yp[c] = y[:TP]
        ys[c * ns:(c + 1) * ns] = y[TP:].reshape(ns, ts, D)
        sp = _unqp(r["ssm_p"].reshape(128, 32, 2))
        srp[0, c] = sp[:, :, 0]
        sip[0, c] = sp[:, :, 1]
        ss = _unqp(r["ssm_s"].reshape(128, 32, 2, ns))
        srs[0, c * ns:(c + 1) * ns] = ss[:, :, 0, :].transpose(2, 0, 1)
        sis[0, c * ns:(c + 1) * ns] = ss[:, :, 1, :].transpose(2, 0, 1)
        lat = r["lat_out"]
        kpe = r["kpe_out"]
        lp[0, c] = lat[:TP]
        kp[0, c] = kpe[:TP]
        ls[0, c * ns:(c + 1) * ns] = lat[TP:].reshape(ns, ts, KVL)
        ks[0, c * ns:(c + 1) * ns] = kpe[TP:].reshape(ns, ts, RP)
    return (yp, ys, srp, sip, srs, sis, lp, kp, ls, ks)


def run(inputs, n_cores, cfg):
    nc = build(cfg)
    sh = prep_shared(inputs)
    in_maps = []
    for c in range(n_cores):
        m = dict(sh)
        m.update(prep_core(inputs, c, cfg))
        in_maps.append(m)
    res = run_bass_kernel_spmd(nc, in_maps, core_ids=list(range(n_cores)))
    return assemble(res.results, cfg, n_cores)


def kernel(**inputs):
    cfg = Cfg(TP=4096, NS=16, TS=8, NPG=128, NPHYS=int(np.asarray(inputs["cache_kv_latent"]).shape[1]))
    return run(inputs, 8, cfg)
```

```python
import math
import os
import numpy as np
import concourse.bass as bass
import concourse.mybir as mybir
from concourse.bass_utils import run_bass_kernel_spmd
from contextlib import ExitStack

F32 = mybir.dt.float32
BF16 = mybir.dt.bfloat16
I32 = mybir.dt.int32
AF = mybir.ActivationFunctionType
ALU = mybir.AluOpType
AX = mybir.AxisListType

D = 1024
DFF = 2816
NF = DFF // 128
EPS = 1e-6
QL = 768
KVL = 256
RP = 32
NH = 16
DN = 64
DV = 64
DQK = DN + RP
ATT_SCALE = DQK ** -0.5
ROPE_THETA = 10000.0
PAGE = 128
TWO_PI = 2.0 * math.pi
C1_2PI = 6.28125
C2_2PI = TWO_PI - 6.28125


class DSem:
    __slots__ = ("sem", "cnt")

    def __init__(self, sem):
        self.sem = sem
        self.cnt = 0


class Buf:
    __slots__ = ("name", "w", "r", "ds", "excl")

    def __init__(self, name, excl=False):
        self.name = name
        self.w = {}
        self.r = {}
        self.ds = None
        self.excl = excl


class Sched:
    ENGS = ("pe", "act", "dve", "pool", "sp")
    HANDLES = {"pe": "tensor", "act": "scalar", "dve": "vector", "pool": "gpsimd", "sp": "sync"}

    def __init__(self, nc, es):
        self.nc = nc
        self.es = es
        self.q = {e: [] for e in self.ENGS}
        self.cnt = {e: 0 for e in self.ENGS}
        self.pending = {e: False for e in self.ENGS}
        self.sem = {e: es.enter_context(nc.semaphore("sem_" + e)) for e in self.ENGS}
        self.waited = {}
        self.ds_free = {}
        self.ds_used = []
        self.ds_all = []
        self.ninst = 0

    def buf(self, name):
        return Buf(name)

    def bufs(self, name, n):
        return [Buf("%s%d" % (name, i)) for i in range(n)]

    def _dsem(self, b, kind):
        if b.ds is None:
            b.ds = {}
        if kind not in b.ds:
            free = self.ds_free.setdefault(kind, [])
            if free:
                d = free.pop()
            else:
                d = DSem(self.es.enter_context(self.nc.semaphore("ds%s_%d" % (kind, len(self.ds_all)))))
                self.ds_all.append(d)
            b.ds[kind] = d
            self.ds_used.append((kind, d))
        return b.ds[kind]

    def _filter(self, eng, deps):
        out = []
        for s, v in deps.items():
            if eng == "pe" and s is self.sem["pe"]:
                continue
            key = (eng, id(s))
            if self.waited.get(key, 0) >= v:
                continue
            self.waited[key] = v
            out.append((s, v))
        return out

    def _deps(self, eng, reads, writes, shared=(), skip_sem=None):
        deps = {}

        def add(s, v):
            if deps.get(s, 0) < v:
                deps[s] = v

        for b in reads:
            for s, v in b.w.items():
                add(s, v)
            if b.excl:
                for s, v in b.r.items():
                    add(s, v)
        for b in writes:
            for s, v in b.w.items():
                add(s, v)
            for s, v in b.r.items():
                add(s, v)
        for b in shared:
            for s, v in b.r.items():
                add(s, v)
        if skip_sem is not None:
            deps.pop(skip_sem, None)
        return self._filter(eng, deps)

    def _mark(self, ev, reads, writes, shared=()):
        s, v = ev
        for b in reads:
            if b.r.get(s, 0) < v:
                b.r[s] = v
        for b in writes:
            b.w = {s: v}
            b.r = {}
        for b in shared:
            if b.w.get(s, 0) < v:
                b.w[s] = v

    def op(self, eng, mname, kw, reads=(), writes=(), inc=True):
        fn = (lambda e, mname=mname, kw=kw: getattr(e, mname)(**kw))
        waits = self._deps(eng, reads, writes)
        if inc:
            self.cnt[eng] += 1
            ev = (self.sem[eng], self.cnt[eng])
            self.pending[eng] = False
            incr = (self.sem[eng], 1)
        else:
            ev = (self.sem[eng], self.cnt[eng] + 1)
            self.pending[eng] = True
            incr = None
        self._mark(ev, reads, writes)
        self.q[eng].append((waits, fn, incr))
        self.ninst += 1

    def dma(self, q, out, in_, reads=(), writes=(), owner=None, indirect=None, shared=(), **kw):
        ds = self._dsem(owner, "sw" if q == "pool" else "hw")
        skip = ds.sem if (owner in writes) else None
        waits = self._deps(q, reads, writes, shared, skip_sem=skip)
        ds.cnt += 1
        ev = (ds.sem, 16 * ds.cnt)
        self._mark(ev, reads, writes, shared)
        if indirect is not None:
            fn = (lambda e: e.indirect_dma_start(out=out, out_offset=None, in_=in_,
                                                 in_offset=bass.IndirectOffsetOnAxis(ap=indirect, axis=0)))
        else:
            fn = (lambda e: e.dma_start(out=out, in_=in_, **kw))
        self.q[q].append((waits, fn, (ds.sem, 16)))
        self.ninst += 1

    def barrier(self):
        for en in self.ENGS:
            if self.pending[en]:
                self.cnt[en] += 1
                self.pending[en] = False
                self.q[en].append(([], (lambda e: e.nop()), (self.sem[en], 1)))
        deps = {}
        for en in self.ENGS:
            if self.cnt[en] > 0:
                deps[self.sem[en]] = self.cnt[en]
        for ds in self.ds_all:
            if ds.cnt > 0:
                deps[ds.sem] = 16 * ds.cnt
        for en in self.ENGS:
            self.q[en].append((self._filter(en, dict(deps)), None, None))

    def end_phase(self):
        self.barrier()
        self.emit()
        for kind, d in self.ds_used:
            self.ds_free.setdefault(kind, []).append(d)
        self.ds_used = []

    def emit(self):
        nc = self.nc
        with nc.Block() as block:
            for en in self.ENGS:
                items = self.q[en]

                def body(e, items=items):
                    for waits, fn, incr in items:
                        for s, v in waits:
                            e.wait_ge(s, v)
                        if fn is not None:
                            ins = fn(e)
                            if incr is not None:
                                ins.then_inc(incr[0], incr[1])

                getattr(block, self.HANDLES[en])(body)
        self.q = {e: [] for e in self.ENGS}


class RR:
    def __init__(self, tensors, bufs):
        self.t = tensors
        self.b = bufs
        self.i = 0

    def next(self):
        k = self.i % len(self.t)
        self.i += 1
        return self.t[k], self.b[k]


class Cfg:
    def __init__(self, TP=4096, NS=16, TS=8, NPG=128, NPHYS=20480, stages=None):
        self.TP = TP
        self.NS = NS
        self.TS = TS
        self.NPG = NPG
        self.NPHYS = NPHYS
        self.NTOK = TP + NS * TS
        assert NS * TS == 128 and TP % 512 == 0 and TS == 8
        self.NT = self.NTOK // 128
        self.NB = TP // 8
        self.NBT = self.NB + NS
        self.PAST = NPG * PAGE
        self.stages = stages


def build(cfg):
    nc = bass.Bass("TRN2", target_bir_lowering=False)
    es = ExitStack()
    with es:
        P = Prog(nc, es, cfg)
        P.run()
    return nc


class Prog:
    def __init__(self, nc, es, cfg):
        self.nc = nc
        self.es = es
        self.cfg = cfg
        self.S = Sched(nc, es)
        self.pes = None
        groups = []
        t = 0
        while t < cfg.NT:
            n = 4 if (t * 128) < cfg.TP else 1
            groups.append((t, n))
            t += n
        self.groups = groups

    def dram_in(self, name, shape, dt=F32):
        return self.nc.dram_tensor(name, list(shape), dt, kind="ExternalInput").ap()

    def dram_out(self, name, shape, dt=F32):
        return self.nc.dram_tensor(name, list(shape), dt, kind="ExternalOutput").ap()

    def dram_tmp(self, name, shape, dt=F32):
        return self.nc.dram_tensor(name, list(shape), dt, kind="Internal").ap()

    def gsb(self, name, shape, dt=F32):
        return self.es.enter_context(self.nc.sbuf_tensor(name, list(shape), dt))

    def sb(self, name, shape, dt=F32):
        self._n = getattr(self, "_n", 0) + 1
        return self.pes.enter_context(self.nc.sbuf_tensor("%s_%d" % (name, self._n), list(shape), dt))

    def ps(self, name, shape, dt=F32):
        self._n = getattr(self, "_n", 0) + 1
        return self.pes.enter_context(self.nc.psum_tensor("%s_%d" % (name, self._n), list(shape), dt))

    def sbn(self, name, n, shape, dt=F32):
        return RR([self.sb("%s%d" % (name, i), shape, dt) for i in range(n)], self.S.bufs(name, n))

    def psn(self, name, n, shape, dt=F32):
        return RR([self.ps("%s%d" % (name, i), shape, dt) for i in range(n)],
                  [Buf("%s%d" % (name, i), excl=True) for i in range(n)])

    def act(self, kw, reads=(), writes=()):
        self.S.op("act", "activation", kw, reads, writes)

    def dve(self, m, kw, reads=(), writes=()):
        self.S.op("dve", m, kw, reads, writes)

    def pool(self, m, kw, reads=(), writes=()):
        self.S.op("pool", m, kw, reads, writes)

    def mm(self, out, lhsT, rhs, start, stop, reads, writes, inc):
        self.S.op("pe", "matmul", dict(out=out, lhsT=lhsT, rhs=rhs, start=start, stop=stop), reads, writes, inc=inc)

    def tr(self, out, in_, reads, writes, inc):
        n = in_.shape[0]
        self.S.op("pe", "transpose", dict(out=out, in_=in_, identity=self.ident[0:n, 0:n]),
                  list(reads) + [self.ident_b], writes, inc=inc)

    def ld(self, out, in_, reads, writes, owner, q="sp", shared=()):
        self.S.dma(q, out, in_, reads=reads, writes=writes, owner=owner, shared=shared)

    def run(self):
        cfg = self.cfg
        S = self.S
        NT, NTOK, TP = cfg.NT, cfg.NTOK, cfg.TP
        self.x_in = self.dram_in("x_in", [NTOK, D])
        self.y_out = self.dram_out("y_out", [NTOK, D])
        self.xs = self.dram_tmp("xs", [NTOK, D])
        self.xs_b = S.bufs("xs", NT)
        self.xin_b = S.buf("x_in")
        self.y_b = S.bufs("y", NT)
        self.norm_pre = self.dram_in("norm_pre", [6, D])
        self.norm_post = self.dram_in("norm_post", [6, D])
        self.wg_h = self.dram_in("wg_h", [4, NF * 128, 8 * 128])
        self.wu_h = self.dram_in("wu_h", [4, NF * 128, 8 * 128])
        self.wd_h = self.dram_in("wd_h", [4, 128, NF * D])
        self.wg_b = self.dram_tmp("wg_b", [4, NF * 128, 8 * 128], BF16)
        self.wu_b = self.dram_tmp("wu_b", [4, NF * 128, 8 * 128], BF16)
        self.wd_b = self.dram_tmp("wd_b", [4, 128, NF * D], BF16)
        self.wcast_b = S.bufs("wcast", 4)
        self.s5_are = self.dram_in("s5_are", [128, 32])
        self.s5_aim = self.dram_in("s5_aim", [128, 32])
        self.s5_ldt = self.dram_in("s5_ldt", [128, 32])
        self.s5_bre = self.dram_in("s5_bre", [128, 32 * 16])
        self.s5_bim = self.dram_in("s5_bim", [128, 32 * 16])
        self.s5_cre = self.dram_in("s5_cre", [128, 32 * 16])
        self.s5_cim = self.dram_in("s5_cim", [128, 32 * 16])
        self.s5_d = self.dram_in("s5_d", [128, 8])
        self.s5_h0 = self.dram_in("s5_h0", [128, 32 * 2 * 16])
        self.wglu_h = self.dram_in("wglu_h", [128, 8 * 2048])
        self.wglu_b = self.dram_tmp("wglu_b", [128, 8 * 2048], BF16)
        self.ssm_p = self.dram_out("ssm_p", [128, 32 * 2])
        self.ssm_s = self.dram_out("ssm_s", [128, 32 * 2 * 16])
        self.ssm_b = S.buf("ssm_out")
        self.uT_s = self.dram_tmp("uT_s", [8, 128, NTOK], BF16)
        self.us_b = S.bufs("uT_s", 8)
        self.win_h = self.dram_in("win_h", [128, 8 * 1056])
        self.win_b = self.dram_tmp("win_b", [128, 8 * 1056], BF16)
        self.wuq_h = self.dram_in("wuq_h", [128, 6 * 1536])
        self.wuq_b = self.dram_tmp("wuq_b", [128, 6 * 1536], BF16)
        self.wukv_h = self.dram_in("wukv_h", [128, 2 * 2048])
        self.wukv_b = self.dram_tmp("wukv_b", [128, 2 * 2048], BF16)
        self.wukT_h = self.dram_in("wukT_h", [64, 16 * 256])
        self.wukT_b = self.dram_tmp("wukT_b", [64, 16 * 256], BF16)
        self.wo_h = self.dram_in("wo_h", [128, 8 * 1024])
        self.wo_b = self.dram_tmp("wo_b", [128, 8 * 1024], BF16)
        self.qnorm = self.dram_in("qnorm", [1, QL])
        self.kvnorm = self.dram_in("kvnorm", [1, KVL])
        self.cache_cat = self.dram_in("cache_cat", [cfg.NPHYS * PAGE, KVL + RP])
        self.ptab = self.dram_in("ptab", [cfg.NS, cfg.NPG], I32)
        self.lat_out = self.dram_out("lat_out", [NTOK, KVL])
        self.kpe_out = self.dram_out("kpe_out", [NTOK, RP])
        self.mla_out_b = S.buf("mla_out")
        self.qT_s = self.dram_tmp("qT_s", [NH, DQK, TP], BF16)
        self.kT_s = self.dram_tmp("kT_s", [NH, DQK, TP], BF16)
        self.v_s = self.dram_tmp("v_s", [TP, NH * 66], BF16)
        self.qkv_s_b = S.buf("qkv_s")
        self.wcast2_b = S.buf("wcast2")

        self.ident = self.gsb("ident", [128, 128], BF16)
        self.ident_f = self.gsb("ident_f", [128, 128], F32)
        self.ident_b = S.buf("ident")
        self.eps_t = self.gsb("eps_t", [128, 1])
        self.eps_b = S.buf("eps")
        self.mask32 = self.gsb("mask32", [128, 128])
        self.mask32_b = S.buf("mask32")

        self.pes = ExitStack()
        with self.pes:
            self.pool("memset", dict(ap=self.ident_f[:], constant=0.0), writes=[self.ident_b])
            self.pool("affine_select", dict(out=self.ident_f[:], in_=self.ident_f[:], pattern=[[-1, 128]],
                                            compare_op=ALU.not_equal, fill=1.0, base=0, channel_multiplier=1),
                      writes=[self.ident_b])
            self.dve("tensor_copy", dict(out=self.ident[:], in_=self.ident_f[:]), writes=[self.ident_b])
            self.dve("memset", dict(ap=self.eps_t[:], constant=EPS), writes=[self.eps_b])
            self.pool("memset", dict(ap=self.mask32[:], constant=0.0), writes=[self.mask32_b])
            for k in range(4):
                self.pool("memset", dict(ap=self.mask32[32 * k:32 * k + 32, 32 * k:32 * k + 32], constant=1.0),
                          writes=[self.mask32_b])
            for i in range(4):
                for (o, s_) in ((self.wg_b, self.wg_h), (self.wu_b, self.wu_h), (self.wd_b, self.wd_h)):
                    S.dma("pool", o[i], s_[i], writes=[self.wcast_b[i]], owner=self.wcast_b[i])
            for (o, s_) in ((self.wglu_b, self.wglu_h), (self.win_b, self.win_h), (self.wuq_b, self.wuq_h),
                            (self.wukv_b, self.wukv_h), (self.wukT_b, self.wukT_h), (self.wo_b, self.wo_h)):
                S.dma("pool", o[:, :], s_[:, :], writes=[self.wcast2_b], owner=self.wcast2_b)
            S.end_phase()

        st = cfg.stages
        xsb = lambda ti: [self.xs_b[ti]]
        yb = lambda ti: [self.y_b[ti]]
        self.ffn(0, 0, self.x_in, lambda ti: [self.xin_b], self.xs, xsb)
        last = (st == "ffn0")
        if not last:
            self.s5(1)
            last = (st == "s5")
        if not last:
            self.ffn(1, 2, self.xs, xsb, self.xs, xsb)
            self.ffn(2, 3, self.xs, xsb, self.xs, xsb)
            last = (st == "ffn2")
        if not last:
            self.mla(4)
            last = (st == "mla")
        if not last:
            self.ffn(3, 5, self.xs, xsb, self.y_out, yb)
        else:
            self.copy_out()

    def copy_out(self):
        S = self.S
        self.pes = ExitStack()
        with self.pes:
            xt = self.sbn("xt", 4, [128, D])
            for ti in range(self.cfg.NT):
                t, b = xt.next()
                self.ld(t[:], self.xs[ti * 128:(ti + 1) * 128, :], [self.xs_b[ti]], [b], b)
                self.ld(self.y_out[ti * 128:(ti + 1) * 128, :], t[:], [b], [self.y_b[ti]], b, q="pool")
            S.end_phase()

    def common_alloc(self, nxt=8):
        self.xt = self.sbn("xt", nxt, [128, D])
        self.gpre = self.sb("gpre", [128, D])
        self.gpost = self.sb("gpost", [128, D])
        self.gpre_b = self.S.buf("gpre")
        self.gpost_b = self.S.buf("gpost")
        self.junk = self.sb("junk", [128, D], BF16)
        self.junk_b = self.S.buf("junk")
        self.small = self.sbn("small", 8, [128, 8])
        self.xn = self.sbn("xn", 2, [128, D], BF16)
        self.yt = self.sbn("yt", 2, [128, D])
        self.pT = self.psn("pT", 2, [128, 1024], BF16)

    def load_gamma(self, dst, dst_b, src_row):
        self.ld(dst[:], src_row.partition_broadcast(128), [], [dst_b], dst_b)

    def rstd_of(self, parts, ncols):
        sm, smb = self.small.next()
        off = 0
        for i, (ap, bufs) in enumerate(parts):
            w = ap.shape[-1]
            self.act(dict(out=self.junk[:, off:off + w], in_=ap, func=AF.Square, accum_out=sm[:, i:i + 1]),
                     reads=bufs, writes=[self.junk_b, smb])
            off += w
        col = len(parts)
        if len(parts) > 1:
            assert len(parts) == 2
            self.dve("tensor_tensor", dict(out=sm[:, 2:3], in0=sm[:, 0:1], in1=sm[:, 1:2], op=ALU.add), [smb], [smb])
            src = sm[:, 2:3]
            col = 3
        else:
            src = sm[:, 0:1]
        self.act(dict(out=sm[:, col:col + 1], in_=src, func=AF.Sqrt, scale=1.0 / ncols, bias=self.eps_t[:, 0:1]),
                 reads=[smb, self.eps_b], writes=[smb])
        self.dve("reciprocal", dict(out=sm[:, col + 1:col + 2], in_=sm[:, col:col + 1]), [smb], [smb])
        return sm[:, col + 1:col + 2], smb

    def transpose_into(self, src, src_b, nch, dstT, dstT_b, col0, eng="act"):
        p, pb = self.pT.next()
        for c in range(nch):
            self.tr(p[:, c * 128:(c + 1) * 128], src[:, c * 128:(c + 1) * 128], [src_b], [pb], inc=(c == nch - 1))
        kw = dict(out=dstT[:, 0:nch, col0:col0 + 128], in_=p[:, 0:nch * 128].rearrange("p (c n) -> p c n", n=128))
        if eng == "act":
            self.S.op("act", "copy", kw, [pb], [dstT_b])
        else:
            self.S.op("dve", "tensor_copy", kw, [pb], [dstT_b])

    def front(self, ti, src, src_bufs, xnT_t, xnT_tb, col0):
        xt, xb = self.xt.next()
        self.ld(xt[:], src[ti * 128:(ti + 1) * 128, :], src_bufs, [xb], xb)
        rstd, rb = self.rstd_of([(xt[:], [xb])], D)
        xn, xnb = self.xn.next()
        self.dve("scalar_tensor_tensor", dict(out=xn[:], in0=xt[:], scalar=rstd, in1=self.gpre[:],
                                              op0=ALU.mult, op1=ALU.mult), [xb, rb, self.gpre_b], [xnb])
        self.transpose_into(xn, xnb, 8, xnT_t, xnT_tb, col0)
        return xt, xb

    def post_residual(self, halves, xt, xb, dst, dst_bufs, ti, coef):
        rstd, rb = self.rstd_of([(h[0], [h[1]]) for h in halves], D)
        yt, yb = self.yt.next()
        for h in range(2):
            self.dve("scalar_tensor_tensor", dict(out=yt[:, h * 512:(h + 1) * 512], in0=halves[h][0], scalar=rstd,
                                                  in1=self.gpost[:, h * 512:(h + 1) * 512], op0=ALU.mult, op1=ALU.mult),
                     [halves[h][1], rb, self.gpost_b], [yb])
        self.dve("scalar_tensor_tensor", dict(out=yt[:], in0=yt[:], scalar=float(coef), in1=xt[:],
                                              op0=ALU.mult, op1=ALU.add), [yb, xb], [yb])
        self.ld(dst[ti * 128:(ti + 1) * 128, :], yt[:], [yb], dst_bufs, yb, q="pool")

    def ffn(self, fi, ni, src, src_bufs_of, dst, dst_bufs_of):
        S = self.S
        self.pes = ExitStack()
        with self.pes:
            self.common_alloc(8)
            xnT = self.sbn("xnT", 2, [128, 8, 512], BF16)
            wg = self.sbn("wg", 6, [128, 8 * 128], BF16)
            wu = self.sbn("wu", 6, [128, 8 * 128], BF16)
            wd = self.sbn("wd", 2, [128, 11 * D], BF16)
            sg = self.sbn("sg", 4, [128, 512])
            hT = self.sb("hT", [128, NF, 512], BF16)
            hT_b = S.buf("hT")
            pA = self.psn("pAll", 6, [128, 512])
            pB = pA
            pY = pA
            self.load_gamma(self.gpre, self.gpre_b, self.norm_pre[ni])
            self.load_gamma(self.gpost, self.gpost_b, self.norm_post[ni])
            wc = [self.wcast_b[fi]]
            def do_front(gi_):
                t0_, n_ = self.groups[gi_]
                xT_, xTb_ = xnT.next()
                return (xT_, xTb_, [self.front(t0_ + i, src, src_bufs_of(t0_ + i), xT_, xTb_, i * 128) for i in range(n_)])
            nxt = do_front(0)
            for gi, (t0, n) in enumerate(self.groups):
                G = n * 128
                xT, xTb, xts = nxt
                for f in range(NF):
                    wgt, wgb = wg.next()
                    wut, wub = wu.next()
                    self.ld(wgt[:], self.wg_b[fi, f * 128:(f + 1) * 128, :], wc, [wgb], wgb)
                    self.ld(wut[:], self.wu_b[fi, f * 128:(f + 1) * 128, :], wc, [wub], wub)
                    a, ab = pA.next()
                    b, bb = pB.next()
                    for kc in range(8):
                        self.mm(a[:, 0:G], wgt[:, kc * 128:(kc + 1) * 128], xT[:, kc, 0:G], kc == 0, kc == 7,
                                [wgb, xTb], [ab], kc == 7)
                    for kc in range(8):
                        self.mm(b[:, 0:G], wut[:, kc * 128:(kc + 1) * 128], xT[:, kc, 0:G], kc == 0, kc == 7,
                                [wub, xTb], [bb], kc == 7)
                    s, sb_ = sg.next()
                    self.act(dict(out=s[:, 0:G], in_=a[:, 0:G], func=AF.Silu), [ab], [sb_])
                    self.dve("tensor_tensor", dict(out=hT[:, f, 0:G], in0=s[:, 0:G], in1=b[:, 0:G], op=ALU.mult),
                             [sb_, bb], [hT_b])
                has_next = gi + 1 < len(self.groups)
                if has_next:
                    nt0, nn = self.groups[gi + 1]
                    nxT, nxTb = xnT.next()
                    nxts = []
                wds = []
                for hf in range(2):
                    wdt, wdb = wd.next()
                    self.ld(wdt[:], self.wd_b[fi, :, hf * 11 * D:(hf + 1) * 11 * D], wc, [wdb], wdb)
                    wds.append((wdt, wdb))
                for i in range(n):
                    ys = []
                    for h in range(2):
                        y, ybuf = pY.next()
                        for f in range(NF):
                            hf, ff = divmod(f, 11)
                            self.mm(y[:], hT[:, f, i * 128:(i + 1) * 128],
                                    wds[hf][0][:, ff * D + h * 512: ff * D + (h + 1) * 512],
                                    f == 0, f == NF - 1, [hT_b, wds[hf][1]], [ybuf], f == NF - 1)
                        ys.append((y[:], ybuf))
                    if has_next and i < nn:
                        nxts.append(self.front(nt0 + i, src, src_bufs_of(nt0 + i), nxT, nxTb, i * 128))
                    self.post_residual(ys, xts[i][0], xts[i][1], dst, dst_bufs_of(t0 + i), t0 + i, 0.5)
                if has_next:
                    assert nn <= n
                    nxt = (nxT, nxTb, nxts)
            S.end_phase()

    def bc(self, ap, axis, shape):
        return ap.unsqueeze(axis).to_broadcast(list(shape))

    def sincos(self, th, thb, n, want):
        out = {}
        for name in want:
            shift = 0.0 if name == "sin" else math.pi / 2
            t = self.sb("sc_t", [128, n])
            ti = self.sb("sc_i", [128, n], I32)
            b = self.S.buf("sc")
            self.dve("tensor_scalar", dict(out=t[:], in0=th, scalar1=shift, scalar2=1.0 / TWO_PI, op0=ALU.add,
                                           op1=ALU.mult), [thb], [b])
            self.dve("tensor_copy", dict(out=ti[:], in_=t[:]), [b], [b])
            self.dve("tensor_copy", dict(out=t[:], in_=ti[:]), [b], [b])
            r = self.sb("sc_r", [128, n])
            self.dve("scalar_tensor_tensor", dict(out=r[:], in0=t[:], scalar=-C1_2PI, in1=th, op0=ALU.mult,
                                                  op1=ALU.add), [b, thb], [b])
            self.dve("scalar_tensor_tensor", dict(out=r[:], in0=t[:], scalar=-C2_2PI, in1=r[:], op0=ALU.mult,
                                                  op1=ALU.add), [b], [b])
            if shift != 0.0:
                self.dve("tensor_scalar", dict(out=r[:], in0=r[:], scalar1=shift, scalar2=None, op0=ALU.add), [b], [b])
            self.dve("tensor_scalar", dict(out=r[:], in0=r[:], scalar1=math.pi, scalar2=-math.pi, op0=ALU.min,
                                           op1=ALU.max), [b], [b])
            o = self.sb("sc_o", [128, n])
            self.act(dict(out=o[:], in_=r[:], func=AF.Sin), [b], [b])
            out[name] = (o, b)
        return out

    def cmul(self, o_re, o_im, a_re, a_im, b_re, b_im, reads, writes, tmp, neg_im=False):
        t1, t2 = tmp
        tt = lambda o, x, y, op: self.dve("tensor_tensor", dict(out=o, in0=x, in1=y, op=op), reads, writes)
        tt(t1, a_re, b_re, ALU.mult)
        tt(t2, a_im, b_im, ALU.mult)
        tt(o_re, t1, t2, ALU.subtract)
        tt(t1, a_re, b_im, ALU.mult)
        tt(t2, a_im, b_re, ALU.mult)
        tt(o_im, t1, t2, ALU.add)
        if neg_im:
            self.dve("tensor_scalar", dict(out=o_im, in0=o_im, scalar1=-1.0, scalar2=None, op0=ALU.mult), reads, writes)

    def s5(self, ni):
        S = self.S
        cfg = self.cfg
        NB, NBT, NTOK, TP, NS = cfg.NB, cfg.NBT, cfg.NTOK, cfg.TP, cfg.NS
        LOGNB = int(round(math.log2(NB)))
        assert 2 ** LOGNB == NB
        self.pes = ExitStack()
        with self.pes:
            self.common_alloc(4)
            self.load_gamma(self.gpre, self.gpre_b, self.norm_pre[ni])
            xnT = self.sbn("xnT", 2, [128, 8, 512], BF16)
            for gi, (t0, n) in enumerate(self.groups):
                G_ = n * 128
                xT, xTb = xnT.next()
                for i in range(n):
                    self.front(t0 + i, self.xs, [self.xs_b[t0 + i]], xT, xTb, i * 128)
                self.ld(self.uT_s[:, :, t0 * 128:t0 * 128 + G_].rearrange("c p t -> p c t"), xT[:, :, 0:G_], [xTb],
                        [], xTb, q="pool", shared=self.us_b)
            S.end_phase()
        self.pes = ExitStack()
        with self.pes:
            self.small = self.sbn("small", 8, [128, 8])
            self.pT = self.psn("pT", 2, [128, 1024], BF16)
            ucs = self.sbn("uc", 2, [128, NTOK], BF16)
            bk = self.psn("bk", 6, [128, 512])
            gb = S.buf("s5gen")
            G = [gb]
            are = self.sb("are", [128, 32]); aim = self.sb("aim", [128, 32]); ldt = self.sb("ldt", [128, 32])
            bre = self.sb("bre", [128, 32, 16]); bim = self.sb("bim", [128, 32, 16])
            cre = self.sb("cre", [128, 32, 16]); cim = self.sb("cim", [128, 32, 16])
            dcol = self.sb("dcol", [128, 8]); h0 = self.sb("h0", [128, 32, 2, NS])
            lb = S.bufs("s5ld", 9)
            for k, (t, src) in enumerate(((are, self.s5_are), (aim, self.s5_aim), (ldt, self.s5_ldt), (dcol, self.s5_d))):
                self.ld(t[:], src[:, :], [], [lb[k]], lb[k])
            for k, (t, src) in enumerate(((bre, self.s5_bre), (bim, self.s5_bim), (cre, self.s5_cre), (cim, self.s5_cim))):
                self.ld(t[:].rearrange("p a b -> p (a b)"), src[:, :], [], [lb[4 + k]], lb[4 + k])
            self.ld(h0[:].rearrange("p a b c -> p (a b c)"), self.s5_h0[:, :], [], [lb[8]], lb[8])
            LB = list(lb)
            dt = self.sb("dt", [128, 32]); trd = self.sb("trd", [128, 32]); th = self.sb("th", [128, 32])
            mag = self.sb("mag", [128, 32]); rho = self.sb("rho", [128, 32])
            self.act(dict(out=dt[:], in_=ldt[:], func=AF.Exp), LB, G)
            self.dve("tensor_tensor", dict(out=trd[:], in0=are[:], in1=dt[:], op=ALU.mult), LB + G, G)
            self.dve("tensor_tensor", dict(out=th[:], in0=aim[:], in1=dt[:], op=ALU.mult), LB + G, G)
            self.act(dict(out=mag[:], in_=trd[:], func=AF.Exp), G, G)
            self.act(dict(out=rho[:], in_=trd[:], func=AF.Exp, scale=8.0), G, G)
            sc = self.sincos(th[:], gb, 32, ("sin", "cos"))
            LRI = self.sb("LRI", [128, 9, 2, 32])
            nLI = self.sb("nLI", [128, 9, 32])
            t1 = self.sb("t1", [128, 32]); t2 = self.sb("t2", [128, 32])
            self.dve("memset", dict(ap=LRI[:, 0, 0, :], constant=1.0), [], G)
            self.dve("memset", dict(ap=LRI[:, 0, 1, :], constant=0.0), [], G)
            self.dve("tensor_tensor", dict(out=LRI[:, 1, 0, :], in0=mag[:], in1=sc["cos"][0][:], op=ALU.mult),
                     G + [sc["cos"][1]], G)
            self.dve("tensor_tensor", dict(out=LRI[:, 1, 1, :], in0=mag[:], in1=sc["sin"][0][:], op=ALU.mult),
                     G + [sc["sin"][1]], G)
            for tau in range(2, 9):
                self.cmul(LRI[:, tau, 0, :], LRI[:, tau, 1, :], LRI[:, tau - 1, 0, :], LRI[:, tau - 1, 1, :],
                          LRI[:, 1, 0, :], LRI[:, 1, 1, :], G, G, (t1[:], t2[:]))
            self.dve("tensor_scalar", dict(out=nLI[:], in0=LRI[:, :, 1, :], scalar1=-1.0, scalar2=None, op0=ALU.mult), G, G)
            gre = self.sb("gre", [128, 32]); gim = self.sb("gim", [128, 32]); nr = self.sb("nr", [128, 32])
            den = self.sb("den", [128, 32])
            tt = lambda o, x, y, op: self.dve("tensor_tensor", dict(out=o, in0=x, in1=y, op=op), LB + G, G)
            self.dve("tensor_scalar", dict(out=nr[:], in0=LRI[:, 1, 0, :], scalar1=-1.0, scalar2=None, op0=ALU.add), G, G)
            tt(t1[:], are[:], are[:], ALU.mult)
            tt(t2[:], aim[:], aim[:], ALU.mult)
            tt(den[:], t1[:], t2[:], ALU.add)
            self.dve("reciprocal", dict(out=den[:], in_=den[:]), G, G)
            tt(t1[:], nr[:], are[:], ALU.mult)
            tt(t2[:], LRI[:, 1, 1, :], aim[:], ALU.mult)
            tt(gre[:], t1[:], t2[:], ALU.add)
            tt(gre[:], gre[:], den[:], ALU.mult)
            tt(t1[:], LRI[:, 1, 1, :], are[:], ALU.mult)
            tt(t2[:], nr[:], aim[:], ALU.mult)
            tt(gim[:], t1[:], t2[:], ALU.subtract)
            tt(gim[:], gim[:], den[:], ALU.mult)
            Pk = self.sb("Pk", [128, LOGNB + 1, 2, 32])
            self.dve("reciprocal", dict(out=t1[:], in_=rho[:]), G, G)
            tt(Pk[:, 0, 0, :], LRI[:, 8, 0, :], t1[:], ALU.mult)
            tt(Pk[:, 0, 1, :], LRI[:, 8, 1, :], t1[:], ALU.mult)
            for k in range(LOGNB):
                self.cmul(Pk[:, k + 1, 0, :], Pk[:, k + 1, 1, :], Pk[:, k, 0, :], Pk[:, k, 1, :],
                          Pk[:, k, 0, :], Pk[:, k, 1, :], G, G, (t1[:], t2[:]))
            Bre = self.sb("Bre", [128, 32, 16]); Bim = self.sb("Bim", [128, 32, 16])
            T1 = self.sb("T1", [128, 32, 16]); T2 = self.sb("T2", [128, 32, 16])
            gre_b = self.bc(gre[:], 2, [128, 32, 16]); gim_b = self.bc(gim[:], 2, [128, 32, 16])
            self.cmul(Bre[:], Bim[:], bre[:], bim[:], gre_b, gim_b, LB + G, G, (T1[:], T2[:]))
            Wre = self.sb("Wre", [128, 8, 4, 16]); Wim = self.sb("Wim", [128, 8, 4, 16])
            Wt1 = self.sb("Wt1", [128, 9, 4, 16]); Wt2 = self.sb("Wt2", [128, 9, 4, 16])
            VVre = self.sb("VVre", [128, 9, 4, 16]); VVim = self.sb("VVim", [128, 9, 4, 16])
            Wexp = self.sb("Wexp", [128, 2, 8, 4, 2, 16], BF16)
            VVexp = self.sb("VVexp", [128, 2, 9, 4, 2, 16], BF16)
            Bexp = self.sb("Bexp", [128, 2, 4, 2, 16], BF16)
            WTz = self.sb("WTz", [128, 4, 2, 8, 128], BF16)
            VVz = self.sb("VVz", [128, 8, 4, 2, 128], BF16)
            BD = self.sb("BD", [128, 8, 128], BF16)
            Ect = self.sb("Ec", [128, 4, NB]); Est = self.sb("Es", [128, 4, NB])
            w_sb = self.sb("w_sb", [128, 4, 2, NBT])
            v_sb = self.sb("v_sb", [128, 4, 2, NB])
            g_sb = self.sb("g_sb", [128, 4, 2, NB])
            zbf = self.sb("zbf", [128, 4, 2, NBT], BF16)
            zs = self.sb("zs", [128, 4, 2, NS])
            zfin = self.sb("zfin", [128, 8, 4, 2])
            E1 = self.sb("E1", [128, 4, NB]); E2 = self.sb("E2", [128, 4, NB])
            ytmp = self.sbn("ytmp", 2, [128, 512])
            cb = S.buf("s5chunk")
            C = [cb]
            for t in (Wexp, VVexp, Bexp, WTz, VVz):
                self.pool("memset", dict(ap=t[:], constant=0.0), [], C)
            self.dve("memset", dict(ap=zbf[:, :, :, 0:1], constant=0.0), [], C)
            for c in range(8):
                ps = slice(4 * c, 4 * c + 4)
                RD = LB + G + C
                uc, ucb = ucs.next()
                self.ld(uc[:], self.uT_s[c], [self.us_b[c]], [ucb], ucb)
                lr8 = self.bc(LRI[:, 0:8, 0, ps], 3, [128, 8, 4, 16]); li8 = self.bc(LRI[:, 0:8, 1, ps], 3, [128, 8, 4, 16])
                Bre8 = self.bc(Bre[:, ps, :], 1, [128, 8, 4, 16]); Bim8 = self.bc(Bim[:, ps, :], 1, [128, 8, 4, 16])
                self.cmul(Wre[:], Wim[:], lr8, li8, Bre8, Bim8, RD, C, (Wt1[:, 0:8], Wt2[:, 0:8]))
                lr9 = self.bc(LRI[:, :, 0, ps], 3, [128, 9, 4, 16]); li9 = self.bc(LRI[:, :, 1, ps], 3, [128, 9, 4, 16])
                cre9 = self.bc(cre[:, ps, :], 1, [128, 9, 4, 16]); cim9 = self.bc(cim[:, ps, :], 1, [128, 9, 4, 16])
                self.cmul(VVre[:], VVim[:], cre9, cim9, lr9, li9, RD, C, (Wt1[:], Wt2[:]), neg_im=True)
                for g2, pr in ((0, slice(0, 64)), (1, slice(64, 128))):
                    for ri, (wsrc, vsrc, bsrc) in enumerate(((Wre, VVre, Bre), (Wim, VVim, Bim))):
                        self.dve("tensor_copy", dict(out=Wexp[pr, ri, :, :, g2, :], in_=wsrc[pr]), RD, C)
                        self.dve("tensor_copy", dict(out=VVexp[pr, ri, :, :, g2, :], in_=vsrc[pr]), RD, C)
                        self.dve("tensor_copy", dict(out=Bexp[pr, ri, :, g2, :], in_=bsrc[pr, ps, :]), RD, C)
                for ri in range(2):
                    p, pb = self.pT.next()
                    for n in range(8):
                        self.tr(p[:, n * 128:(n + 1) * 128], Wexp[:, ri, n].rearrange("p a b c -> p (a b c)"), C, [pb],
                                inc=(n == 7))
                    for p4 in range(4):
                        self.S.op("act", "copy", dict(out=WTz[32 * p4:32 * p4 + 32, p4, ri, :, :],
                                                      in_=p[32 * p4:32 * p4 + 32, :].rearrange("p (n q) -> p n q", q=128)),
                                  [pb], C)
                for p4 in range(4):
                    for ri in range(2):
                        self.dve("tensor_copy", dict(out=VVz[:, :, p4, ri, 32 * p4:32 * p4 + 32],
                                                     in_=VVexp[:, ri, 1:9, p4].rearrange("p t a b -> p t (a b)")), C, C)
                for half in range(2):
                    bank, bb = bk.next()
                    first = True
                    for t4 in range(4):
                        tau = half * 4 + t4
                        for ri in range(2):
                            self.mm(bank[:, t4 * 128:(t4 + 1) * 128], Bexp[:, ri].rearrange("p a b c -> p (a b c)"),
                                    VVexp[:, ri, tau].rearrange("p a b c -> p (a b c)"), ri == 0, ri == 1,
                                    C, [bb], (t4 == 3 and ri == 1))
                    self.dve("tensor_tensor", dict(out=BD[:, half * 4:half * 4 + 4, :],
                                                   in0=bank[:, :].rearrange("p (t n) -> p t n", n=128),
                                                   in1=self.bc(self.mask32[:], 1, [128, 4, 128]), op=ALU.mult),
                             [bb, self.mask32_b], C)
                self.dve("memset", dict(ap=Ect[:, :, 0:1], constant=1.0), [], C)
                self.dve("memset", dict(ap=Est[:, :, 0:1], constant=0.0), [], C)
                for k in range(LOGNB):
                    n = 2 ** k
                    pc = self.bc(Pk[:, k, 0, ps], 2, [128, 4, n]); psn_ = self.bc(Pk[:, k, 1, ps], 2, [128, 4, n])
                    self.cmul(Ect[:, :, n:2 * n], Est[:, :, n:2 * n], Ect[:, :, 0:n], Est[:, :, 0:n], pc, psn_, G + C, C,
                              (E1[:, :, 0:n], E2[:, :, 0:n]))
                for gi, (t0, n) in enumerate(self.groups):
                    c0 = t0 * 128
                    ntok = n * 128
                    nb = ntok // 8
                    gb0 = c0 // 8
                    bank, bb = bk.next()
                    first = True
                    for p4 in range(4):
                        for ri in range(2):
                            for s_ in range(8):
                                self.mm(bank[:, (p4 * 2 + ri) * 64:(p4 * 2 + ri) * 64 + nb], WTz[:, p4, ri, 7 - s_, :],
                                        uc[:, c0 + s_:c0 + ntok:8], s_ == 0, s_ == 7, C + [ucb], [bb],
                                        (p4 == 3 and ri == 1 and s_ == 7))
                    self.S.op("act", "copy", dict(out=w_sb[:, :, :, gb0:gb0 + nb],
                                                  in_=bank[:, :].rearrange("p (a b n) -> p a b n", a=4, b=2)[:, :, :, 0:nb]),
                              [bb], C)
                wre_ = w_sb[:, :, 0, 0:NB]; wim_ = w_sb[:, :, 1, 0:NB]
                tt2 = lambda o, x, y, op: self.dve("tensor_tensor", dict(out=o, in0=x, in1=y, op=op), C, C)
                tt2(E1[:], Ect[:], wre_, ALU.mult); tt2(E2[:], Est[:], wim_, ALU.mult)
                tt2(v_sb[:, :, 0, :], E1[:], E2[:], ALU.add)
                tt2(E1[:], Ect[:], wim_, ALU.mult); tt2(E2[:], Est[:], wre_, ALU.mult)
                tt2(v_sb[:, :, 1, :], E1[:], E2[:], ALU.subtract)
                for p4 in range(4):
                    for ri in range(2):
                        self.dve("tensor_tensor_scan", dict(out=g_sb[:, p4, ri, :],
                                                            data0=rho[:, 4 * c + p4:4 * c + p4 + 1].to_broadcast([128, NB]),
                                                            data1=v_sb[:, p4, ri, :], initial=0.0, op0=ALU.mult, op1=ALU.add),
                                 G + C, C)
                tt2(E1[:], Ect[:], g_sb[:, :, 0, :], ALU.mult); tt2(E2[:], Est[:], g_sb[:, :, 1, :], ALU.mult)
                tt2(zbf[:, :, 0, 1:NB], E1[:, :, 0:NB - 1], E2[:, :, 0:NB - 1], ALU.subtract)
                tt2(zfin[:, c, :, 0], E1[:, :, NB - 1], E2[:, :, NB - 1], ALU.subtract)
                tt2(E1[:], Ect[:], g_sb[:, :, 1, :], ALU.mult); tt2(E2[:], Est[:], g_sb[:, :, 0, :], ALU.mult)
                tt2(zbf[:, :, 1, 1:NB], E1[:, :, 0:NB - 1], E2[:, :, 0:NB - 1], ALU.add)
                tt2(zfin[:, c, :, 1], E1[:, :, NB - 1], E2[:, :, NB - 1], ALU.add)
                self.dve("tensor_copy", dict(out=zbf[:, :, :, NB:NBT], in_=h0[:, ps, :, :]), LB + C, C)
                self.ld(self.ssm_p[:, 8 * c:8 * c + 8], zfin[:, c].rearrange("p a b -> p (a b)"), C, [], cb, q="pool", shared=[self.ssm_b])
                l8r = self.bc(LRI[:, 8, 0, ps], 2, [128, 4, NS]); l8i = self.bc(LRI[:, 8, 1, ps], 2, [128, 4, NS])
                self.cmul(zs[:, :, 0, :], zs[:, :, 1, :], h0[:, ps, 0, :], h0[:, ps, 1, :], l8r, l8i, LB + G + C, C,
                          (E1[:, :, 0:NS], E2[:, :, 0:NS]))
                tt2(zs[:], zs[:], w_sb[:, :, :, NB:NBT], ALU.add)
                self.ld(self.ssm_s[:, 4 * c * 2 * NS:(4 * c + 4) * 2 * NS], zs[:].rearrange("p a b c -> p (a b c)"), C,
                        [], cb, q="pool", shared=[self.ssm_b])
                for gi, (t0, n) in enumerate(self.groups):
                    c0 = t0 * 128
                    ntok = n * 128
                    nb = ntok // 8
                    gb0 = (c0 // 8) if c0 < TP else NB
                    bank, bb = bk.next()
                    first = True
                    for r in range(8):
                        for tau in range(r + 1):
                            self.mm(bank[:, r * 64:r * 64 + nb], BD[:, tau, :], uc[:, c0 + r - tau:c0 + ntok:8],
                                    tau == 0, False, C + [ucb], [bb], False)
                        for p4 in range(4):
                            for ri in range(2):
                                last = (r == 7 and p4 == 3 and ri == 1)
                                self.mm(bank[:, r * 64:r * 64 + nb], VVz[:, r, p4, ri, :], zbf[:, p4, ri, gb0:gb0 + nb],
                                        False, (p4 == 3 and ri == 1), C, [bb], last)
                    yt_, ytb = ytmp.next()
                    self.dve("scalar_tensor_tensor", dict(
                        out=yt_[:, 0:ntok].rearrange("p (b r) -> p b r", r=8),
                        in0=uc[:, c0:c0 + ntok].rearrange("p (b r) -> p b r", r=8), scalar=dcol[:, c:c + 1],
                        in1=bank[:, :].rearrange("p (r b) -> p b r", r=8)[:, 0:nb, :], op0=ALU.mult, op1=ALU.add),
                        [bb, ucb] + LB, [ytb])
                    self.act(dict(out=uc[:, c0:c0 + ntok], in_=yt_[:, 0:ntok], func=AF.Gelu_apprx_tanh), [ytb],
                             [ucb])
                self.ld(self.uT_s[c], uc[:], [ucb], [self.us_b[c]], ucb, q="pool")
            S.end_phase()
        self.pes = ExitStack()
        with self.pes:
            self.common_alloc(4)
            self.load_gamma(self.gpost, self.gpost_b, self.norm_post[ni])
            wglu = self.sb("wglu", [128, 8, 2048], BF16)
            wglu_b = S.buf("wglu")
            self.ld(wglu[:].rearrange("p a b -> p (a b)"), self.wglu_b[:, :], [self.wcast2_b], [wglu_b], wglu_b)
            bk = self.psn("bk", 6, [128, 512])
            hTs = self.sbn("hTt", 3, [128, 8, 128], BF16)
            sgl = self.sbn("sgl", 2, [128, 512])
            mo = self.sbn("mo", 2, [128, D])
            for gi, (t0, n) in enumerate(self.groups):
                for i in range(n):
                    ti = t0 + i
                    xt, xb = self.xt.next()
                    self.ld(xt[:], self.xs[ti * 128:(ti + 1) * 128, :], [self.xs_b[ti]], [xb], xb)
                    hT, hTb = hTs.next()
                    self.ld(hT[:], self.uT_s[:, :, ti * 128:(ti + 1) * 128].rearrange("c p t -> p c t"), self.us_b, [hTb], hTb)
                    m, mb = mo.next()
                    for h in range(2):
                        zv, zvb = bk.next()
                        zg, zgb = bk.next()
                        for (bank, bb, col) in ((zv, zvb, h * 512), (zg, zgb, 1024 + h * 512)):
                            for kc in range(8):
                                self.mm(bank[:], hT[:, kc, :], wglu[:, kc, col:col + 512], kc == 0,
                                        kc == 7, [hTb, wglu_b], [bb], kc == 7)
                        sg, sgb = sgl.next()
                        self.act(dict(out=sg[:], in_=zg[:], func=AF.Sigmoid), [zgb], [sgb])
                        self.dve("tensor_tensor", dict(out=m[:, h * 512:(h + 1) * 512], in0=sg[:], in1=zv[:], op=ALU.mult),
                                 [sgb, zvb], [mb])
                    self.post_residual([(m[:, 0:512], mb), (m[:, 512:1024], mb)], xt, xb, self.xs, [self.xs_b[ti]], ti, 1.0)
            S.end_phase()

    def mla(self, ni):
        S = self.S
        cfg = self.cfg
        NT, NTOK, TP, NS, NPG = cfg.NT, cfg.NTOK, cfg.TP, cfg.NS, cfg.NPG
        NTP = TP // 128
        QTs = self.gsb("QTs", [128, NH, 128], BF16)
        ckvn_s = self.gsb("ckvn_s", [128, 289], BF16)
        ckvnT_s = self.gsb("ckvnT_s", [128, 2, 128], BF16)
        kpeT_s = self.gsb("kpeT_s", [32, 128], BF16)
        smp_b = S.buf("smp")
        SM = [smp_b]
        self.pes = ExitStack()
        with self.pes:
            self.common_alloc(4)
            self.load_gamma(self.gpre, self.gpre_b, self.norm_pre[ni])
            xnT = self.sbn("xnT", 2, [128, 8, 512], BF16)
            win = self.sb("win", [128, 8, 1056], BF16)
            wuq = self.sb("wuq", [128, 6, 1536], BF16)
            wukv = self.sb("wukv", [128, 2, 2048], BF16)
            wb = S.bufs("mlaw", 3)
            for k, (t, src) in enumerate(((win, self.win_b), (wuq, self.wuq_b), (wukv, self.wukv_b))):
                self.ld(t[:].rearrange("p a b -> p (a b)"), src[:, :], [self.wcast2_b], [wb[k]], wb[k])
            qn_bc = self.sb("qn_bc", [128, QL]); kvn_bc = self.sb("kvn_bc", [128, KVL])
            nb_ = S.bufs("nrm", 2)
            self.ld(qn_bc[:], self.qnorm[0].partition_broadcast(128), [], [nb_[0]], nb_[0])
            self.ld(kvn_bc[:], self.kvnorm[0].partition_broadcast(128), [], [nb_[1]], nb_[1])
            tb = S.buf("ropetab")
            TB = [tb]
            posf = self.sb("posf", [128, NT]); pi_ = self.sb("pi_", [128, 1], I32); invf = self.sb("invf", [128, 16])
            ang = self.sb("ang", [128, NT, 16])
            self.pool("iota", dict(out=posf[:, 0:NTP], pattern=[[128, NTP]], base=0, channel_multiplier=1,
                                   allow_small_or_imprecise_dtypes=True), [], TB)
            self.pool("iota", dict(out=pi_[:], pattern=[[0, 1]], base=0, channel_multiplier=1), [], TB)
            if "and" not in os.environ.get("M1_SKIP", ""):
                self.dve("tensor_single_scalar", dict(out=pi_[:], in_=pi_[:], scalar=7, op=ALU.bitwise_and), TB, TB)
            self.dve("tensor_copy", dict(out=posf[:, NTP:NT], in_=pi_[:]), TB, TB)
            self.dve("tensor_scalar", dict(out=posf[:, NTP:NT], in0=posf[:, NTP:NT], scalar1=float(cfg.PAST), scalar2=None,
                                           op0=ALU.add), TB, TB)
            self.pool("iota", dict(out=invf[:], pattern=[[1, 16]], base=0, channel_multiplier=0,
                                   allow_small_or_imprecise_dtypes=True), [], TB)
            self.act(dict(out=invf[:], in_=invf[:], func=AF.Exp, scale=-math.log(ROPE_THETA) / 16.0), TB, TB)
            self.dve("tensor_tensor", dict(out=ang[:], in0=self.bc(posf[:], 2, [128, NT, 16]),
                                           in1=self.bc(invf[:], 1, [128, NT, 16]), op=ALU.mult), TB, TB)
            sc = self.sincos(ang[:].rearrange("p a b -> p (a b)"), tb, NT * 16, ("sin", "cos"))
            cosT = sc["cos"][0][:].rearrange("p (a b) -> p a b", b=16)
            sinT = sc["sin"][0][:].rearrange("p (a b) -> p a b", b=16)
            RT = [sc["cos"][1], sc["sin"][1]]
            bk = self.psn("bk", 6, [128, 512])
            cqn = self.sbn("cqn", 2, [128, QL], BF16)
            ckf = self.sbn("ckf", 2, [128, KVL])
            ckb = self.sbn("ckb", 2, [128, KVL], BF16)
            kpf = self.sbn("kpf", 2, [128, RP])
            rt = self.sbn("rt", 2, [128, 2, 16, 16])
            cqnT = self.sbn("cqnT", 2, [128, 6, 128], BF16)
            ckvT = self.sbn("ckvT", 2, [128, 2, 128], BF16)
            qsb = self.sbn("qsb", 2, [128, NH, DQK], BF16)
            ksb = self.sbn("ksb", 2, [128, NH, DQK], BF16)
            vsb = self.sbn("vsb", 2, [128, NH, 66], BF16)
            for k in range(2):
                self.pool("memset", dict(ap=vsb.t[k][:, :, 64:66], constant=1.0), [], [vsb.b[k]])
            self.pool("memset", dict(ap=ckvn_s[:, 256:288], constant=0.0), [], SM)
            self.pool("memset", dict(ap=ckvn_s[:, 288:289], constant=1.0), [], SM)
            QTst = self.sbn("QTst", 1, [128, NH, 512], BF16)
            KTst = self.sbn("KTst", 1, [128, NH, 512], BF16)
            STOP = os.environ.get("M1_STOP", "")
            if os.environ.get("KDEBUG"):
                print("M1 sbuf remaining", self.nc.sbuf_bytes_remaining)
            for gi, (t0, n) in enumerate(self.groups if STOP != "A" else []):
                G = n * 128
                c0 = t0 * 128
                is_s = (c0 >= TP)
                xT, xTb = xnT.next()
                for i in range(n):
                    self.front(t0 + i, self.xs, [self.xs_b[t0 + i]], xT, xTb, i * 128)
                qst, qstb = QTst.next()
                kst, kstb = KTst.next()
                for i in range(n):
                    ti = t0 + i
                    rows = slice(ti * 128, (ti + 1) * 128)
                    pj = [bk.next() for _ in range(3)]
                    for (bank, bb), (col, ncol) in zip(pj, ((0, 512), (512, 512), (1024, 32))):
                        for kc in range(8):
                            self.mm(bank[:, 0:ncol], xT[:, kc, i * 128:(i + 1) * 128], win[:, kc, col:col + ncol],
                                    kc == 0, kc == 7, [xTb, wb[0]], [bb], kc == 7)
                    (p0, p0b), (p1, p1b), (p2, p2b) = pj
                    rq, rqb = self.rstd_of([(p0[:, 0:512], [p0b]), (p1[:, 0:256], [p1b])], QL)
                    cq, cqb = cqn.next()
                    self.dve("scalar_tensor_tensor", dict(out=cq[:, 0:512], in0=p0[:, 0:512], scalar=rq, in1=qn_bc[:, 0:512],
                                                          op0=ALU.mult, op1=ALU.mult), [p0b, rqb, nb_[0]], [cqb])
                    self.dve("scalar_tensor_tensor", dict(out=cq[:, 512:768], in0=p1[:, 0:256], scalar=rq,
                                                          in1=qn_bc[:, 512:768], op0=ALU.mult, op1=ALU.mult),
                             [p1b, rqb, nb_[0]], [cqb])
                    rk, rkb = self.rstd_of([(p1[:, 256:512], [p1b])], KVL)
                    cf, cfb = ckf.next()
                    self.dve("scalar_tensor_tensor", dict(out=cf[:], in0=p1[:, 256:512], scalar=rk, in1=kvn_bc[:],
                                                          op0=ALU.mult, op1=ALU.mult), [p1b, rkb, nb_[1]], [cfb])
                    self.ld(self.lat_out[rows, :], cf[:], [cfb], [], cfb, q="pool", shared=[self.mla_out_b])
                    cb_, cbb = ckb.next()
                    self.S.op("act", "copy", dict(out=cb_[:], in_=cf[:]), [cfb], [cbb])
                    if is_s:
                        self.S.op("act", "copy", dict(out=ckvn_s[:, 0:256], in_=cf[:]), [cfb], SM)
                    kp, kpb = kpf.next()
                    r_, rb_ = rt.next()
                    cs = cosT[:, ti, :]
                    sn = sinT[:, ti, :]
                    t1 = r_[:, 0, 0, :]; t2 = r_[:, 1, 0, :]
                    tt = lambda o, x, y, op: self.dve("tensor_tensor", dict(out=o, in0=x, in1=y, op=op),
                                                      [p2b, rb_] + RT, [rb_, kpb])
                    tt(t1, p2[:, 0:16], cs, ALU.mult); tt(t2, p2[:, 16:32], sn, ALU.mult)
                    tt(kp[:, 0:16], t1, t2, ALU.subtract)
                    tt(t1, p2[:, 0:16], sn, ALU.mult); tt(t2, p2[:, 16:32], cs, ALU.mult)
                    tt(kp[:, 16:32], t1, t2, ALU.add)
                    self.ld(self.kpe_out[rows, :], kp[:], [kpb], [], kpb, q="pool", shared=[self.mla_out_b])
                    if STOP == "B":
                        continue
                    cqT, cqTb = cqnT.next()
                    self.transpose_into(cq, cqb, 6, cqT, cqTb, 0, eng="dve")
                    ckT, ckTb = ckvT.next()
                    self.transpose_into(cb_, cbb, 2, ckT, ckTb, 0, eng="dve")
                    if is_s:
                        self.dve("tensor_copy", dict(out=ckvnT_s[:], in_=ckT[:]), [ckTb], SM)
                    qb_ = [bk.next() for _ in range(3)]
                    for nbk, (bank, bb) in enumerate(qb_):
                        for kc in range(6):
                            self.mm(bank[:], cqT[:, kc, :], wuq[:, kc, nbk * 512:(nbk + 1) * 512], kc == 0, kc == 5,
                                    [cqTb, wb[1]], [bb], kc == 5)
                    q_, q_b = qsb.next()
                    for hh in range(2):
                        self.S.op("act", "copy", dict(out=q_[:, hh * 8:(hh + 1) * 8, 0:64],
                                                      in_=qb_[hh][0][:, :].rearrange("p (h d) -> p h d", d=64)),
                                  [qb_[hh][1]], [q_b])
                    qr = qb_[2][0][:, :].rearrange("p (h r) -> p h r", r=32)
                    qrb = qb_[2][1]
                    csb = self.bc(cs, 1, [128, 16, 16]); snb = self.bc(sn, 1, [128, 16, 16])
                    T1 = r_[:, 0]; T2 = r_[:, 1]
                    tq = lambda o, x, y, op: self.dve("tensor_tensor", dict(out=o, in0=x, in1=y, op=op),
                                                      [qrb, rb_] + RT, [rb_, q_b])
                    tq(T1, qr[:, :, 0:16], csb, ALU.mult); tq(T2, qr[:, :, 16:32], snb, ALU.mult)
                    tq(q_[:, :, 64:80], T1, T2, ALU.subtract)
                    tq(T1, qr[:, :, 0:16], snb, ALU.mult); tq(T2, qr[:, :, 16:32], csb, ALU.mult)
                    tq(q_[:, :, 80:96], T1, T2, ALU.add)
                    for hh in range(2):
                        p, pb = self.pT.next()
                        for h8 in range(8):
                            self.tr(p[0:96, h8 * 128:(h8 + 1) * 128], q_[:, hh * 8 + h8, :], [q_b], [pb], inc=(h8 == 7))
                        dst = QTs[0:96, hh * 8:(hh + 1) * 8, :] if is_s else qst[0:96, hh * 8:(hh + 1) * 8, i * 128:(i + 1) * 128]
                        self.S.op("act", "copy", dict(out=dst, in_=p[0:96, :].rearrange("p (h n) -> p h n", n=128)),
                                  [pb], SM if is_s else [qstb])
                    if STOP == "C":
                        continue
                    if is_s:
                        kb16 = cq
                        kpb16 = self.sb("kpb16", [128, RP], BF16)
                        kpb16_b = S.buf("kpb16")
                        self.dve("tensor_copy", dict(out=kpb16[:], in_=kp[:]), [kpb], [kpb16_b])
                        if "kpT" not in os.environ.get("M1_SKIP", ""):
                            p, pb = self.pT.next()
                            self.tr(p[0:32, 0:128], kpb16[:, :], [kpb16_b], [pb], inc=True)
                            self.dve("tensor_copy", dict(out=kpeT_s[:, :], in_=p[0:32, 0:128]), [pb], SM)
                        continue
                    kvb = [bk.next() for _ in range(4)]
                    for nbk, (bank, bb) in enumerate(kvb):
                        for kc in range(2):
                            self.mm(bank[:], ckT[:, kc, :], wukv[:, kc, nbk * 512:(nbk + 1) * 512], kc == 0, kc == 1,
                                    [ckTb, wb[2]], [bb], kc == 1)
                    if STOP == "D1":
                        continue
                    k_, k_b = ksb.next()
                    v_, v_b = vsb.next()
                    for nbk, (bank, bb) in enumerate(kvb):
                        kv4 = bank[:, :].rearrange("p (h e) -> p h e", e=128)
                        self.S.op("act", "copy", dict(out=k_[:, nbk * 4:(nbk + 1) * 4, 0:64], in_=kv4[:, :, 0:64]), [bb], [k_b])
                        self.dve("tensor_copy", dict(out=v_[:, nbk * 4:(nbk + 1) * 4, 0:64], in_=kv4[:, :, 64:128]), [bb], [v_b])
                    if STOP == "D2":
                        continue
                    self.dve("tensor_copy", dict(out=k_[:, :, 64:96], in_=self.bc(kp[:], 1, [128, NH, RP])), [kpb], [k_b])
                    if STOP == "D3":
                        continue
                    if "vst" not in os.environ.get("M1_SKIP", ""):
                        self.ld(self.v_s[rows, :], v_[:].rearrange("p h e -> p (h e)"), [v_b], [], v_b, q="pool", shared=[self.qkv_s_b])
                    for hh in range(2):
                        p, pb = self.pT.next()
                        for h8 in range(8):
                            self.tr(p[0:96, h8 * 128:(h8 + 1) * 128], k_[:, hh * 8 + h8, :], [k_b], [pb], inc=(h8 == 7))
                        self.dve("tensor_copy", dict(out=kst[0:96, hh * 8:(hh + 1) * 8, i * 128:(i + 1) * 128],
                                                     in_=p[0:96, :].rearrange("p (h n) -> p h n", n=128)), [pb], [kstb])
                if not is_s and "qkst" not in os.environ.get("M1_SKIP", ""):
                    self.ld(self.qT_s[:, :, c0:c0 + G].rearrange("h d t -> d h t"), qst[0:96, :, 0:G], [qstb],
                            [], qstb, q="pool", shared=[self.qkv_s_b])
                    self.ld(self.kT_s[:, :, c0:c0 + G].rearrange("h d t -> d h t"), kst[0:96, :, 0:G], [kstb],
                            [], kstb, q="pool", shared=[self.qkv_s_b])
            S.end_phase()

        oTs = self.gsb("oTs", [128, 8, 128], BF16)
        oTs_b = S.buf("oTs")
        if os.environ.get("MLA_M1_ONLY"):
            return
        SKIP = os.environ.get("MLA_SKIP", "").split(",")
        self.pes = ExitStack()
        with self.pes:
            self.pT = self.psn("pT", 2, [128, 1024], BF16)
            bko = self.psn("bko", 4, [128, 512])
            bks = self.psn("bks", 2, [128, 512])
            bk = bko
            PT = self.sbn("PT", 3, [128, 512], BF16)
            wukT = self.sb("wukT", [64, NH, 256], BF16)
            wukv = self.sb("wukv2", [128, 2, 2048], BF16)
            wuvz = self.sb("wuvz", [128, 2, NH, 128], BF16)
            wb = S.bufs("mlaw2", 4)
            self.ld(wukT[:].rearrange("p a b -> p (a b)"), self.wukT_b[:, :], [self.wcast2_b], [wb[0]], wb[0])
            self.ld(wukv[:].rearrange("p a b -> p (a b)"), self.wukv_b[:, :], [self.wcast2_b], [wb[1]], wb[1])
            self.pool("memset", dict(ap=wuvz[:], constant=0.0), [], [wb[2]])
            for cc in range(2):
                for par in range(2):
                    src = wukv[:, cc, :].rearrange("p (k two e) -> p k two e", two=2, e=128)[:, :, par, 64:128]
                    dst = wuvz[:, cc].rearrange("p (k two) n -> p k two n", two=2)[:, :, par, par * 64:(par + 1) * 64]
                    self.dve("tensor_copy", dict(out=dst, in_=src), [wb[1]], [wb[2]])
            qlatT = self.sb("qlatT", [128, 2, NS, 8, NH], BF16)
            qrT = self.sb("qrT", [32, NS, 8, NH], BF16)
            olatT = self.sb("olatT", [128, 2, NS, 8, NH], BF16)
            ql_b = S.buf("qlat")
            ol_b = S.buf("olat")
            for h in range(NH):
                for cc in range(2):
                    bank, bb = bk.next()
                    self.mm(bank[:, 0:128], wukT[0:64, h, cc * 128:(cc + 1) * 128], QTs[0:64, h, :], True, True,
                            [wb[0]] + SM, [bb], True)
                    self.dve("tensor_copy", dict(out=qlatT[:, cc].rearrange("p s t h -> p (s t) h")[:, :, h],
                                                 in_=bank[:, 0:128]), [bb], [ql_b])
                bank, bb = bk.next()
                self.mm(bank[0:32, 0:128], self.ident[0:96, 64:96], QTs[0:96, h, :], True, True, [self.ident_b] + SM, [bb], True)
                self.dve("tensor_copy", dict(out=qrT[:].rearrange("p s t h -> p (s t) h")[:, :, h], in_=bank[0:32, 0:128]),
                         [bb], [ql_b])
            nmask = self.sb("nmask", [8, 8, NH], BF16)
            nm_b = S.buf("nmask")
            self.pool("memset", dict(ap=nmask[:], constant=1.0), [], [nm_b])
            self.pool("affine_select", dict(out=nmask[:], in_=nmask[:], pattern=[[1, 8], [0, NH]], compare_op=ALU.is_ge,
                                            fill=0.0, base=0, channel_multiplier=-1), [], [nm_b])
            ptb_ = self.sb("ptb", [128, NS * NPG], I32)
            idx = self.sb("idx", [128, NS * NPG // 4], I32)
            iof = self.sb("iof", [128, 1])
            ix_b = S.buf("idx")
            self.ld(ptb_[:], self.ptab.rearrange("s j -> (s j)").partition_broadcast(128), [], [ix_b], ix_b)
            for q4 in range(4):
                self.pool("iota", dict(out=iof[32 * q4:32 * q4 + 32, :], pattern=[[0, 1]], base=0, channel_multiplier=1,
                                       allow_small_or_imprecise_dtypes=True), [], [ix_b])
            for q4 in range(4):
                self.dve("tensor_scalar", dict(out=idx[32 * q4:32 * q4 + 32, :],
                                               in0=ptb_[32 * q4:32 * q4 + 32, :].rearrange("p (m f) -> p m f", f=4)[:, :, q4],
                                               scalar1=32.0, scalar2=iof[32 * q4:32 * q4 + 32, 0:1], op0=ALU.mult,
                                               op1=ALU.add), [ix_b], [ix_b])
            ones_k = self.sb("ones_k", [128, 1], BF16)
            self.dve("memset", dict(ap=ones_k[:], constant=1.0), [], [ix_b])
            cache4 = self.cache_cat.rearrange("(a f) c -> a (f c)", f=4)
            NPGS = 6
            pg = self.sbn("pg", NPGS, [128, 4, 288], BF16)
            ckT4 = self.sbn("ckT4", 3, [128, 4, 2, 128], BF16)
            kpT4 = self.sbn("kpT4", 3, [32, 4, 128], BF16)
            pn = self.sbn("pn", 2, [8, 128], BF16)
            knew = self.sbn("knew", 2, [8, 289], BF16)
            olat = self.sbn("olat", 2, [128, 256], BF16)
            rl1 = self.sbn("rl1", 2, [128, 1])
            for b in range(NS if "decode" not in SKIP else 0):
                acc, accb = bk.next()
                qlb = [qlatT[:, cc, b].rearrange("p t h -> p (t h)") for cc in range(2)]
                qrb = qrT[:, b].rearrange("p t h -> p (t h)")
                sn_, snb = bk.next()
                for cc in range(2):
                    self.mm(sn_[0:8, 0:128], ckvnT_s[:, cc, 8 * b:8 * b + 8], qlb[cc], cc == 0, False, SM + [ql_b], [snb], False)
                self.mm(sn_[0:8, 0:128], kpeT_s[0:32, 8 * b:8 * b + 8], qrb, False, True, SM + [ql_b], [snb], True)
                pn_, pnb = pn.next()
                self.act(dict(out=pn_[:], in_=sn_[0:8, 0:128], func=AF.Exp, scale=ATT_SCALE), [snb], [pnb])
                self.dve("tensor_tensor", dict(out=pn_[:], in0=pn_[:], in1=nmask[:].rearrange("p t h -> p (t h)"), op=ALU.mult),
                         [pnb, nm_b], [pnb])
                kn, knb = bk.next()
                self.mm(kn[0:8, 0:289], self.ident[:, 8 * b:8 * b + 8], ckvn_s[:, 0:289], True, True, SM + [self.ident_b], [knb], True)
                kw_, kwb = knew.next()
                self.dve("tensor_copy", dict(out=kw_[:], in_=kn[0:8, 0:289]), [knb], [kwb])
                accl, acclb = bk.next()
                self.mm(acc[:, 0:288], pn_[:], kw_[:, 0:288], True, False, [pnb, kwb], [accb], False)
                self.mm(accl[:, 0:1], pn_[:], kw_[:, 288:289], True, False, [pnb, kwb], [acclb], False)
                nb4 = NPG // 4
                pgs = {}
                ctk = {}
                pts = {}

                def stage_G(k):
                    pgt, pgb = pg.next()
                    col = b * nb4 + k
                    S.dma("pool", pgt[:].rearrange("p a c -> p (a c)"), cache4, reads=[ix_b], writes=[pgb], owner=pgb,
                          indirect=idx[:, col:col + 1])
                    pgs[k] = (pgt, pgb)

                def stage_A(k):
                    pgt, pgb = pgs[k]
                    p, pb = self.pT.next()
                    for jj in range(4):
                        for cc in range(2):
                            self.tr(p[:, (jj * 2 + cc) * 128:(jj * 2 + cc + 1) * 128], pgt[:, jj, cc * 128:(cc + 1) * 128],
                                    [pgb], [pb], inc=(jj == 3 and cc == 1))
                    c4, c4b = ckT4.next()
                    self.S.op("act", "copy", dict(out=c4[:].rearrange("p a b n -> p (a b n)"), in_=p[:, :]), [pb], [c4b])
                    p2, p2b = self.pT.next()
                    for jj in range(4):
                        self.tr(p2[0:32, jj * 128:(jj + 1) * 128], pgt[:, jj, 256:288], [pgb], [p2b], inc=(jj == 3))
                    k4, k4b = kpT4.next()
                    self.dve("tensor_copy", dict(out=k4[:].rearrange("p a n -> p (a n)"), in_=p2[0:32, 0:512]), [p2b], [k4b])
                    ctk[k] = (c4, c4b, k4, k4b)

                def stage_B(k):
                    c4, c4b, k4, k4b = ctk.pop(k)
                    st, stb = bks.next()
                    for jj in range(4):
                        for cc in range(2):
                            self.mm(st[:, jj * 128:(jj + 1) * 128], c4[:, jj, cc, :], qlb[cc], cc == 0, False,
                                    [c4b, ql_b], [stb], False)
                        self.mm(st[:, jj * 128:(jj + 1) * 128], k4[0:32, jj, :], qrb, False, True, [k4b, ql_b], [stb], jj == 3)
                    pt, ptb = PT.next()
                    self.act(dict(out=pt[:], in_=st[:], func=AF.Exp, scale=ATT_SCALE), [stb], [ptb])
                    pts[k] = (pt, ptb)

                def stage_C(k):
                    pt, ptb = pts.pop(k)
                    pgt, pgb = pgs.pop(k)
                    for jj in range(4):
                        last = (k == nb4 - 1 and jj == 3)
                        self.mm(acc[:, 0:288], pt[:, jj * 128:(jj + 1) * 128], pgt[:, jj, :], False, last, [ptb, pgb],
                                [accb], False)
                        self.mm(accl[:, 0:1], pt[:, jj * 128:(jj + 1) * 128], ones_k[:, 0:1], False, last, [ptb, ix_b],
                                [acclb], last)

                PF = min(NPGS - 2, nb4)
                for k in range(PF):
                    stage_G(k)
                stage_A(0)
                if nb4 > 1:
                    stage_A(1)
                stage_B(0)
                for k in range(nb4):
                    if k + PF < nb4:
                        stage_G(k + PF)
                    if k + 2 < nb4:
                        stage_A(k + 2)
                    if k + 1 < nb4:
                        stage_B(k + 1)
                    stage_C(k)
                r1, r1b = rl1.next()
                self.dve("reciprocal", dict(out=r1[:], in_=accl[:, 0:1]), [acclb], [r1b])
                ol, olb = olat.next()
                self.dve("tensor_scalar", dict(out=ol[:], in0=acc[:, 0:256], scalar1=r1[:, 0:1], scalar2=None, op0=ALU.mult),
                         [accb, r1b], [olb])
                p, pb = self.pT.next()
                for cc in range(2):
                    self.tr(p[:, cc * 128:(cc + 1) * 128], ol[:, cc * 128:(cc + 1) * 128], [olb], [pb], inc=(cc == 1))
                self.dve("tensor_copy", dict(out=olatT[:, :, b].rearrange("p c t h -> p c (t h)"),
                                             in_=p[:, 0:256].rearrange("p (c n) -> p c n", n=128)), [pb], [ol_b])
            for k in range(8):
                bank, bb = bk.next()
                first = True
                for par in range(2):
                    h = 2 * k + par
                    for cc in range(2):
                        self.mm(bank[:, 0:128], wuvz[:, cc, h, :], olatT[:, cc].rearrange("p s t h -> p (s t) h")[:, :, h],
                                first, (par == 1 and cc == 1), [wb[2], ol_b], [bb], (par == 1 and cc == 1))
                        first = False
                self.dve("tensor_copy", dict(out=oTs[:, k, :], in_=bank[:, 0:128]), [bb], [oTs_b])
            S.end_phase()

        self.pes = ExitStack()
        with self.pes:
            self.common_alloc(4)
            self.load_gamma(self.gpost, self.gpost_b, self.norm_post[ni])
            wo = self.sb("wo", [128, 8, D], BF16)
            wo_b_ = S.buf("wo")
            self.ld(wo[:].rearrange("p a b -> p (a b)"), self.wo_b[:, :], [self.wcast2_b], [wo_b_], wo_b_)
            bko = self.psn("bko", 4, [128, 512])
            bks = self.psn("bks", 2, [128, 512])
            bk = bko
            o_all = self.sb("o_all", [128, NTP, D], BF16)
            o_b = S.bufs("o_all", NTP // 4)
            masks = self.sb("masks", [128, 4, 512], BF16)
            mk_b = S.buf("masks")
            self.pool("memset", dict(ap=masks[:], constant=1.0), [], [mk_b])
            for j in range(4):
                self.pool("affine_select", dict(out=masks[:, j, :], in_=masks[:, j, :], pattern=[[1, 512]],
                                                compare_op=ALU.is_ge, fill=0.0, base=-128 * j, channel_multiplier=-1),
                          [], [mk_b])
            QTh = self.sbn("QTh", 2, [128, TP], BF16)
            KTh = self.sbn("KTh", 2, [128, TP], BF16)
            Vh = self.sbn("Vh", 2, [128, NTP, 66], BF16)
            PT = self.sbn("PT", 3, [128, 512], BF16)
            rl = self.sbn("rl", 2, [128, 4])
            v_view = self.v_s.rearrange("(t p) (h e) -> p t h e", p=128, e=66)
            NQG = TP // 512

            for h in range(NH if "prompt" not in SKIP else 0):
                qt, qtb = QTh.next()
                kt_, ktb = KTh.next()
                vh, vhb = Vh.next()
                self.ld(qt[0:96, :], self.qT_s[h], [self.qkv_s_b], [qtb], qtb)
                self.ld(kt_[0:96, :], self.kT_s[h], [self.qkv_s_b], [ktb], ktb)
                with self.nc.allow_non_contiguous_dma(reason="per-head V slice (130B rows)"):
                    pass
                self.S.dma("sp", vh[:], v_view[:, :, h, :], reads=[self.qkv_s_b], writes=[vhb], owner=vhb)
                for Qg in range(NQG):
                    oaccs = [bko.next() for _ in range(4)]
                    nkt = 4 * Qg + 4
                    def issue_st(kt):
                        st, stb = bks.next()
                        self.mm(st[:], kt_[0:96, kt * 128:(kt + 1) * 128], qt[0:96, Qg * 512:(Qg + 1) * 512], True, True,
                                [ktb, qtb], [stb], True)
                        pt, ptb = PT.next()
                        self.act(dict(out=pt[:], in_=st[:], func=AF.Exp, scale=ATT_SCALE), [stb], [ptb])
                        j = kt - 4 * Qg
                        if j >= 0:
                            self.dve("tensor_tensor", dict(out=pt[:], in0=pt[:], in1=masks[:, j, :], op=ALU.mult),
                                     [ptb, mk_b], [ptb])
                        return pt, ptb
                    nxt_pt = issue_st(0)
                    for kt in range(nkt):
                        pt, ptb = nxt_pt
                        if kt + 1 < nkt:
                            nxt_pt = issue_st(kt + 1)
                        for qi in range(4):
                            if 4 * Qg + qi >= kt:
                                last = (kt == 4 * Qg + qi)
                                self.mm(oaccs[qi][0][:, 0:65], pt[:, qi * 128:(qi + 1) * 128], vh[:, kt, 0:65],
                                        kt == 0, last, [ptb, vhb], [oaccs[qi][1]], last)
                    r, rb = rl.next()
                    for qi in range(4):
                        oacc, oab = oaccs[qi]
                        self.dve("reciprocal", dict(out=r[:, qi:qi + 1], in_=oacc[:, 64:65]), [oab], [rb])
                        self.dve("tensor_scalar", dict(out=o_all[:, Qg * 4 + qi, h * 64:(h + 1) * 64], in0=oacc[:, 0:64],
                                                       scalar1=r[:, qi:qi + 1], scalar2=None, op0=ALU.mult),
                                 [oab, rb], [o_b[Qg]])
            oT = self.sbn("oT", 2, [128, 8, 128], BF16)
            for ti in range(NT if "m3" not in SKIP else 0):
                xt, xb = self.xt.next()
                self.ld(xt[:], self.xs[ti * 128:(ti + 1) * 128, :], [self.xs_b[ti]], [xb], xb)
                if ti < NTP:
                    ot, otb = oT.next()
                    p, pb = self.pT.next()
                    for c in range(8):
                        self.tr(p[:, c * 128:(c + 1) * 128], o_all[:, ti, c * 128:(c + 1) * 128], [o_b[ti // 4]], [pb], inc=(c == 7))
                    self.S.op("act", "copy", dict(out=ot[:], in_=p[:, :].rearrange("p (c n) -> p c n", n=128)), [pb], [otb])
                else:
                    ot, otb = oTs, oTs_b
                ys = []
                for hh in range(2):
                    y, ybuf = bk.next()
                    for kc in range(8):
                        self.mm(y[:], ot[:, kc, :], wo[:, kc, hh * 512:(hh + 1) * 512], kc == 0, kc == 7, [otb, wo_b_], [ybuf],
                                kc == 7)
                    ys.append((y[:], ybuf))
                self.post_residual(ys, xt, xb, self.xs, [self.xs_b[ti]], ti, 1.0)
            S.end_phase()


def _kc(w, nk):
    n = w.shape[1]
    return np.ascontiguousarray(w.reshape(nk, 128, n).transpose(1, 0, 2)).reshape(128, nk * n)


def _qp(a):
    rest = a.shape[2:]
    a = a.reshape((32, 2, 64) + rest)
    perm = (1, 2, 0) + tuple(range(3, 3 + len(rest)))
    return np.ascontiguousarray(a.transpose(perm)).reshape((128, 32) + rest)


def prep_shared(inp):
    f32 = np.float32
    A = lambda k: np.asarray(inp[k], f32)
    g = A("ffn_w_gate").reshape(4, 8, 128, NF, 128)
    u = A("ffn_w_up").reshape(4, 8, 128, NF, 128)
    d = A("ffn_w_down").reshape(4, NF, 128, D)
    sh = {
        "wg_h": np.ascontiguousarray(g.transpose(0, 3, 2, 1, 4)).reshape(4, NF * 128, 8 * 128),
        "wu_h": np.ascontiguousarray(u.transpose(0, 3, 2, 1, 4)).reshape(4, NF * 128, 8 * 128),
        "wd_h": np.ascontiguousarray(d.transpose(0, 2, 1, 3)).reshape(4, 128, NF * D),
        "norm_pre": np.ascontiguousarray(A("norm_pre").reshape(6, D)),
        "norm_post": np.ascontiguousarray(A("norm_post").reshape(6, D)),
        "s5_are": _qp(A("ssm_a_re")[0]),
        "s5_aim": _qp(A("ssm_a_im")[0]),
        "s5_ldt": _qp(np.broadcast_to(A("ssm_log_dt")[0][:, None], (64, 64))),
        "s5_bre": _qp(A("ssm_b_re")[0]).reshape(128, 512),
        "s5_bim": _qp(A("ssm_b_im")[0]).reshape(128, 512),
        "s5_cre": _qp(A("ssm_c_re")[0].transpose(0, 2, 1)).reshape(128, 512),
        "s5_cim": _qp(A("ssm_c_im")[0].transpose(0, 2, 1)).reshape(128, 512),
        "s5_d": np.ascontiguousarray(A("ssm_d")[0].reshape(8, 128).T),
        "wglu_h": _kc(A("ssm_w_glu")[0], 8),
        "win_h": _kc(A("mla_w_in")[0], 8),
        "wuq_h": _kc(np.concatenate([A("mla_w_uq")[0].reshape(QL, NH, DQK)[:, :, :DN].reshape(QL, NH * DN),
                                     A("mla_w_uq")[0].reshape(QL, NH, DQK)[:, :, DN:].reshape(QL, NH * RP)], axis=1), 6),
        "wukv_h": _kc(A("mla_w_ukv")[0], 2),
        "wukT_h": np.ascontiguousarray(A("mla_w_ukv")[0].reshape(256, 16, 128)[:, :, :64].transpose(2, 1, 0)).reshape(64, 4096),
        "wo_h": _kc(A("mla_w_o")[0], 8),
        "qnorm": np.ascontiguousarray(A("mla_q_norm").reshape(1, QL)),
        "kvnorm": np.ascontiguousarray(A("mla_kv_norm").reshape(1, KVL)),
        "cache_cat": np.concatenate([A("cache_kv_latent")[0].reshape(-1, KVL), A("cache_k_rope")[0].reshape(-1, RP)], axis=1),
    }
    return sh


def prep_core(inp, c, cfg):
    f32 = np.float32
    ns = cfg.NS
    xp = np.asarray(inp["x_prompt"], f32)[c].reshape(cfg.TP, D)
    xsm = np.asarray(inp["x_sample"], f32)[c * ns:(c + 1) * ns].reshape(ns * cfg.TS, D)
    hre = np.asarray(inp["state_ssm_re"], f32)[0, c * ns:(c + 1) * ns]
    him = np.asarray(inp["state_ssm_im"], f32)[0, c * ns:(c + 1) * ns]
    h = np.stack([hre, him], axis=0)
    h = h.transpose(2, 3, 0, 1)
    h0 = _qp(np.ascontiguousarray(h)).reshape(128, 32 * 2 * ns)
    return {
        "x_in": np.ascontiguousarray(np.concatenate([xp, xsm], axis=0)),
        "s5_h0": h0,
        "ptab": np.ascontiguousarray(np.asarray(inp["page_table"], np.int32)[c * ns:(c + 1) * ns]),
    }


def _unqp(a):
    rest = a.shape[2:]
    a = a.reshape((2, 64, 32) + rest)
    perm = (2, 0, 1) + tuple(range(3, 3 + len(rest)))
    return np.ascontiguousarray(a.transpose(perm)).reshape((64, 64) + rest)


def assemble(results, cfg, n_cores):
    TP, ns, ts = cfg.TP, cfg.NS, cfg.TS
    f32 = np.float32
    yp = np.zeros((n_cores, TP, D), f32)
    ys = np.zeros((n_cores * ns, ts, D), f32)
    srp = np.zeros((1, n_cores, 64, 64), f32)
    sip = np.zeros((1, n_cores, 64, 64), f32)
    srs = np.zeros((1, n_cores * ns, 64, 64), f32)
    sis = np.zeros((1, n_cores * ns, 64, 64), f32)
    lp = np.zeros((1, n_cores, TP, KVL), f32)
    kp = np.zeros((1, n_cores, TP, RP), f32)
    ls = np.zeros((1, n_cores * ns, ts, KVL), f32)
    ks = np.zeros((1, n_cores * ns, ts, RP), f32)
    for c, r in enumerate(results):
        y = r["y_out"]
        yp[c] = y[:TP]
        ys[c * ns:(c + 1) * ns] = y[TP:].reshape(ns, ts, D)
        sp = _unqp(r["ssm_p"].reshape(128, 32, 2))
        srp[0, c] = sp[:, :, 0]
        sip[0, c] = sp[:, :, 1]
        ss = _unqp(r["ssm_s"].reshape(128, 32, 2, ns))
        srs[0, c * ns:(c + 1) * ns] = ss[:, :, 0, :].transpose(2, 0, 1)
        sis[0, c * ns:(c + 1) * ns] = ss[:, :, 1, :].transpose(2, 0, 1)
        lat = r["lat_out"]
        kpe = r["kpe_out"]
        lp[0, c] = lat[:TP]
        kp[0, c] = kpe[:TP]
        ls[0, c * ns:(c + 1) * ns] = lat[TP:].reshape(ns, ts, KVL)
        ks[0, c * ns:(c + 1) * ns] = kpe[TP:].reshape(ns, ts, RP)
    return (yp, ys, srp, sip, srs, sis, lp, kp, ls, ks)


def run(inputs, n_cores, cfg):
    nc = build(cfg)
    sh = prep_shared(inputs)
    in_maps = []
    for c in range(n_cores):
        m = dict(sh)
        m.update(prep_core(inputs, c, cfg))
        in_maps.append(m)
    res = run_bass_kernel_spmd(nc, in_maps, core_ids=list(range(n_cores)))
    return assemble(res.results, cfg, n_cores)


def kernel(**inputs):
    cfg = Cfg(TP=4096, NS=16, TS=8, NPG=128, NPHYS=int(np.asarray(inputs["cache_kv_latent"]).shape[1]))
    return run(inputs, 8, cfg)
```

```python
import math
import os
import numpy as np
import concourse.bass as bass
import concourse.mybir as mybir
from concourse.bass_utils import run_bass_kernel_spmd
from contextlib import ExitStack

F32 = mybir.dt.float32
BF16 = mybir.dt.bfloat16
I32 = mybir.dt.int32
AF = mybir.ActivationFunctionType
ALU = mybir.AluOpType
AX = mybir.AxisListType

D = 1024
DFF = 2816
NF = DFF // 128
EPS = 1e-6
QL = 768
KVL = 256
RP = 32
NH = 16
DN = 64
DV = 64
DQK = DN + RP
ATT_SCALE = DQK ** -0.5
ROPE_THETA = 10000.0
PAGE = 128
TWO_PI = 2.0 * math.pi
C1_2PI = 6.28125
C2_2PI = TWO_PI - 6.28125


class DSem:
    __slots__ = ("sem", "cnt")

    def __init__(self, sem):
        self.sem = sem
        self.cnt = 0


class Buf:
    __slots__ = ("name", "w", "r", "ds", "excl")

    def __init__(self, name, excl=False):
        self.name = name
        self.w = {}
        self.r = {}
        self.ds = None
        self.excl = excl


class Sched:
    ENGS = ("pe", "act", "dve", "pool", "sp")
    HANDLES = {"pe": "tensor", "act": "scalar", "dve": "vector", "pool": "gpsimd", "sp": "sync"}

    def __init__(self, nc, es):
        self.nc = nc
        self.es = es
        self.q = {e: [] for e in self.ENGS}
        self.cnt = {e: 0 for e in self.ENGS}
        self.pending = {e: False for e in self.ENGS}
        self.sem = {e: es.enter_context(nc.semaphore("sem_" + e)) for e in self.ENGS}
        self.waited = {}
        self.ds_free = {}
        self.ds_used = []
        self.ds_all = []
        self.ninst = 0

    def buf(self, name):
        return Buf(name)

    def bufs(self, name, n):
        return [Buf("%s%d" % (name, i)) for i in range(n)]

    def _dsem(self, b, kind):
        if b.ds is None:
            b.ds = {}
        if kind not in b.ds:
            free = self.ds_free.setdefault(kind, [])
            if free:
                d = free.pop()
            else:
                d = DSem(self.es.enter_context(self.nc.semaphore("ds%s_%d" % (kind, len(self.ds_all)))))
                self.ds_all.append(d)
            b.ds[kind] = d
            self.ds_used.append((kind, d))
        return b.ds[kind]

    def _filter(self, eng, deps):
        out = []
        for s, v in deps.items():
            if eng == "pe" and s is self.sem["pe"]:
                continue
            key = (eng, id(s))
            if self.waited.get(key, 0) >= v:
                continue
            self.waited[key] = v
            out.append((s, v))
        return out

    def _deps(self, eng, reads, writes, shared=(), skip_sem=None):
        deps = {}

        def add(s, v):
            if deps.get(s, 0) < v:
                deps[s] = v

        for b in reads:
            for s, v in b.w.items():
                add(s, v)
            if b.excl:
                for s, v in b.r.items():
                    add(s, v)
        for b in writes:
            for s, v in b.w.items():
                add(s, v)
            for s, v in b.r.items():
                add(s, v)
        for b in shared:
            for s, v in b.r.items():
                add(s, v)
        if skip_sem is not None:
            deps.pop(skip_sem, None)
        return self._filter(eng, deps)

    def _mark(self, ev, reads, writes, shared=()):
        s, v = ev
        for b in reads:
            if b.r.get(s, 0) < v:
                b.r[s] = v
        for b in writes:
            b.w = {s: v}
            b.r = {}
        for b in shared:
            if b.w.get(s, 0) < v:
                b.w[s] = v

    def op(self, eng, mname, kw, reads=(), writes=(), inc=True):
        fn = (lambda e, mname=mname, kw=kw: getattr(e, mname)(**kw))
        waits = self._deps(eng, reads, writes)
        if inc:
            self.cnt[eng] += 1
            ev = (self.sem[eng], self.cnt[eng])
            self.pending[eng] = False
            incr = (self.sem[eng], 1)
        else:
            ev = (self.sem[eng], self.cnt[eng] + 1)
            self.pending[eng] = True
            incr = None
        self._mark(ev, reads, writes)
        self.q[eng].append((waits, fn, incr))
        self.ninst += 1

    def dma(self, q, out, in_, reads=(), writes=(), owner=None, indirect=None, shared=(), **kw):
        ds = self._dsem(owner, "sw" if q == "pool" else "hw")
        skip = ds.sem if (owner in writes) else None
        waits = self._deps(q, reads, writes, shared, skip_sem=skip)
        ds.cnt += 1
        ev = (ds.sem, 16 * ds.cnt)
        self._mark(ev, reads, writes, shared)
        if indirect is not None:
            fn = (lambda e: e.indirect_dma_start(out=out, out_offset=None, in_=in_,
                                                 in_offset=bass.IndirectOffsetOnAxis(ap=indirect, axis=0)))
        else:
            fn = (lambda e: e.dma_start(out=out, in_=in_, **kw))
        self.q[q].append((waits, fn, (ds.sem, 16)))
        self.ninst += 1

    def barrier(self):
        for en in self.ENGS:
            if self.pending[en]:
                self.cnt[en] += 1
                self.pending[en] = False
                self.q[en].append(([], (lambda e: e.nop()), (self.sem[en], 1)))
        deps = {}
        for en in self.ENGS:
            if self.cnt[en] > 0:
                deps[self.sem[en]] = self.cnt[en]
        for ds in self.ds_all:
            if ds.cnt > 0:
                deps[ds.sem] = 16 * ds.cnt
        for en in self.ENGS:
            self.q[en].append((self._filter(en, dict(deps)), None, None))

    def end_phase(self):
        self.barrier()
        self.emit()
        for kind, d in self.ds_used:
            self.ds_free.setdefault(kind, []).append(d)
        self.ds_used = []

    def emit(self):
        nc = self.nc
        with nc.Block() as block:
            for en in self.ENGS:
                items = self.q[en]

                def body(e, items=items):
                    for waits, fn, incr in items:
                        for s, v in waits:
                            e.wait_ge(s, v)
                        if fn is not None:
                            ins = fn(e)
                            if incr is not None:
                                ins.then_inc(incr[0], incr[1])

                getattr(block, self.HANDLES[en])(body)
        self.q = {e: [] for e in self.ENGS}


class RR:
    def __init__(self, tensors, bufs):
        self.t = tensors
        self.b = bufs
        self.i = 0

    def next(self):
        k = self.i % len(self.t)
        self.i += 1
        return self.t[k], self.b[k]


class Cfg:
    def __init__(self, TP=4096, NS=16, TS=8, NPG=128, NPHYS=20480, stages=None):
        self.TP = TP
        self.NS = NS
        self.TS = TS
        self.NPG = NPG
        self.NPHYS = NPHYS
        self.NTOK = TP + NS * TS
        assert NS * TS == 128 and TP % 512 == 0 and TS == 8
        self.NT = self.NTOK // 128
        self.NB = TP // 8
        self.NBT = self.NB + NS
        self.PAST = NPG * PAGE
        self.stages = stages


def build(cfg):
    nc = bass.Bass("TRN2", target_bir_lowering=False)
    es = ExitStack()
    with es:
        P = Prog(nc, es, cfg)
        P.run()
    return nc


class Prog:
    def __init__(self, nc, es, cfg):
        self.nc = nc
        self.es = es
        self.cfg = cfg
        self.S = Sched(nc, es)
        self.pes = None
        groups = []
        t = 0
        while t < cfg.NT:
            n = 4 if (t * 128) < cfg.TP else 1
            groups.append((t, n))
            t += n
        self.groups = groups

    def dram_in(self, name, shape, dt=F32):
        return self.nc.dram_tensor(name, list(shape), dt, kind="ExternalInput").ap()

    def dram_out(self, name, shape, dt=F32):
        return self.nc.dram_tensor(name, list(shape), dt, kind="ExternalOutput").ap()

    def dram_tmp(self, name, shape, dt=F32):
        return self.nc.dram_tensor(name, list(shape), dt, kind="Internal").ap()

    def gsb(self, name, shape, dt=F32):
        return self.es.enter_context(self.nc.sbuf_tensor(name, list(shape), dt))

    def sb(self, name, shape, dt=F32):
        self._n = getattr(self, "_n", 0) + 1
        return self.pes.enter_context(self.nc.sbuf_tensor("%s_%d" % (name, self._n), list(shape), dt))

    def ps(self, name, shape, dt=F32):
        self._n = getattr(self, "_n", 0) + 1
        return self.pes.enter_context(self.nc.psum_tensor("%s_%d" % (name, self._n), list(shape), dt))

    def sbn(self, name, n, shape, dt=F32):
        return RR([self.sb("%s%d" % (name, i), shape, dt) for i in range(n)], self.S.bufs(name, n))

    def psn(self, name, n, shape, dt=F32):
        return RR([self.ps("%s%d" % (name, i), shape, dt) for i in range(n)],
                  [Buf("%s%d" % (name, i), excl=True) for i in range(n)])

    def act(self, kw, reads=(), writes=()):
        self.S.op("act", "activation", kw, reads, writes)

    def dve(self, m, kw, reads=(), writes=()):
        self.S.op("dve", m, kw, reads, writes)

    def pool(self, m, kw, reads=(), writes=()):
        self.S.op("pool", m, kw, reads, writes)

    def mm(self, out, lhsT, rhs, start, stop, reads, writes, inc):
        self.S.op("pe", "matmul", dict(out=out, lhsT=lhsT, rhs=rhs, start=start, stop=stop), reads, writes, inc=inc)

    def tr(self, out, in_, reads, writes, inc):
        n = in_.shape[0]
        self.S.op("pe", "transpose", dict(out=out, in_=in_, identity=self.ident[0:n, 0:n]),
                  list(reads) + [self.ident_b], writes, inc=inc)

    def ld(self, out, in_, reads, writes, owner, q="sp", shared=()):
        self.S.dma(q, out, in_, reads=reads, writes=writes, owner=owner, shared=shared)

    def run(self):
        cfg = self.cfg
        S = self.S
        NT, NTOK, TP = cfg.NT, cfg.NTOK, cfg.TP
        self.x_in = self.dram_in("x_in", [NTOK, D])
        self.y_out = self.dram_out("y_out", [NTOK, D])
        self.xs = self.dram_tmp("xs", [NTOK, D])
        self.xs_b = S.bufs("xs", NT)
        self.xin_b = S.buf("x_in")
        self.y_b = S.bufs("y", NT)
        self.norm_pre = self.dram_in("norm_pre", [6, D])
        self.norm_post = self.dram_in("norm_post", [6, D])
        self.wg_h = self.dram_in("wg_h", [4, NF * 128, 8 * 128])
        self.wu_h = self.dram_in("wu_h", [4, NF * 128, 8 * 128])
        self.wd_h = self.dram_in("wd_h", [4, 128, NF * D])
        self.wg_b = self.dram_tmp("wg_b", [4, NF * 128, 8 * 128], BF16)
        self.wu_b = self.dram_tmp("wu_b", [4, NF * 128, 8 * 128], BF16)
        self.wd_b = self.dram_tmp("wd_b", [4, 128, NF * D], BF16)
        self.wcast_b = S.bufs("wcast", 4)
        self.s5_are = self.dram_in("s5_are", [128, 32])
        self.s5_aim = self.dram_in("s5_aim", [128, 32])
        self.s5_ldt = self.dram_in("s5_ldt", [128, 32])
        self.s5_bre = self.dram_in("s5_bre", [128, 32 * 16])
        self.s5_bim = self.dram_in("s5_bim", [128, 32 * 16])
        self.s5_cre = self.dram_in("s5_cre", [128, 32 * 16])
        self.s5_cim = self.dram_in("s5_cim", [128, 32 * 16])
        self.s5_d = self.dram_in("s5_d", [128, 8])
        self.s5_h0 = self.dram_in("s5_h0", [128, 32 * 2 * 16])
        self.wglu_h = self.dram_in("wglu_h", [128, 8 * 2048])
        self.wglu_b = self.dram_tmp("wglu_b", [128, 8 * 2048], BF16)
        self.ssm_p = self.dram_out("ssm_p", [128, 32 * 2])
        self.ssm_s = self.dram_out("ssm_s", [128, 32 * 2 * 16])
        self.ssm_b = S.buf("ssm_out")
        self.uT_s = self.dram_tmp("uT_s", [8, 128, NTOK], BF16)
        self.us_b = S.bufs("uT_s", 8)
        self.win_h = self.dram_in("win_h", [128, 8 * 1056])
        self.win_b = self.dram_tmp("win_b", [128, 8 * 1056], BF16)
        self.wuq_h = self.dram_in("wuq_h", [128, 6 * 1536])
        self.wuq_b = self.dram_tmp("wuq_b", [128, 6 * 1536], BF16)
        self.wukv_h = self.dram_in("wukv_h", [128, 2 * 2048])
        self.wukv_b = self.dram_tmp("wukv_b", [128, 2 * 2048], BF16)
        self.wukT_h = self.dram_in("wukT_h", [64, 16 * 256])
        self.wukT_b = self.dram_tmp("wukT_b", [64, 16 * 256], BF16)
        self.wo_h = self.dram_in("wo_h", [128, 8 * 1024])
        self.wo_b = self.dram_tmp("wo_b", [128, 8 * 1024], BF16)
        self.qnorm = self.dram_in("qnorm", [1, QL])
        self.kvnorm = self.dram_in("kvnorm", [1, KVL])
        self.cache_cat = self.dram_in("cache_cat", [cfg.NPHYS * PAGE, KVL + RP])
        self.ptab = self.dram_in("ptab", [cfg.NS, cfg.NPG], I32)
        self.lat_out = self.dram_out("lat_out", [NTOK, KVL])
        self.kpe_out = self.dram_out("kpe_out", [NTOK, RP])
        self.mla_out_b = S.buf("mla_out")
        self.qT_s = self.dram_tmp("qT_s", [NH, DQK, TP], BF16)
        self.kT_s = self.dram_tmp("kT_s", [NH, DQK, TP], BF16)
        self.v_s = self.dram_tmp("v_s", [TP, NH * 66], BF16)
        self.qkv_s_b = S.buf("qkv_s")
        self.wcast2_b = S.buf("wcast2")

        self.ident = self.gsb("ident", [128, 128], BF16)
        self.ident_f = self.gsb("ident_f", [128, 128], F32)
        self.ident_b = S.buf("ident")
        self.eps_t = self.gsb("eps_t", [128, 1])
        self.eps_b = S.buf("eps")
        self.mask32 = self.gsb("mask32", [128, 128])
        self.mask32_b = S.buf("mask32")

        self.pes = ExitStack()
        with self.pes:
            self.pool("memset", dict(ap=self.ident_f[:], constant=0.0), writes=[self.ident_b])
            self.pool("affine_select", dict(out=self.ident_f[:], in_=self.ident_f[:], pattern=[[-1, 128]],
                                            compare_op=ALU.not_equal, fill=1.0, base=0, channel_multiplier=1),
                      writes=[self.ident_b])
            self.dve("tensor_copy", dict(out=self.ident[:], in_=self.ident_f[:]), writes=[self.ident_b])
            self.dve("memset", dict(ap=self.eps_t[:], constant=EPS), writes=[self.eps_b])
            self.pool("memset", dict(ap=self.mask32[:], constant=0.0), writes=[self.mask32_b])
            for k in range(4):
                self.pool("memset", dict(ap=self.mask32[32 * k:32 * k + 32, 32 * k:32 * k + 32], constant=1.0),
                          writes=[self.mask32_b])
            for i in range(4):
                for (o, s_) in ((self.wg_b, self.wg_h), (self.wu_b, self.wu_h), (self.wd_b, self.wd_h)):
                    S.dma("pool", o[i], s_[i], writes=[self.wcast_b[i]], owner=self.wcast_b[i])
            for (o, s_) in ((self.wglu_b, self.wglu_h), (self.win_b, self.win_h), (self.wuq_b, self.wuq_h),
                            (self.wukv_b, self.wukv_h), (self.wukT_b, self.wukT_h), (self.wo_b, self.wo_h)):
                S.dma("pool", o[:, :], s_[:, :], writes=[self.wcast2_b], owner=self.wcast2_b)
            S.end_phase()

        st = cfg.stages
        xsb = lambda ti: [self.xs_b[ti]]
        yb = lambda ti: [self.y_b[ti]]
        self.ffn(0, 0, self.x_in, lambda ti: [self.xin_b], self.xs, xsb)
        last = (st == "ffn0")
        if not last:
            self.s5(1)
            last = (st == "s5")
        if not last:
            self.ffn(1, 2, self.xs, xsb, self.xs, xsb)
            self.ffn(2, 3, self.xs, xsb, self.xs, xsb)
            last = (st == "ffn2")
        if not last:
            self.mla(4)
            last = (st == "mla")
        if not last:
            self.ffn(3, 5, self.xs, xsb, self.y_out, yb)
        else:
            self.copy_out()

    def copy_out(self):
        S = self.S
        self.pes = ExitStack()
        with self.pes:
            xt = self.sbn("xt", 4, [128, D])
            for ti in range(self.cfg.NT):
                t, b = xt.next()
                self.ld(t[:], self.xs[ti * 128:(ti + 1) * 128, :], [self.xs_b[ti]], [b], b)
                self.ld(self.y_out[ti * 128:(ti + 1) * 128, :], t[:], [b], [self.y_b[ti]], b, q="pool")
            S.end_phase()

    def common_alloc(self, nxt=8):
        self.xt = self.sbn("xt", nxt, [128, D])
        self.gpre = self.sb("gpre", [128, D])
        self.gpost = self.sb("gpost", [128, D])
        self.gpre_b = self.S.buf("gpre")
        self.gpost_b = self.S.buf("gpost")
        self.junk = self.sb("junk", [128, D], BF16)
        self.junk_b = self.S.buf("junk")
        self.small = self.sbn("small", 8, [128, 8])
        self.xn = self.sbn("xn", 2, [128, D], BF16)
        self.yt = self.sbn("yt", 2, [128, D])
        self.pT = self.psn("pT", 2, [128, 1024], BF16)

    def load_gamma(self, dst, dst_b, src_row):
        self.ld(dst[:], src_row.partition_broadcast(128), [], [dst_b], dst_b)

    def rstd_of(self, parts, ncols):
        sm, smb = self.small.next()
        off = 0
        for i, (ap, bufs) in enumerate(parts):
            w = ap.shape[-1]
            self.act(dict(out=self.junk[:, off:off + w], in_=ap, func=AF.Square, accum_out=sm[:, i:i + 1]),
                     reads=bufs, writes=[self.junk_b, smb])
            off += w
        col = len(parts)
        if len(parts) > 1:
            assert len(parts) == 2
            self.dve("tensor_tensor", dict(out=sm[:, 2:3], in0=sm[:, 0:1], in1=sm[:, 1:2], op=ALU.add), [smb], [smb])
            src = sm[:, 2:3]
            col = 3
        else:
            src = sm[:, 0:1]
        self.act(dict(out=sm[:, col:col + 1], in_=src, func=AF.Sqrt, scale=1.0 / ncols, bias=self.eps_t[:, 0:1]),
                 reads=[smb, self.eps_b], writes=[smb])
        self.dve("reciprocal", dict(out=sm[:, col + 1:col + 2], in_=sm[:, col:col + 1]), [smb], [smb])
        return sm[:, col + 1:col + 2], smb

    def transpose_into(self, src, src_b, nch, dstT, dstT_b, col0, eng="act"):
        p, pb = self.pT.next()
        for c in range(nch):
            self.tr(p[:, c * 128:(c + 1) * 128], src[:, c * 128:(c + 1) * 128], [src_b], [pb], inc=(c == nch - 1))
        kw = dict(out=dstT[:, 0:nch, col0:col0 + 128], in_=p[:, 0:nch * 128].rearrange("p (c n) -> p c n", n=128))
        if eng == "act":
            self.S.op("act", "copy", kw, [pb], [dstT_b])
        else:
            self.S.op("dve", "tensor_copy", kw, [pb], [dstT_b])

    def front(self, ti, src, src_bufs, xnT_t, xnT_tb, col0):
        xt, xb = self.xt.next()
        self.ld(xt[:], src[ti * 128:(ti + 1) * 128, :], src_bufs, [xb], xb)
        rstd, rb = self.rstd_of([(xt[:], [xb])], D)
        xn, xnb = self.xn.next()
        self.dve("scalar_tensor_tensor", dict(out=xn[:], in0=xt[:], scalar=rstd, in1=self.gpre[:],
                                              op0=ALU.mult, op1=ALU.mult), [xb, rb, self.gpre_b], [xnb])
        self.transpose_into(xn, xnb, 8, xnT_t, xnT_tb, col0)
        return xt, xb

    def post_residual(self, halves, xt, xb, dst, dst_bufs, ti, coef):
        rstd, rb = self.rstd_of([(h[0], [h[1]]) for h in halves], D)
        yt, yb = self.yt.next()
        for h in range(2):
            self.dve("scalar_tensor_tensor", dict(out=yt[:, h * 512:(h + 1) * 512], in0=halves[h][0], scalar=rstd,
                                                  in1=self.gpost[:, h * 512:(h + 1) * 512], op0=ALU.mult, op1=ALU.mult),
                     [halves[h][1], rb, self.gpost_b], [yb])
        self.dve("scalar_tensor_tensor", dict(out=yt[:], in0=yt[:], scalar=float(coef), in1=xt[:],
                                              op0=ALU.mult, op1=ALU.add), [yb, xb], [yb])
        self.ld(dst[ti * 128:(ti + 1) * 128, :], yt[:], [yb], dst_bufs, yb, q="pool")

    def ffn(self, fi, ni, src, src_bufs_of, dst, dst_bufs_of):
        S = self.S
        self.pes = ExitStack()
        with self.pes:
            self.common_alloc(8)
            xnT = self.sbn("xnT", 2, [128, 8, 512], BF16)
            wg = self.sbn("wg", 6, [128, 8 * 128], BF16)
            wu = self.sbn("wu", 6, [128, 8 * 128], BF16)
            wd = self.sbn("wd", 2, [128, 11 * D], BF16)
            sg = self.sbn("sg", 4, [128, 512])
            hT = self.sb("hT", [128, NF, 512], BF16)
            hT_b = S.buf("hT")
            pA = self.psn("pAll", 6, [128, 512])
            pB = pA
            pY = pA
            self.load_gamma(self.gpre, self.gpre_b, self.norm_pre[ni])
            self.load_gamma(self.gpost, self.gpost_b, self.norm_post[ni])
            wc = [self.wcast_b[fi]]
            def do_front(gi_):
                t0_, n_ = self.groups[gi_]
                xT_, xTb_ = xnT.next()
                return (xT_, xTb_, [self.front(t0_ + i, src, src_bufs_of(t0_ + i), xT_, xTb_, i * 128) for i in range(n_)])
            nxt = do_front(0)
            for gi, (t0, n) in enumerate(self.groups):
                G = n * 128
                xT, xTb, xts = nxt
                for f in range(NF):
                    wgt, wgb = wg.next()
                    wut, wub = wu.next()
                    self.ld(wgt[:], self.wg_b[fi, f * 128:(f + 1) * 128, :], wc, [wgb], wgb)
                    self.ld(wut[:], self.wu_b[fi, f * 128:(f + 1) * 128, :], wc, [wub], wub)
                    a, ab = pA.next()
                    b, bb = pB.next()
                    for kc in range(8):
                        self.mm(a[:, 0:G], wgt[:, kc * 128:(kc + 1) * 128], xT[:, kc, 0:G], kc == 0, kc == 7,
                                [wgb, xTb], [ab], kc == 7)
                    for kc in range(8):
                        self.mm(b[:, 0:G], wut[:, kc * 128:(kc + 1) * 128], xT[:, kc, 0:G], kc == 0, kc == 7,
                                [wub, xTb], [bb], kc == 7)
                    s, sb_ = sg.next()
                    self.act(dict(out=s[:, 0:G], in_=a[:, 0:G], func=AF.Silu), [ab], [sb_])
                    self.dve("tensor_tensor", dict(out=hT[:, f, 0:G], in0=s[:, 0:G], in1=b[:, 0:G], op=ALU.mult),
                             [sb_, bb], [hT_b])
                has_next = gi + 1 < len(self.groups)
                if has_next:
                    nt0, nn = self.groups[gi + 1]
                    nxT, nxTb = xnT.next()
                    nxts = []
                wds = []
                for hf in range(2):
                    wdt, wdb = wd.next()
                    self.ld(wdt[:], self.wd_b[fi, :, hf * 11 * D:(hf + 1) * 11 * D], wc, [wdb], wdb)
                    wds.append((wdt, wdb))
                for i in range(n):
                    ys = []
                    for h in range(2):
                        y, ybuf = pY.next()
                        for f in range(NF):
                            hf, ff = divmod(f, 11)
                            self.mm(y[:], hT[:, f, i * 128:(i + 1) * 128],
                                    wds[hf][0][:, ff * D + h * 512: ff * D + (h + 1) * 512],
                                    f == 0, f == NF - 1, [hT_b, wds[hf][1]], [ybuf], f == NF - 1)
                        ys.append((y[:], ybuf))
                    if has_next and i < nn:
                        nxts.append(self.front(nt0 + i, src, src_bufs_of(nt0 + i), nxT, nxTb, i * 128))
                    self.post_residual(ys, xts[i][0], xts[i][1], dst, dst_bufs_of(t0 + i), t0 + i, 0.5)
                if has_next:
                    assert nn <= n
                    nxt = (nxT, nxTb, nxts)
            S.end_phase()

    def bc(self, ap, axis, shape):
        return ap.unsqueeze(axis).to_broadcast(list(shape))

    def sincos(self, th, thb, n, want):
        out = {}
        for name in want:
            shift = 0.0 if name == "sin" else math.pi / 2
            t = self.sb("sc_t", [128, n])
            ti = self.sb("sc_i", [128, n], I32)
            b = self.S.buf("sc")
            self.dve("tensor_scalar", dict(out=t[:], in0=th, scalar1=shift, scalar2=1.0 / TWO_PI, op0=ALU.add,
                                           op1=ALU.mult), [thb], [b])
            self.dve("tensor_copy", dict(out=ti[:], in_=t[:]), [b], [b])
            self.dve("tensor_copy", dict(out=t[:], in_=ti[:]), [b], [b])
            r = self.sb("sc_r", [128, n])
            self.dve("scalar_tensor_tensor", dict(out=r[:], in0=t[:], scalar=-C1_2PI, in1=th, op0=ALU.mult,
                                                  op1=ALU.add), [b, thb], [b])
            self.dve("scalar_tensor_tensor", dict(out=r[:], in0=t[:], scalar=-C2_2PI, in1=r[:], op0=ALU.mult,
                                                  op1=ALU.add), [b], [b])
            if shift != 0.0:
                self.dve("tensor_scalar", dict(out=r[:], in0=r[:], scalar1=shift, scalar2=None, op0=ALU.add), [b], [b])
            self.dve("tensor_scalar", dict(out=r[:], in0=r[:], scalar1=math.pi, scalar2=-math.pi, op0=ALU.min,
                                           op1=ALU.max), [b], [b])
            o = self.sb("sc_o", [128, n])
            self.act(dict(out=o[:], in_=r[:], func=AF.Sin), [b], [b])
            out[name] = (o, b)
        return out

    def cmul(self, o_re, o_im, a_re, a_im, b_re, b_im, reads, writes, tmp, neg_im=False):
        t1, t2 = tmp
        tt = lambda o, x, y, op: self.dve("tensor_tensor", dict(out=o, in0=x, in1=y, op=op), reads, writes)
        tt(t1, a_re, b_re, ALU.mult)
        tt(t2, a_im, b_im, ALU.mult)
        tt(o_re, t1, t2, ALU.subtract)
        tt(t1, a_re, b_im, ALU.mult)
        tt(t2, a_im, b_re, ALU.mult)
        tt(o_im, t1, t2, ALU.add)
        if neg_im:
            self.dve("tensor_scalar", dict(out=o_im, in0=o_im, scalar1=-1.0, scalar2=None, op0=ALU.mult), reads, writes)

    def s5(self, ni):
        S = self.S
        cfg = self.cfg
        NB, NBT, NTOK, TP, NS = cfg.NB, cfg.NBT, cfg.NTOK, cfg.TP, cfg.NS
        LOGNB = int(round(math.log2(NB)))
        assert 2 ** LOGNB == NB
        self.pes = ExitStack()
        with self.pes:
            self.common_alloc(4)
            self.load_gamma(self.gpre, self.gpre_b, self.norm_pre[ni])
            xnT = self.sbn("xnT", 2, [128, 8, 512], BF16)
            for gi, (t0, n) in enumerate(self.groups):
                G_ = n * 128
                xT, xTb = xnT.next()
                for i in range(n):
                    self.front(t0 + i, self.xs, [self.xs_b[t0 + i]], xT, xTb, i * 128)
                self.ld(self.uT_s[:, :, t0 * 128:t0 * 128 + G_].rearrange("c p t -> p c t"), xT[:, :, 0:G_], [xTb],
                        [], xTb, q="pool", shared=self.us_b)
            S.end_phase()
        self.pes = ExitStack()
        with self.pes:
            self.small = self.sbn("small", 8, [128, 8])
            self.pT = self.psn("pT", 2, [128, 1024], BF16)
            ucs = self.sbn("uc", 2, [128, NTOK], BF16)
            bk = self.psn("bk", 6, [128, 512])
            gb = S.buf("s5gen")
            G = [gb]
            are = self.sb("are", [128, 32]); aim = self.sb("aim", [128, 32]); ldt = self.sb("ldt", [128, 32])
            bre = self.sb("bre", [128, 32, 16]); bim = self.sb("bim", [128, 32, 16])
            cre = self.sb("cre", [128, 32, 16]); cim = self.sb("cim", [128, 32, 16])
            dcol = self.sb("dcol", [128, 8]); h0 = self.sb("h0", [128, 32, 2, NS])
            lb = S.bufs("s5ld", 9)
            for k, (t, src) in enumerate(((are, self.s5_are), (aim, self.s5_aim), (ldt, self.s5_ldt), (dcol, self.s5_d))):
                self.ld(t[:], src[:, :], [], [lb[k]], lb[k])
            for k, (t, src) in enumerate(((bre, self.s5_bre), (bim, self.s5_bim), (cre, self.s5_cre), (cim, self.s5_cim))):
                self.ld(t[:].rearrange("p a b -> p (a b)"), src[:, :], [], [lb[4 + k]], lb[4 + k])
            self.ld(h0[:].rearrange("p a b c -> p (a b c)"), self.s5_h0[:, :], [], [lb[8]], lb[8])
            LB = list(lb)
            dt = self.sb("dt", [128, 32]); trd = self.sb("trd", [128, 32]); th = self.sb("th", [128, 32])
            mag = self.sb("mag", [128, 32]); rho = self.sb("rho", [128, 32])
            self.act(dict(out=dt[:], in_=ldt[:], func=AF.Exp), LB, G)
            self.dve("tensor_tensor", dict(out=trd[:], in0=are[:], in1=dt[:], op=ALU.mult), LB + G, G)
            self.dve("tensor_tensor", dict(out=th[:], in0=aim[:], in1=dt[:], op=ALU.mult), LB + G, G)
            self.act(dict(out=mag[:], in_=trd[:], func=AF.Exp), G, G)
            self.act(dict(out=rho[:], in_=trd[:], func=AF.Exp, scale=8.0), G, G)
            sc = self.sincos(th[:], gb, 32, ("sin", "cos"))
            LRI = self.sb("LRI", [128, 9, 2, 32])
            nLI = self.sb("nLI", [128, 9, 32])
            t1 = self.sb("t1", [128, 32]); t2 = self.sb("t2", [128, 32])
            self.dve("memset", dict(ap=LRI[:, 0, 0, :], constant=1.0), [], G)
            self.dve("memset", dict(ap=LRI[:, 0, 1, :], constant=0.0), [], G)
            self.dve("tensor_tensor", dict(out=LRI[:, 1, 0, :], in0=mag[:], in1=sc["cos"][0][:], op=ALU.mult),
                     G + [sc["cos"][1]], G)
            self.dve("tensor_tensor", dict(out=LRI[:, 1, 1, :], in0=mag[:], in1=sc["sin"][0][:], op=ALU.mult),
                     G + [sc["sin"][1]], G)
            for tau in range(2, 9):
                self.cmul(LRI[:, tau, 0, :], LRI[:, tau, 1, :], LRI[:, tau - 1, 0, :], LRI[:, tau - 1, 1, :],
                          LRI[:, 1, 0, :], LRI[:, 1, 1, :], G, G, (t1[:], t2[:]))
            self.dve("tensor_scalar", dict(out=nLI[:], in0=LRI[:, :, 1, :], scalar1=-1.0, scalar2=None, op0=ALU.mult), G, G)
            gre = self.sb("gre", [128, 32]); gim = self.sb("gim", [128, 32]); nr = self.sb("nr", [128, 32])
            den = self.sb("den", [128, 32])
            tt = lambda o, x, y, op: self.dve("tensor_tensor", dict(out=o, in0=x, in1=y, op=op), LB + G, G)
            self.dve("tensor_scalar", dict(out=nr[:], in0=LRI[:, 1, 0, :], scalar1=-1.0, scalar2=None, op0=ALU.add), G, G)
            tt(t1[:], are[:], are[:], ALU.mult)
            tt(t2[:], aim[:], aim[:], ALU.mult)
            tt(den[:], t1[:], t2[:], ALU.add)
            self.dve("reciprocal", dict(out=den[:], in_=den[:]), G, G)
            tt(t1[:], nr[:], are[:], ALU.mult)
            tt(t2[:], LRI[:, 1, 1, :], aim[:], ALU.mult)
            tt(gre[:], t1[:], t2[:], ALU.add)
            tt(gre[:], gre[:], den[:], ALU.mult)
            tt(t1[:], LRI[:, 1, 1, :], are[:], ALU.mult)
            tt(t2[:], nr[:], aim[:], ALU.mult)
            tt(gim[:], t1[:], t2[:], ALU.subtract)
            tt(gim[:], gim[:], den[:], ALU.mult)
            Pk = self.sb("Pk", [128, LOGNB + 1, 2, 32])
            self.dve("reciprocal", dict(out=t1[:], in_=rho[:]), G, G)
            tt(Pk[:, 0, 0, :], LRI[:, 8, 0, :], t1[:], ALU.mult)
            tt(Pk[:, 0, 1, :], LRI[:, 8, 1, :], t1[:], ALU.mult)
            for k in range(LOGNB):
                self.cmul(Pk[:, k + 1, 0, :], Pk[:, k + 1, 1, :], Pk[:, k, 0, :], Pk[:, k, 1, :],
                          Pk[:, k, 0, :], Pk[:, k, 1, :], G, G, (t1[:], t2[:]))
            Bre = self.sb("Bre", [128, 32, 16]); Bim = self.sb("Bim", [128, 32, 16])
            T1 = self.sb("T1", [128, 32, 16]); T2 = self.sb("T2", [128, 32, 16])
            gre_b = self.bc(gre[:], 2, [128, 32, 16]); gim_b = self.bc(gim[:], 2, [128, 32, 16])
            self.cmul(Bre[:], Bim[:], bre[:], bim[:], gre_b, gim_b, LB + G, G, (T1[:], T2[:]))
            Wre = self.sb("Wre", [128, 8, 4, 16]); Wim = self.sb("Wim", [128, 8, 4, 16])
            Wt1 = self.sb("Wt1", [128, 9, 4, 16]); Wt2 = self.sb("Wt2", [128, 9, 4, 16])
            VVre = self.sb("VVre", [128, 9, 4, 16]); VVim = self.sb("VVim", [128, 9, 4, 16])
            Wexp = self.sb("Wexp", [128, 2, 8, 4, 2, 16], BF16)
            VVexp = self.sb("VVexp", [128, 2, 9, 4, 2, 16], BF16)
            Bexp = self.sb("Bexp", [128, 2, 4, 2, 16], BF16)
            WTz = self.sb("WTz", [128, 4, 2, 8, 128], BF16)
            VVz = self.sb("VVz", [128, 8, 4, 2, 128], BF16)
            BD = self.sb("BD", [128, 8, 128], BF16)
            Ect = self.sb("Ec", [128, 4, NB]); Est = self.sb("Es", [128, 4, NB])
            w_sb = self.sb("w_sb", [128, 4, 2, NBT])
            v_sb = self.sb("v_sb", [128, 4, 2, NB])
            g_sb = self.sb("g_sb", [128, 4, 2, NB])
            zbf = self.sb("zbf", [128, 4, 2, NBT], BF16)
            zs = self.sb("zs", [128, 4, 2, NS])
            zfin = self.sb("zfin", [128, 8, 4, 2])
            E1 = self.sb("E1", [128, 4, NB]); E2 = self.sb("E2", [128, 4, NB])
            ytmp = self.sbn("ytmp", 2, [128, 512])
            cb = S.buf("s5chunk")
            C = [cb]
            for t in (Wexp, VVexp, Bexp, WTz, VVz):
                self.pool("memset", dict(ap=t[:], constant=0.0), [], C)
            self.dve("memset", dict(ap=zbf[:, :, :, 0:1], constant=0.0), [], C)
            for c in range(8):
                ps = slice(4 * c, 4 * c + 4)
                RD = LB + G + C
                uc, ucb = ucs.next()
                self.ld(uc[:], self.uT_s[c], [self.us_b[c]], [ucb], ucb)
                lr8 = self.bc(LRI[:, 0:8, 0, ps], 3, [128, 8, 4, 16]); li8 = self.bc(LRI[:, 0:8, 1, ps], 3, [128, 8, 4, 16])
                Bre8 = self.bc(Bre[:, ps, :], 1, [128, 8, 4, 16]); Bim8 = self.bc(Bim[:, ps, :], 1, [128, 8, 4, 16])
                self.cmul(Wre[:], Wim[:], lr8, li8, Bre8, Bim8, RD, C, (Wt1[:, 0:8], Wt2[:, 0:8]))
                lr9 = self.bc(LRI[:, :, 0, ps], 3, [128, 9, 4, 16]); li9 = self.bc(LRI[:, :, 1, ps], 3, [128, 9, 4, 16])
                cre9 = self.bc(cre[:, ps, :], 1, [128, 9, 4, 16]); cim9 = self.bc(cim[:, ps, :], 1, [128, 9, 4, 16])
                self.cmul(VVre[:], VVim[:], cre9, cim9, lr9, li9, RD, C, (Wt1[:], Wt2[:]), neg_im=True)
                for g2, pr in ((0, slice(0, 64)), (1, slice(64, 128))):
                    for ri, (wsrc, vsrc, bsrc) in enumerate(((Wre, VVre, Bre), (Wim, VVim, Bim))):
                        self.dve("tensor_copy", dict(out=Wexp[pr, ri, :, :, g2, :], in_=wsrc[pr]), RD, C)
                        self.dve("tensor_copy", dict(out=VVexp[pr, ri, :, :, g2, :], in_=vsrc[pr]), RD, C)
                        self.dve("tensor_copy", dict(out=Bexp[pr, ri, :, g2, :], in_=bsrc[pr, ps, :]), RD, C)
                for ri in range(2):
                    p, pb = self.pT.next()
                    for n in range(8):
                        self.tr(p[:, n * 128:(n + 1) * 128], Wexp[:, ri, n].rearrange("p a b c -> p (a b c)"), C, [pb],
                                inc=(n == 7))
                    for p4 in range(4):
                        self.S.op("act", "copy", dict(out=WTz[32 * p4:32 * p4 + 32, p4, ri, :, :],
                                                      in_=p[32 * p4:32 * p4 + 32, :].rearrange("p (n q) -> p n q", q=128)),
                                  [pb], C)
                for p4 in range(4):
                    for ri in range(2):
                        self.dve("tensor_copy", dict(out=VVz[:, :, p4, ri, 32 * p4:32 * p4 + 32],
                                                     in_=VVexp[:, ri, 1:9, p4].rearrange("p t a b -> p t (a b)")), C, C)
                for half in range(2):
                    bank, bb = bk.next()
                    first = True
                    for t4 in range(4):
                        tau = half * 4 + t4
                        for ri in range(2):
                            self.mm(bank[:, t4 * 128:(t4 + 1) * 128], Bexp[:, ri].rearrange("p a b c -> p (a b c)"),
                                    VVexp[:, ri, tau].rearrange("p a b c -> p (a b c)"), ri == 0, ri == 1,
                                    C, [bb], (t4 == 3 and ri == 1))
                    self.dve("tensor_tensor", dict(out=BD[:, half * 4:half * 4 + 4, :],
                                                   in0=bank[:, :].rearrange("p (t n) -> p t n", n=128),
                                                   in1=self.bc(self.mask32[:], 1, [128, 4, 128]), op=ALU.mult),
                             [bb, self.mask32_b], C)
                self.dve("memset", dict(ap=Ect[:, :, 0:1], constant=1.0), [], C)
                self.dve("memset", dict(ap=Est[:, :, 0:1], constant=0.0), [], C)
                for k in range(LOGNB):
                    n = 2 ** k
                    pc = self.bc(Pk[:, k, 0, ps], 2, [128, 4, n]); psn_ = self.bc(Pk[:, k, 1, ps], 2, [128, 4, n])
                    self.cmul(Ect[:, :, n:2 * n], Est[:, :, n:2 * n], Ect[:, :, 0:n], Est[:, :, 0:n], pc, psn_, G + C, C,
                              (E1[:, :, 0:n], E2[:, :, 0:n]))
                for gi, (t0, n) in enumerate(self.groups):
                    c0 = t0 * 128
                    ntok = n * 128
                    nb = ntok // 8
                    gb0 = c0 // 8
                    bank, bb = bk.next()
                    first = True
                    for p4 in range(4):
                        for ri in range(2):
                            for s_ in range(8):
                                self.mm(bank[:, (p4 * 2 + ri) * 64:(p4 * 2 + ri) * 64 + nb], WTz[:, p4, ri, 7 - s_, :],
                                        uc[:, c0 + s_:c0 + ntok:8], s_ == 0, s_ == 7, C + [ucb], [bb],
                                        (p4 == 3 and ri == 1 and s_ == 7))
                    self.S.op("act", "copy", dict(out=w_sb[:, :, :, gb0:gb0 + nb],
                                                  in_=bank[:, :].rearrange("p (a b n) -> p a b n", a=4, b=2)[:, :, :, 0:nb]),
                              [bb], C)
                wre_ = w_sb[:, :, 0, 0:NB]; wim_ = w_sb[:, :, 1, 0:NB]
                tt2 = lambda o, x, y, op: self.dve("tensor_tensor", dict(out=o, in0=x, in1=y, op=op), C, C)
                tt2(E1[:], Ect[:], wre_, ALU.mult); tt2(E2[:], Est[:], wim_, ALU.mult)
                tt2(v_sb[:, :, 0, :], E1[:], E2[:], ALU.add)
                tt2(E1[:], Ect[:], wim_, ALU.mult); tt2(E2[:], Est[:], wre_, ALU.mult)
                tt2(v_sb[:, :, 1, :], E1[:], E2[:], ALU.subtract)
                for p4 in range(4):
                    for ri in range(2):
                        self.dve("tensor_tensor_scan", dict(out=g_sb[:, p4, ri, :],
                                                            data0=rho[:, 4 * c + p4:4 * c + p4 + 1].to_broadcast([128, NB]),
                                                            data1=v_sb[:, p4, ri, :], initial=0.0, op0=ALU.mult, op1=ALU.add),
                                 G + C, C)
                tt2(E1[:], Ect[:], g_sb[:, :, 0, :], ALU.mult); tt2(E2[:], Est[:], g_sb[:, :, 1, :], ALU.mult)
                tt2(zbf[:, :, 0, 1:NB], E1[:, :, 0:NB - 1], E2[:, :, 0:NB - 1], ALU.subtract)
                tt2(zfin[:, c, :, 0], E1[:, :, NB - 1], E2[:, :, NB - 1], ALU.subtract)
                tt2(E1[:], Ect[:], g_sb[:, :, 1, :], ALU.mult); tt2(E2[:], Est[:], g_sb[:, :, 0, :], ALU.mult)
                tt2(zbf[:, :, 1, 1:NB], E1[:, :, 0:NB - 1], E2[:, :, 0:NB - 1], ALU.add)
                tt2(zfin[:, c, :, 1], E1[:, :, NB - 1], E2[:, :, NB - 1], ALU.add)
                self.dve("tensor_copy", dict(out=zbf[:, :, :, NB:NBT], in_=h0[:, ps, :, :]), LB + C, C)
                self.ld(self.ssm_p[:, 8 * c:8 * c + 8], zfin[:, c].rearrange("p a b -> p (a b)"), C, [], cb, q="pool", shared=[self.ssm_b])
                l8r = self.bc(LRI[:, 8, 0, ps], 2, [128, 4, NS]); l8i = self.bc(LRI[:, 8, 1, ps], 2, [128, 4, NS])
                self.cmul(zs[:, :, 0, :], zs[:, :, 1, :], h0[:, ps, 0, :], h0[:, ps, 1, :], l8r, l8i, LB + G + C, C,
                          (E1[:, :, 0:NS], E2[:, :, 0:NS]))
                tt2(zs[:], zs[:], w_sb[:, :, :, NB:NBT], ALU.add)
                self.ld(self.ssm_s[:, 4 * c * 2 * NS:(4 * c + 4) * 2 * NS], zs[:].rearrange("p a b c -> p (a b c)"), C,
                        [], cb, q="pool", shared=[self.ssm_b])
                for gi, (t0, n) in enumerate(self.groups):
                    c0 = t0 * 128
                    ntok = n * 128
                    nb = ntok // 8
                    gb0 = (c0 // 8) if c0 < TP else NB
                    bank, bb = bk.next()
                    first = True
                    for r in range(8):
                        for tau in range(r + 1):
                            self.mm(bank[:, r * 64:r * 64 + nb], BD[:, tau, :], uc[:, c0 + r - tau:c0 + ntok:8],
                                    tau == 0, False, C + [ucb], [bb], False)
                        for p4 in range(4):
                            for ri in range(2):
                                last = (r == 7 and p4 == 3 and ri == 1)
                                self.mm(bank[:, r * 64:r * 64 + nb], VVz[:, r, p4, ri, :], zbf[:, p4, ri, gb0:gb0 + nb],
                                        False, (p4 == 3 and ri == 1), C, [bb], last)
                    yt_, ytb = ytmp.next()
                    self.dve("scalar_tensor_tensor", dict(
                        out=yt_[:, 0:ntok].rearrange("p (b r) -> p b r", r=8),
                        in0=uc[:, c0:c0 + ntok].rearrange("p (b r) -> p b r", r=8), scalar=dcol[:, c:c + 1],
                        in1=bank[:, :].rearrange("p (r b) -> p b r", r=8)[:, 0:nb, :], op0=ALU.mult, op1=ALU.add),
                        [bb, ucb] + LB, [ytb])
                    self.act(dict(out=uc[:, c0:c0 + ntok], in_=yt_[:, 0:ntok], func=AF.Gelu_apprx_tanh), [ytb],
                             [ucb])
                self.ld(self.uT_s[c], uc[:], [ucb], [self.us_b[c]], ucb, q="pool")
            S.end_phase()
        self.pes = ExitStack()
        with self.pes:
            self.common_alloc(4)
            self.load_gamma(self.gpost, self.gpost_b, self.norm_post[ni])
            wglu = self.sb("wglu", [128, 8, 2048], BF16)
            wglu_b = S.buf("wglu")
            self.ld(wglu[:].rearrange("p a b -> p (a b)"), self.wglu_b[:, :], [self.wcast2_b], [wglu_b], wglu_b)
            bk = self.psn("bk", 6, [128, 512])
            hTs = self.sbn("hTt", 3, [128, 8, 128], BF16)
            sgl = self.sbn("sgl", 2, [128, 512])
            mo = self.sbn("mo", 2, [128, D])
            for gi, (t0, n) in enumerate(self.groups):
                for i in range(n):
                    ti = t0 + i
                    xt, xb = self.xt.next()
                    self.ld(xt[:], self.xs[ti * 128:(ti + 1) * 128, :], [self.xs_b[ti]], [xb], xb)
                    hT, hTb = hTs.next()
                    self.ld(hT[:], self.uT_s[:, :, ti * 128:(ti + 1) * 128].rearrange("c p t -> p c t"), self.us_b, [hTb], hTb)
                    m, mb = mo.next()
                    for h in range(2):
                        zv, zvb = bk.next()
                        zg, zgb = bk.next()
                        for (bank, bb, col) in ((zv, zvb, h * 512), (zg, zgb, 1024 + h * 512)):
                            for kc in range(8):
                                self.mm(bank[:], hT[:, kc, :], wglu[:, kc, col:col + 512], kc == 0,
                                        kc == 7, [hTb, wglu_b], [bb], kc == 7)
                        sg, sgb = sgl.next()
                        self.act(dict(out=sg[:], in_=zg[:], func=AF.Sigmoid), [zgb], [sgb])
                        self.dve("tensor_tensor", dict(out=m[:, h * 512:(h + 1) * 512], in0=sg[:], in1=zv[:], op=ALU.mult),
                                 [sgb, zvb], [mb])
                    self.post_residual([(m[:, 0:512], mb), (m[:, 512:1024], mb)], xt, xb, self.xs, [self.xs_b[ti]], ti, 1.0)
            S.end_phase()

    def mla(self, ni):
        S = self.S
        cfg = self.cfg
        NT, NTOK, TP, NS, NPG = cfg.NT, cfg.NTOK, cfg.TP, cfg.NS, cfg.NPG
        NTP = TP // 128
        QTs = self.gsb("QTs", [128, NH, 128], BF16)
        ckvn_s = self.gsb("ckvn_s", [128, 289], BF16)
        ckvnT_s = self.gsb("ckvnT_s", [128, 2, 128], BF16)
        kpeT_s = self.gsb("kpeT_s", [32, 128], BF16)
        smp_b = S.buf("smp")
        SM = [smp_b]
        self.pes = ExitStack()
        with self.pes:
            self.common_alloc(4)
            self.load_gamma(self.gpre, self.gpre_b, self.norm_pre[ni])
            xnT = self.sbn("xnT", 2, [128, 8, 512], BF16)
            win = self.sb("win", [128, 8, 1056], BF16)
            wuq = self.sb("wuq", [128, 6, 1536], BF16)
            wukv = self.sb("wukv", [128, 2, 2048], BF16)
            wb = S.bufs("mlaw", 3)
            for k, (t, src) in enumerate(((win, self.win_b), (wuq, self.wuq_b), (wukv, self.wukv_b))):
                self.ld(t[:].rearrange("p a b -> p (a b)"), src[:, :], [self.wcast2_b], [wb[k]], wb[k])
            qn_bc = self.sb("qn_bc", [128, QL]); kvn_bc = self.sb("kvn_bc", [128, KVL])
            nb_ = S.bufs("nrm", 2)
            self.ld(qn_bc[:], self.qnorm[0].partition_broadcast(128), [], [nb_[0]], nb_[0])
            self.ld(kvn_bc[:], self.kvnorm[0].partition_broadcast(128), [], [nb_[1]], nb_[1])
            tb = S.buf("ropetab")
            TB = [tb]
            posf = self.sb("posf", [128, NT]); pi_ = self.sb("pi_", [128, 1], I32); invf = self.sb("invf", [128, 16])
            ang = self.sb("ang", [128, NT, 16])
            self.pool("iota", dict(out=posf[:, 0:NTP], pattern=[[128, NTP]], base=0, channel_multiplier=1,
                                   allow_small_or_imprecise_dtypes=True), [], TB)
            self.pool("iota", dict(out=pi_[:], pattern=[[0, 1]], base=0, channel_multiplier=1), [], TB)
            if "and" not in os.environ.get("M1_SKIP", ""):
                self.dve("tensor_single_scalar", dict(out=pi_[:], in_=pi_[:], scalar=7, op=ALU.bitwise_and), TB, TB)
            self.dve("tensor_copy", dict(out=posf[:, NTP:NT], in_=pi_[:]), TB, TB)
            self.dve("tensor_scalar", dict(out=posf[:, NTP:NT], in0=posf[:, NTP:NT], scalar1=float(cfg.PAST), scalar2=None,
                                           op0=ALU.add), TB, TB)
            self.pool("iota", dict(out=invf[:], pattern=[[1, 16]], base=0, channel_multiplier=0,
                                   allow_small_or_imprecise_dtypes=True), [], TB)
            self.act(dict(out=invf[:], in_=invf[:], func=AF.Exp, scale=-math.log(ROPE_THETA) / 16.0), TB, TB)
            self.dve("tensor_tensor", dict(out=ang[:], in0=self.bc(posf[:], 2, [128, NT, 16]),
                                           in1=self.bc(invf[:], 1, [128, NT, 16]), op=ALU.mult), TB, TB)
            sc = self.sincos(ang[:].rearrange("p a b -> p (a b)"), tb, NT * 16, ("sin", "cos"))
            cosT = sc["cos"][0][:].rearrange("p (a b) -> p a b", b=16)
            sinT = sc["sin"][0][:].rearrange("p (a b) -> p a b", b=16)
            RT = [sc["cos"][1], sc["sin"][1]]
            bk = self.psn("bk", 6, [128, 512])
            cqn = self.sbn("cqn", 2, [128, QL], BF16)
            ckf = self.sbn("ckf", 2, [128, KVL])
            ckb = self.sbn("ckb", 2, [128, KVL], BF16)
            kpf = self.sbn("kpf", 2, [128, RP])
            rt = self.sbn("rt", 2, [128, 2, 16, 16])
            cqnT = self.sbn("cqnT", 2, [128, 6, 128], BF16)
            ckvT = self.sbn("ckvT", 2, [128, 2, 128], BF16)
            qsb = self.sbn("qsb", 2, [128, NH, DQK], BF16)
            ksb = self.sbn("ksb", 2, [128, NH, DQK], BF16)
            vsb = self.sbn("vsb", 2, [128, NH, 66], BF16)
            for k in range(2):
                self.pool("memset", dict(ap=vsb.t[k][:, :, 64:66], constant=1.0), [], [vsb.b[k]])
            self.pool("memset", dict(ap=ckvn_s[:, 256:288], constant=0.0), [], SM)
            self.pool("memset", dict(ap=ckvn_s[:, 288:289], constant=1.0), [], SM)
            QTst = self.sbn("QTst", 1, [128, NH, 512], BF16)
            KTst = self.sbn("KTst", 1, [128, NH, 512], BF16)
            STOP = os.environ.get("M1_STOP", "")
            if os.environ.get("KDEBUG"):
                print("M1 sbuf remaining", self.nc.sbuf_bytes_remaining)
            for gi, (t0, n) in enumerate(self.groups if STOP != "A" else []):
                G = n * 128
                c0 = t0 * 128
                is_s = (c0 >= TP)
                xT, xTb = xnT.next()
                for i in range(n):
                    self.front(t0 + i, self.xs, [self.xs_b[t0 + i]], xT, xTb, i * 128)
                qst, qstb = QTst.next()
                kst, kstb = KTst.next()
                for i in range(n):
                    ti = t0 + i
                    rows = slice(ti * 128, (ti + 1) * 128)
                    pj = [bk.next() for _ in range(3)]
                    for (bank, bb), (col, ncol) in zip(pj, ((0, 512), (512, 512), (1024, 32))):
                        for kc in range(8):
                            self.mm(bank[:, 0:ncol], xT[:, kc, i * 128:(i + 1) * 128], win[:, kc, col:col + ncol],
                                    kc == 0, kc == 7, [xTb, wb[0]], [bb], kc == 7)
                    (p0, p0b), (p1, p1b), (p2, p2b) = pj
                    rq, rqb = self.rstd_of([(p0[:, 0:512], [p0b]), (p1[:, 0:256], [p1b])], QL)
                    cq, cqb = cqn.next()
                    self.dve("scalar_tensor_tensor", dict(out=cq[:, 0:512], in0=p0[:, 0:512], scalar=rq, in1=qn_bc[:, 0:512],
                                                          op0=ALU.mult, op1=ALU.mult), [p0b, rqb, nb_[0]], [cqb])
                    self.dve("scalar_tensor_tensor", dict(out=cq[:, 512:768], in0=p1[:, 0:256], scalar=rq,
                                                          in1=qn_bc[:, 512:768], op0=ALU.mult, op1=ALU.mult),
                             [p1b, rqb, nb_[0]], [cqb])
                    rk, rkb = self.rstd_of([(p1[:, 256:512], [p1b])], KVL)
                    cf, cfb = ckf.next()
                    self.dve("scalar_tensor_tensor", dict(out=cf[:], in0=p1[:, 256:512], scalar=rk, in1=kvn_bc[:],
                                                          op0=ALU.mult, op1=ALU.mult), [p1b, rkb, nb_[1]], [cfb])
                    self.ld(self.lat_out[rows, :], cf[:], [cfb], [], cfb, q="pool", shared=[self.mla_out_b])
                    cb_, cbb = ckb.next()
                    self.S.op("act", "copy", dict(out=cb_[:], in_=cf[:]), [cfb], [cbb])
                    if is_s:
                        self.S.op("act", "copy", dict(out=ckvn_s[:, 0:256], in_=cf[:]), [cfb], SM)
                    kp, kpb = kpf.next()
                    r_, rb_ = rt.next()
                    cs = cosT[:, ti, :]
                    sn = sinT[:, ti, :]
                    t1 = r_[:, 0, 0, :]; t2 = r_[:, 1, 0, :]
                    tt = lambda o, x, y, op: self.dve("tensor_tensor", dict(out=o, in0=x, in1=y, op=op),
                                                      [p2b, rb_] + RT, [rb_, kpb])
                    tt(t1, p2[:, 0:16], cs, ALU.mult); tt(t2, p2[:, 16:32], sn, ALU.mult)
                    tt(kp[:, 0:16], t1, t2, ALU.subtract)
                    tt(t1, p2[:, 0:16], sn, ALU.mult); tt(t2, p2[:, 16:32], cs, ALU.mult)
                    tt(kp[:, 16:32], t1, t2, ALU.add)
                    self.ld(self.kpe_out[rows, :], kp[:], [kpb], [], kpb, q="pool", shared=[self.mla_out_b])
                    if STOP == "B":
                        continue
                    cqT, cqTb = cqnT.next()
                    self.transpose_into(cq, cqb, 6, cqT, cqTb, 0, eng="dve")
                    ckT, ckTb = ckvT.next()
                    self.transpose_into(cb_, cbb, 2, ckT, ckTb, 0, eng="dve")
                    if is_s:
                        self.dve("tensor_copy", dict(out=ckvnT_s[:], in_=ckT[:]), [ckTb], SM)
                    qb_ = [bk.next() for _ in range(3)]
                    for nbk, (bank, bb) in enumerate(qb_):
                        for kc in range(6):
                            self.mm(bank[:], cqT[:, kc, :], wuq[:, kc, nbk * 512:(nbk + 1) * 512], kc == 0, kc == 5,
                                    [cqTb, wb[1]], [bb], kc == 5)
                    q_, q_b = qsb.next()
                    for hh in range(2):
                        self.S.op("act", "copy", dict(out=q_[:, hh * 8:(hh + 1) * 8, 0:64],
                                                      in_=qb_[hh][0][:, :].rearrange("p (h d) -> p h d", d=64)),
                                  [qb_[hh][1]], [q_b])
                    qr = qb_[2][0][:, :].rearrange("p (h r) -> p h r", r=32)
                    qrb = qb_[2][1]
                    csb = self.bc(cs, 1, [128, 16, 16]); snb = self.bc(sn, 1, [128, 16, 16])
                    T1 = r_[:, 0]; T2 = r_[:, 1]
                    tq = lambda o, x, y, op: self.dve("tensor_tensor", dict(out=o, in0=x, in1=y, op=op),
                                                      [qrb, rb_] + RT, [rb_, q_b])
                    tq(T1, qr[:, :, 0:16], csb, ALU.mult); tq(T2, qr[:, :, 16:32], snb, ALU.mult)
                    tq(q_[:, :, 64:80], T1, T2, ALU.subtract)
                    tq(T1, qr[:, :, 0:16], snb, ALU.mult); tq(T2, qr[:, :, 16:32], csb, ALU.mult)
                    tq(q_[:, :, 80:96], T1, T2, ALU.add)
                    for hh in range(2):
                        p, pb = self.pT.next()
                        for h8 in range(8):
                            self.tr(p[0:96, h8 * 128:(h8 + 1) * 128], q_[:, hh * 8 + h8, :], [q_b], [pb], inc=(h8 == 7))
                        dst = QTs[0:96, hh * 8:(hh + 1) * 8, :] if is_s else qst[0:96, hh * 8:(hh + 1) * 8, i * 128:(i + 1) * 128]
                        self.S.op("act", "copy", dict(out=dst, in_=p[0:96, :].rearrange("p (h n) -> p h n", n=128)),
                                  [pb], SM if is_s else [qstb])
                    if STOP == "C":
                        continue
                    if is_s:
                        kb16 = cq
                        kpb16 = self.sb("kpb16", [128, RP], BF16)
                        kpb16_b = S.buf("kpb16")
                        self.dve("tensor_copy", dict(out=kpb16[:], in_=kp[:]), [kpb], [kpb16_b])
                        if "kpT" not in os.environ.get("M1_SKIP", ""):
                            p, pb = self.pT.next()
                            self.tr(p[0:32, 0:128], kpb16[:, :], [kpb16_b], [pb], inc=True)
                            self.dve("tensor_copy", dict(out=kpeT_s[:, :], in_=p[0:32, 0:128]), [pb], SM)
                        continue
                    kvb = [bk.next() for _ in range(4)]
                    for nbk, (bank, bb) in enumerate(kvb):
                        for kc in range(2):
                            self.mm(bank[:], ckT[:, kc, :], wukv[:, kc, nbk * 512:(nbk + 1) * 512], kc == 0, kc == 1,
                                    [ckTb, wb[2]], [bb], kc == 1)
                    if STOP == "D1":
                        continue
                    k_, k_b = ksb.next()
                    v_, v_b = vsb.next()
                    for nbk, (bank, bb) in enumerate(kvb):
                        kv4 = bank[:, :].rearrange("p (h e) -> p h e", e=128)
                        self.S.op("act", "copy", dict(out=k_[:, nbk * 4:(nbk + 1) * 4, 0:64], in_=kv4[:, :, 0:64]), [bb], [k_b])
                        self.dve("tensor_copy", dict(out=v_[:, nbk * 4:(nbk + 1) * 4, 0:64], in_=kv4[:, :, 64:128]), [bb], [v_b])
                    if STOP == "D2":
                        continue
                    self.dve("tensor_copy", dict(out=k_[:, :, 64:96], in_=self.bc(kp[:], 1, [128, NH, RP])), [kpb], [k_b])
                    if STOP == "D3":
                        continue
                    if "vst" not in os.environ.get("M1_SKIP", ""):
                        self.ld(self.v_s[rows, :], v_[:].rearrange("p h e -> p (h e)"), [v_b], [], v_b, q="pool", shared=[self.qkv_s_b])
                    for hh in range(2):
                        p, pb = self.pT.next()
                        for h8 in range(8):
                            self.tr(p[0:96, h8 * 128:(h8 + 1) * 128], k_[:, hh * 8 + h8, :], [k_b], [pb], inc=(h8 == 7))
                        self.dve("tensor_copy", dict(out=kst[0:96, hh * 8:(hh + 1) * 8, i * 128:(i + 1) * 128],
                                                     in_=p[0:96, :].rearrange("p (h n) -> p h n", n=128)), [pb], [kstb])
                if not is_s and "qkst" not in os.environ.get("M1_SKIP", ""):
                    self.ld(self.qT_s[:, :, c0:c0 + G].rearrange("h d t -> d h t"), qst[0:96, :, 0:G], [qstb],
                            [], qstb, q="pool", shared=[self.qkv_s_b])
                    self.ld(self.kT_s[:, :, c0:c0 + G].rearrange("h d t -> d h t"), kst[0:96, :, 0:G], [kstb],
                            [], kstb, q="pool", shared=[self.qkv_s_b])
            S.end_phase()

        oTs = self.gsb("oTs", [128, 8, 128], BF16)
        oTs_b = S.buf("oTs")
        if os.environ.get("MLA_M1_ONLY"):
            return
        SKIP = os.environ.get("MLA_SKIP", "").split(",")
        self.pes = ExitStack()
        with self.pes:
            self.pT = self.psn("pT", 2, [128, 1024], BF16)
            bko = self.psn("bko", 4, [128, 512])
            bks = self.psn("bks", 2, [128, 512])
            bk = bko
            PT = self.sbn("PT", 3, [128, 512], BF16)
            wukT = self.sb("wukT", [64, NH, 256], BF16)
            wukv = self.sb("wukv2", [128, 2, 2048], BF16)
            wuvz = self.sb("wuvz", [128, 2, NH, 128], BF16)
            wb = S.bufs("mlaw2", 4)
            self.ld(wukT[:].rearrange("p a b -> p (a b)"), self.wukT_b[:, :], [self.wcast2_b], [wb[0]], wb[0])
            self.ld(wukv[:].rearrange("p a b -> p (a b)"), self.wukv_b[:, :], [self.wcast2_b], [wb[1]], wb[1])
            self.pool("memset", dict(ap=wuvz[:], constant=0.0), [], [wb[2]])
            for cc in range(2):
                for par in range(2):
                    src = wukv[:, cc, :].rearrange("p (k two e) -> p k two e", two=2, e=128)[:, :, par, 64:128]
                    dst = wuvz[:, cc].rearrange("p (k two) n -> p k two n", two=2)[:, :, par, par * 64:(par + 1) * 64]
                    self.dve("tensor_copy", dict(out=dst, in_=src), [wb[1]], [wb[2]])
            qlatT = self.sb("qlatT", [128, 2, NS, 8, NH], BF16)
            qrT = self.sb("qrT", [32, NS, 8, NH], BF16)
            olatT = self.sb("olatT", [128, 2, NS, 8, NH], BF16)
            ql_b = S.buf("qlat")
            ol_b = S.buf("olat")
            for h in range(NH):
                for cc in range(2):
                    bank, bb = bk.next()
                    self.mm(bank[:, 0:128], wukT[0:64, h, cc * 128:(cc + 1) * 128], QTs[0:64, h, :], True, True,
                            [wb[0]] + SM, [bb], True)
                    self.dve("tensor_copy", dict(out=qlatT[:, cc].rearrange("p s t h -> p (s t) h")[:, :, h],
                                                 in_=bank[:, 0:128]), [bb], [ql_b])
                bank, bb = bk.next()
                self.mm(bank[0:32, 0:128], self.ident[0:96, 64:96], QTs[0:96, h, :], True, True, [self.ident_b] + SM, [bb], True)
                self.dve("tensor_copy", dict(out=qrT[:].rearrange("p s t h -> p (s t) h")[:, :, h], in_=bank[0:32, 0:128]),
                         [bb], [ql_b])
            nmask = self.sb("nmask", [8, 8, NH], BF16)
            nm_b = S.buf("nmask")
            self.pool("memset", dict(ap=nmask[:], constant=1.0), [], [nm_b])
            self.pool("affine_select", dict(out=nmask[:], in_=nmask[:], pattern=[[1, 8], [0, NH]], compare_op=ALU.is_ge,
                                            fill=0.0, base=0, channel_multiplier=-1), [], [nm_b])
            ptb_ = self.sb("ptb", [128, NS * NPG], I32)
            idx = self.sb("idx", [128, NS * NPG // 4], I32)
            iof = self.sb("iof", [128, 1])
            ix_b = S.buf("idx")
            self.ld(ptb_[:], self.ptab.rearrange("s j -> (s j)").partition_broadcast(128), [], [ix_b], ix_b)
            for q4 in range(4):
                self.pool("iota", dict(out=iof[32 * q4:32 * q4 + 32, :], pattern=[[0, 1]], base=0, channel_multiplier=1,
                                       allow_small_or_imprecise_dtypes=True), [], [ix_b])
            for q4 in range(4):
                self.dve("tensor_scalar", dict(out=idx[32 * q4:32 * q4 + 32, :],
                                               in0=ptb_[32 * q4:32 * q4 + 32, :].rearrange("p (m f) -> p m f", f=4)[:, :, q4],
                                               scalar1=32.0, scalar2=iof[32 * q4:32 * q4 + 32, 0:1], op0=ALU.mult,
                                               op1=ALU.add), [ix_b], [ix_b])
            ones_k = self.sb("ones_k", [128, 1], BF16)
            self.dve("memset", dict(ap=ones_k[:], constant=1.0), [], [ix_b])
            cache4 = self.cache_cat.rearrange("(a f) c -> a (f c)", f=4)
            NPGS = 6
            pg = self.sbn("pg", NPGS, [128, 4, 288], BF16)
            ckT4 = self.sbn("ckT4", 3, [128, 4, 2, 128], BF16)
            kpT4 = self.sbn("kpT4", 3, [32, 4, 128], BF16)
            pn = self.sbn("pn", 2, [8, 128], BF16)
            knew = self.sbn("knew", 2, [8, 289], BF16)
            olat = self.sbn("olat", 2, [128, 256], BF16)
            rl1 = self.sbn("rl1", 2, [128, 1])
            for b in range(NS if "decode" not in SKIP else 0):
                acc, accb = bk.next()
                qlb = [qlatT[:, cc, b].rearrange("p t h -> p (t h)") for cc in range(2)]
                qrb = qrT[:, b].rearrange("p t h -> p (t h)")
                sn_, snb = bk.next()
                for cc in range(2):
                    self.mm(sn_[0:8, 0:128], ckvnT_s[:, cc, 8 * b:8 * b + 8], qlb[cc], cc == 0, False, SM + [ql_b], [snb], False)
                self.mm(sn_[0:8, 0:128], kpeT_s[0:32, 8 * b:8 * b + 8], qrb, False, True, SM + [ql_b], [snb], True)
                pn_, pnb = pn.next()
                self.act(dict(out=pn_[:], in_=sn_[0:8, 0:128], func=AF.Exp, scale=ATT_SCALE), [snb], [pnb])
                self.dve("tensor_tensor", dict(out=pn_[:], in0=pn_[:], in1=nmask[:].rearrange("p t h -> p (t h)"), op=ALU.mult),
                         [pnb, nm_b], [pnb])
                kn, knb = bk.next()
                self.mm(kn[0:8, 0:289], self.ident[:, 8 * b:8 * b + 8], ckvn_s[:, 0:289], True, True, SM + [self.ident_b], [knb], True)
                kw_, kwb = knew.next()
                self.dve("tensor_copy", dict(out=kw_[:], in_=kn[0:8, 0:289]), [knb], [kwb])
                accl, acclb = bk.next()
                self.mm(acc[:, 0:288], pn_[:], kw_[:, 0:288], True, False, [pnb, kwb], [accb], False)
                self.mm(accl[:, 0:1], pn_[:], kw_[:, 288:289], True, False, [pnb, kwb], [acclb], False)
                nb4 = NPG // 4
                pgs = {}
                ctk = {}
                pts = {}

                def stage_G(k):
                    pgt, pgb = pg.next()
                    col = b * nb4 + k
                    S.dma("pool", pgt[:].rearrange("p a c -> p (a c)"), cache4, reads=[ix_b], writes=[pgb], owner=pgb,
                          indirect=idx[:, col:col + 1])
                    pgs[k] = (pgt, pgb)

                def stage_A(k):
                    pgt, pgb = pgs[k]
                    p, pb = self.pT.next()
                    for jj in range(4):
                        for cc in range(2):
                            self.tr(p[:, (jj * 2 + cc) * 128:(jj * 2 + cc + 1) * 128], pgt[:, jj, cc * 128:(cc + 1) * 128],
                                    [pgb], [pb], inc=(jj == 3 and cc == 1))
                    c4, c4b = ckT4.next()
                    self.S.op("act", "copy", dict(out=c4[:].rearrange("p a b n -> p (a b n)"), in_=p[:, :]), [pb], [c4b])
                    p2, p2b = self.pT.next()
                    for jj in range(4):
                        self.tr(p2[0:32, jj * 128:(jj + 1) * 128], pgt[:, jj, 256:288], [pgb], [p2b], inc=(jj == 3))
                    k4, k4b = kpT4.next()
                    self.dve("tensor_copy", dict(out=k4[:].rearrange("p a n -> p (a n)"), in_=p2[0:32, 0:512]), [p2b], [k4b])
                    ctk[k] = (c4, c4b, k4, k4b)

                def stage_B(k):
                    c4, c4b, k4, k4b = ctk.pop(k)
                    st, stb = bks.next()
                    for jj in range(4):
                        for cc in range(2):
                            self.mm(st[:, jj * 128:(jj + 1) * 128], c4[:, jj, cc, :], qlb[cc], cc == 0, False,
                                    [c4b, ql_b], [stb], False)
                        self.mm(st[:, jj * 128:(jj + 1) * 128], k4[0:32, jj, :], qrb, False, True, [k4b, ql_b], [stb], jj == 3)
                    pt, ptb = PT.next()
                    self.act(dict(out=pt[:], in_=st[:], func=AF.Exp, scale=ATT_SCALE), [stb], [ptb])
                    pts[k] = (pt, ptb)

                def stage_C(k):
                    pt, ptb = pts.pop(k)
                    pgt, pgb = pgs.pop(k)
                    for jj in range(4):
                        last = (k == nb4 - 1 and jj == 3)
                        self.mm(acc[:, 0:288], pt[:, jj * 128:(jj + 1) * 128], pgt[:, jj, :], False, last, [ptb, pgb],
                                [accb], False)
                        self.mm(accl[:, 0:1], pt[:, jj * 128:(jj + 1) * 128], ones_k[:, 0:1], False, last, [ptb, ix_b],
                                [acclb], last)

                PF = min(NPGS - 2, nb4)
                for k in range(PF):
                    stage_G(k)
                stage_A(0)
                if nb4 > 1:
                    stage_A(1)
                stage_B(0)
                for k in range(nb4):
                    if k + PF < nb4:
                        stage_G(k + PF)
                    if k + 2 < nb4:
                        stage_A(k + 2)
                    if k + 1 < nb4:
                        stage_B(k + 1)
                    stage_C(k)
                r1, r1b = rl1.next()
                self.dve("reciprocal", dict(out=r1[:], in_=accl[:, 0:1]), [acclb], [r1b])
                ol, olb = olat.next()
                self.dve("tensor_scalar", dict(out=ol[:], in0=acc[:, 0:256], scalar1=r1[:, 0:1], scalar2=None, op0=ALU.mult),
                         [accb, r1b], [olb])
                p, pb = self.pT.next()
                for cc in range(2):
                    self.tr(p[:, cc * 128:(cc + 1) * 128], ol[:, cc * 128:(cc + 1) * 128], [olb], [pb], inc=(cc == 1))
                self.dve("tensor_copy", dict(out=olatT[:, :, b].rearrange("p c t h -> p c (t h)"),
                                             in_=p[:, 0:256].rearrange("p (c n) -> p c n", n=128)), [pb], [ol_b])
            for k in range(8):
                bank, bb = bk.next()
                first = True
                for par in range(2):
                    h = 2 * k + par
                    for cc in range(2):
                        self.mm(bank[:, 0:128], wuvz[:, cc, h, :], olatT[:, cc].rearrange("p s t h -> p (s t) h")[:, :, h],
                                first, (par == 1 and cc == 1), [wb[2], ol_b], [bb], (par == 1 and cc == 1))
                        first = False
                self.dve("tensor_copy", dict(out=oTs[:, k, :], in_=bank[:, 0:128]), [bb], [oTs_b])
            S.end_phase()

        self.pes = ExitStack()
        with self.pes:
            self.common_alloc(4)
            self.load_gamma(self.gpost, self.gpost_b, self.norm_post[ni])
            wo = self.sb("wo", [128, 8, D], BF16)
            wo_b_ = S.buf("wo")
            self.ld(wo[:].rearrange("p a b -> p (a b)"), self.wo_b[:, :], [self.wcast2_b], [wo_b_], wo_b_)
            bko = self.psn("bko", 4, [128, 512])
            bks = self.psn("bks", 2, [128, 512])
            bk = bko
            o_all = self.sb("o_all", [128, NTP, D], BF16)
            o_b = S.bufs("o_all", NTP // 4)
            masks = self.sb("masks", [128, 4, 512], BF16)
            mk_b = S.buf("masks")
            self.pool("memset", dict(ap=masks[:], constant=1.0), [], [mk_b])
            for j in range(4):
                self.pool("affine_select", dict(out=masks[:, j, :], in_=masks[:, j, :], pattern=[[1, 512]],
                                                compare_op=ALU.is_ge, fill=0.0, base=-128 * j, channel_multiplier=-1),
                          [], [mk_b])
            QTh = self.sbn("QTh", 2, [128, TP], BF16)
            KTh = self.sbn("KTh", 2, [128, TP], BF16)
            Vh = self.sbn("Vh", 2, [128, NTP, 66], BF16)
            PT = self.sbn("PT", 3, [128, 512], BF16)
            rl = self.sbn("rl", 2, [128, 4])
            v_view = self.v_s.rearrange("(t p) (h e) -> p t h e", p=128, e=66)
            NQG = TP // 512

            for h in range(NH if "prompt" not in SKIP else 0):
                qt, qtb = QTh.next()
                kt_, ktb = KTh.next()
                vh, vhb = Vh.next()
                self.ld(qt[0:96, :], self.qT_s[h], [self.qkv_s_b], [qtb], qtb)
                self.ld(kt_[0:96, :], self.kT_s[h], [self.qkv_s_b], [ktb], ktb)
                with self.nc.allow_non_contiguous_dma(reason="per-head V slice (130B rows)"):
                    pass
                self.S.dma("sp", vh[:], v_view[:, :, h, :], reads=[self.qkv_s_b], writes=[vhb], owner=vhb)
                for Qg in range(NQG):
                    oaccs = [bko.next() for _ in range(4)]
                    nkt = 4 * Qg + 4
                    def issue_st(kt):
                        j = kt - 4 * Qg
                        lo = max(j, 0) * 128
                        st, stb = bks.next()
                        self.mm(st[:, lo:512], kt_[0:96, kt * 128:(kt + 1) * 128], qt[0:96, Qg * 512 + lo:(Qg + 1) * 512],
                                True, True, [ktb, qtb], [stb], True)
                        pt, ptb = PT.next()
                        self.act(dict(out=pt[:, lo:512], in_=st[:, lo:512], func=AF.Exp, scale=ATT_SCALE), [stb], [ptb])
                        if j >= 0:
                            self.dve("tensor_tensor", dict(out=pt[:, lo:lo + 128], in0=pt[:, lo:lo + 128],
                                                           in1=masks[:, 0, 0:128], op=ALU.mult), [ptb, mk_b], [ptb])
                        return pt, ptb
                    nxt_pt = issue_st(0)
                    for kt in range(nkt):
                        pt, ptb = nxt_pt
                        if kt + 1 < nkt:
                            nxt_pt = issue_st(kt + 1)
                        for qi in range(4):
                            if 4 * Qg + qi >= kt:
                                last = (kt == 4 * Qg + qi)
                                self.mm(oaccs[qi][0][:, 0:65], pt[:, qi * 128:(qi + 1) * 128], vh[:, kt, 0:65],
                                        kt == 0, last, [ptb, vhb], [oaccs[qi][1]], last)
                    r, rb = rl.next()
                    for qi in range(4):
                        oacc, oab = oaccs[qi]
                        self.dve("reciprocal", dict(out=r[:, qi:qi + 1], in_=oacc[:, 64:65]), [oab], [rb])
                        self.dve("tensor_scalar", dict(out=o_all[:, Qg * 4 + qi, h * 64:(h + 1) * 64], in0=oacc[:, 0:64],
                                                       scalar1=r[:, qi:qi + 1], scalar2=None, op0=ALU.mult),
                                 [oab, rb], [o_b[Qg]])
            oT = self.sbn("oT", 2, [128, 8, 128], BF16)
            for ti in range(NT if "m3" not in SKIP else 0):
                xt, xb = self.xt.next()
                self.ld(xt[:], self.xs[ti * 128:(ti + 1) * 128, :], [self.xs_b[ti]], [xb], xb)
                if ti < NTP:
                    ot, otb = oT.next()
                    p, pb = self.pT.next()
                    for c in range(8):
                        self.tr(p[:, c * 128:(c + 1) * 128], o_all[:, ti, c * 128:(c + 1) * 128], [o_b[ti // 4]], [pb], inc=(c == 7))
                    self.S.op("act", "copy", dict(out=ot[:], in_=p[:, :].rearrange("p (c n) -> p c n", n=128)), [pb], [otb])
                else:
                    ot, otb = oTs, oTs_b
                ys = []
                for hh in range(2):
                    y, ybuf = bk.next()
                    for kc in range(8):
                        self.mm(y[:], ot[:, kc, :], wo[:, kc, hh * 512:(hh + 1) * 512], kc == 0, kc == 7, [otb, wo_b_], [ybuf],
                                kc == 7)
                    ys.append((y[:], ybuf))
                self.post_residual(ys, xt, xb, self.xs, [self.xs_b[ti]], ti, 1.0)
            S.end_phase()


def _kc(w, nk):
    n = w.shape[1]
    return np.ascontiguousarray(w.reshape(nk, 128, n).transpose(1, 0, 2)).reshape(128, nk * n)


def _qp(a):
    rest = a.shape[2:]
    a = a.reshape((32, 2, 64) + rest)
    perm = (1, 2, 0) + tuple(range(3, 3 + len(rest)))
    return np.ascontiguousarray(a.transpose(perm)).reshape((128, 32) + rest)


def prep_shared(inp):
    f32 = np.float32
    A = lambda k: np.asarray(inp[k], f32)
    g = A("ffn_w_gate").reshape(4, 8, 128, NF, 128)
    u = A("ffn_w_up").reshape(4, 8, 128, NF, 128)
    d = A("ffn_w_down").reshape(4, NF, 128, D)
    sh = {
        "wg_h": np.ascontiguousarray(g.transpose(0, 3, 2, 1, 4)).reshape(4, NF * 128, 8 * 128),
        "wu_h": np.ascontiguousarray(u.transpose(0, 3, 2, 1, 4)).reshape(4, NF * 128, 8 * 128),
        "wd_h": np.ascontiguousarray(d.transpose(0, 2, 1, 3)).reshape(4, 128, NF * D),
        "norm_pre": np.ascontiguousarray(A("norm_pre").reshape(6, D)),
        "norm_post": np.ascontiguousarray(A("norm_post").reshape(6, D)),
        "s5_are": _qp(A("ssm_a_re")[0]),
        "s5_aim": _qp(A("ssm_a_im")[0]),
        "s5_ldt": _qp(np.broadcast_to(A("ssm_log_dt")[0][:, None], (64, 64))),
        "s5_bre": _qp(A("ssm_b_re")[0]).reshape(128, 512),
        "s5_bim": _qp(A("ssm_b_im")[0]).reshape(128, 512),
        "s5_cre": _qp(A("ssm_c_re")[0].transpose(0, 2, 1)).reshape(128, 512),
        "s5_cim": _qp(A("ssm_c_im")[0].transpose(0, 2, 1)).reshape(128, 512),
        "s5_d": np.ascontiguousarray(A("ssm_d")[0].reshape(8, 128).T),
        "wglu_h": _kc(A("ssm_w_glu")[0], 8),
        "win_h": _kc(A("mla_w_in")[0], 8),
        "wuq_h": _kc(np.concatenate([A("mla_w_uq")[0].reshape(QL, NH, DQK)[:, :, :DN].reshape(QL, NH * DN),
                                     A("mla_w_uq")[0].reshape(QL, NH, DQK)[:, :, DN:].reshape(QL, NH * RP)], axis=1), 6),
        "wukv_h": _kc(A("mla_w_ukv")[0], 2),
        "wukT_h": np.ascontiguousarray(A("mla_w_ukv")[0].reshape(256, 16, 128)[:, :, :64].transpose(2, 1, 0)).reshape(64, 4096),
        "wo_h": _kc(A("mla_w_o")[0], 8),
        "qnorm": np.ascontiguousarray(A("mla_q_norm").reshape(1, QL)),
        "kvnorm": np.ascontiguousarray(A("mla_kv_norm").reshape(1, KVL)),
        "cache_cat": np.concatenate([A("cache_kv_latent")[0].reshape(-1, KVL), A("cache_k_rope")[0].reshape(-1, RP)], axis=1),
    }
    return sh


def prep_core(inp, c, cfg):
    f32 = np.float32
    ns = cfg.NS
    xp = np.asarray(inp["x_prompt"], f32)[c].reshape(cfg.TP, D)
    xsm = np.asarray(inp["x_sample"], f32)[c * ns:(c + 1) * ns].reshape(ns * cfg.TS, D)
    hre = np.asarray(inp["state_ssm_re"], f32)[0, c * ns:(c + 1) * ns]
    him = np.asarray(inp["state_ssm_im"], f32)[0, c * ns:(c + 1) * ns]
    h = np.stack([hre, him], axis=0)
    h = h.transpose(2, 3, 0, 1)
    h0 = _qp(np.ascontiguousarray(h)).reshape(128, 32 * 2 * ns)
    return {
        "x_in": np.ascontiguousarray(np.concatenate([xp, xsm], axis=0)),
        "s5_h0": h0,
        "ptab": np.ascontiguousarray(np.asarray(inp["page_table"], np.int32)[c * ns:(c + 1) * ns]),
    }


def _unqp(a):
    rest = a.shape[2:]
    a = a.reshape((2, 64, 32) + rest)
    perm = (2, 0, 1) + tuple(range(3, 3 + len(rest)))
    return np.ascontiguousarray(a.transpose(perm)).reshape((64, 64) + rest)


def assemble(results, cfg, n_cores):
    TP, ns, ts = cfg.TP, cfg.NS, cfg.TS
    f32 = np.float32
    yp = np.zeros((n_cores, TP, D), f32)
    ys = np.zeros((n_cores * ns, ts, D), f32)
    srp = np.zeros((1, n_cores, 64, 64), f32)
    sip = np.zeros((1, n_cores, 64, 64), f32)
    srs = np.zeros((1, n_cores * ns, 64, 64), f32)
    sis = np.zeros((1, n_cores * ns, 64, 64), f32)
    lp = np.zeros((1, n_cores, TP, KVL), f32)
    kp = np.zeros((1, n_cores, TP, RP), f32)
    ls = np.zeros((1, n_cores * ns, ts, KVL), f32)
    ks = np.zeros((1, n_cores * ns, ts, RP), f32)
    for c, r in enumerate(results):
        y = r["y_out"]
        yp[c] = y[:TP]
        ys[c * ns:(c + 1) * ns] = y[TP:].reshape(ns, ts, D)
        sp = _unqp(r["ssm_p"].reshape(128, 32, 2))
        srp[0, c] = sp[:, :, 0]
        sip[0, c] = sp[:, :, 1]
        ss = _unqp(r["ssm_s"].reshape(128, 32, 2, ns))
        srs[0, c * ns:(c + 1) * ns] = ss[:, :, 0, :].transpose(2, 0, 1)
        sis[0, c * ns:(c + 1) * ns] = ss[:, :, 1, :].transpose(2, 0, 1)
        lat = r["lat_out"]
        kpe = r["kpe_out"]
        lp[0, c] = lat[:TP]
        kp[0, c] = kpe[:TP]
        ls[0, c * ns:(c + 1) * ns] = lat[TP:].reshape(ns, ts, KVL)
        ks[0, c * ns:(c + 1) * ns] = kpe[TP:].reshape(ns, ts, RP)
    return (yp, ys, srp, sip, srs, sis, lp, kp, ls, ks)


def run(inputs, n_cores, cfg):
    nc = build(cfg)
    sh = prep_shared(inputs)
    in_maps = []
    for c in range(n_cores):
        m = dict(sh)
        m.update(prep_core(inputs, c, cfg))
        in_maps.append(m)
    res = run_bass_kernel_spmd(nc, in_maps, core_ids=list(range(n_cores)))
    return assemble(res.results, cfg, n_cores)


def kernel(**inputs):
    cfg = Cfg(TP=4096, NS=16, TS=8, NPG=128, NPHYS=int(np.asarray(inputs["cache_kv_latent"]).shape[1]))
    return run(inputs, 8, cfg)
```
